# Optimizing a Trainium2 kernel written in Bass

```python
import jax, jax.numpy as jnp
from jax import lax
import numpy as np

D_MODEL = 1024
BATCH = 16
SEQ = 2048
DEPTH = 1
DEC_BATCH = 8
DEC_SEQ = 16
PAST_LEN = 2048

CHUNK = 64
D_A = 1024
H_A = 4
DH_A = D_A // H_A
D_B = 1024
NB_B = 8
BS_B = D_B // NB_B
CONV_W = 4
LRU_C = 8.0
EPS = 1e-6
SPLITS = [D_A, D_A, D_A, D_A, D_A, H_A, H_A, D_B, D_B]
D_IN = sum(SPLITS)

kernel_name = "hybrid_mlstm_rglru_stream_step"


def _rmsnorm(x, g):
    xf = x.astype(jnp.float32)
    y = xf * lax.rsqrt(jnp.mean(xf * xf, axis=-1, keepdims=True) + EPS)
    return (y * g.astype(jnp.float32)).astype(x.dtype)


def _mlstm_block(state, inp):
    C, n, m = state
    q, k, v, ig, lf = inp
    L = q.shape[2]
    b = jnp.cumsum(lf, axis=-1)
    causal = jnp.tril(jnp.ones((L, L), dtype=bool))
    logw = b[..., :, None] - b[..., None, :] + ig[..., None, :]
    logw = jnp.where(causal, logw, -jnp.inf)
    m_inter = b + m[..., None]
    m_t = jnp.maximum(m_inter, jnp.max(logw, axis=-1))
    w = jnp.exp(logw - m_t[..., None])
    s_inter = jnp.exp(m_inter - m_t)
    wqk = w * jnp.einsum('bhqd,bhsd->bhqs', q, k)
    num = s_inter[..., None] * jnp.einsum('bhvk,bhqk->bhqv', C, q) + jnp.einsum('bhqs,bhsv->bhqv', wqk, v)
    den = s_inter * jnp.einsum('bhk,bhqk->bhq', n, q) + jnp.sum(wqk, axis=-1)
    h = num / jnp.maximum(jnp.abs(den), jnp.exp(-m_t))[..., None]
    m_end = m_t[..., -1]
    decay_prev = jnp.exp(b[..., -1] + m - m_end)
    ws = jnp.exp(b[..., -1:] - b + ig - m_end[..., None])
    C_new = decay_prev[..., None, None] * C + jnp.einsum('bhs,bhsv,bhsk->bhvk', ws, v, k)
    n_new = decay_prev[..., None] * n + jnp.einsum('bhs,bhsk->bhk', ws, k)
    return (C_new, n_new, m_end), h


def _mlstm(q, k, v, ig, lf, state, chunk):
    B, L = q.shape[0], q.shape[1]
    nc = L // chunk

    def to_blocks(t):
        t = t.reshape((B, nc, chunk) + t.shape[2:])
        return jnp.moveaxis(jnp.moveaxis(t, 1, 0), 3, 2)

    xs = (to_blocks(q), to_blocks(k), to_blocks(v), to_blocks(ig), to_blocks(lf))
    state_out, h = lax.scan(_mlstm_block, state, xs)
    h = jnp.transpose(h, (1, 0, 3, 2, 4)).reshape(B, L, H_A, DH_A)
    return h, state_out


def _rglru(xb, conv_buf, h0, conv_w, conv_b, w_ra, b_ra, w_rx, b_rx, lam, start_pos):
    B, L = xb.shape[0], xb.shape[1]
    xpad = jnp.concatenate([conv_buf.astype(xb.dtype), xb], axis=1)
    xc = conv_b + xpad[:, 0:L] * conv_w[0]
    for j in range(1, CONV_W):
        xc = xc + xpad[:, j:j + L] * conv_w[j]
    new_buf = xpad[:, -(CONV_W - 1):]
    xc = xc.astype(jnp.float32)
    blk = xc.reshape(B, L, NB_B, BS_B)
    r = jax.nn.sigmoid(jnp.einsum('blni,nij->blnj', blk, w_ra.astype(jnp.float32)).reshape(B, L, D_B) + b_ra)
    i = jax.nn.sigmoid(jnp.einsum('blni,nij->blnj', blk, w_rx.astype(jnp.float32)).reshape(B, L, D_B) + b_rx)
    log_a = -LRU_C * r * jax.nn.softplus(-lam.astype(jnp.float32))
    a = jnp.exp(log_a)
    mult = jnp.sqrt(-jnp.expm1(2.0 * log_a))
    pos = jnp.arange(L) + start_pos
    mult = jnp.where((pos == 0)[None, :, None], 1.0, mult)
    bt = mult * i * xc
    bt = bt.at[:, 0].add(a[:, 0] * h0.astype(jnp.float32))

    def combine(e1, e2):
        a1, b1 = e1
        a2, b2 = e2
        return a1 * a2, a2 * b1 + b2

    _, h = lax.associative_scan(combine, (a, bt), axis=1)
    return h, h[:, -1], new_buf


def _layer(x, c, st, lw, start_pos, chunk):
    (w_mod, b_mod, g_norm, w_in, b_if, g_head, conv_w, conv_b, w_ra, b_ra, w_rx, b_rx, lam,
     w_gate, b_gate, w_out_a, w_out_b, w_o) = lw
    C0, n0, m0, h0, buf0 = st
    B, L = x.shape[0], x.shape[1]
    mod = c @ w_mod + b_mod
    shift, scale, gate = mod[:, :D_MODEL], mod[:, D_MODEL:2 * D_MODEL], mod[:, 2 * D_MODEL:]
    u = _rmsnorm(x, g_norm) * (1.0 + scale[:, None]) + shift[:, None]
    proj = u @ w_in
    idx = [int(s) for s in np.cumsum(SPLITS)[:-1]]
    q, k, v, o, z_a, ig, fg, xb, z_b = jnp.split(proj, idx, axis=-1)
    f32 = jnp.float32
    qh = q.astype(f32).reshape(B, L, H_A, DH_A)
    kh = k.astype(f32).reshape(B, L, H_A, DH_A) * (DH_A ** -0.5)
    vh = v.astype(f32).reshape(B, L, H_A, DH_A)
    ig = ig.astype(f32) + b_if[:H_A].astype(f32)
    lf = jax.nn.log_sigmoid(fg.astype(f32) + b_if[H_A:].astype(f32))
    h_a, (C1, n1, m1) = _mlstm(qh, kh, vh, ig, lf, (C0.astype(f32), n0.astype(f32), m0.astype(f32)), chunk)
    h_a = h_a * lax.rsqrt(jnp.mean(h_a * h_a, axis=-1, keepdims=True) + EPS)
    h_a = h_a.reshape(B, L, D_A) * g_head.astype(f32)
    y_a = (jax.nn.silu(z_a.astype(f32)) * jax.nn.sigmoid(o.astype(f32)) * h_a).astype(x.dtype)
    h_b, h1, buf1 = _rglru(xb, buf0, h0, conv_w, conv_b, w_ra, b_ra, w_rx, b_rx, lam, start_pos)
    y_b = (jax.nn.silu(z_b.astype(f32)) * h_b).astype(x.dtype)
    g = jax.nn.sigmoid(u @ w_gate + b_gate)
    merged = g[..., :D_MODEL] * (y_a @ w_out_a) + g[..., D_MODEL:] * (y_b @ w_out_b)
    y = x + gate[:, None] * (merged @ w_o)
    return y, (C1, n1, m1, h1.astype(x.dtype), buf1)


def setup_inputs(seed: int = 0) -> dict:
    key = jax.random.key(seed)
    ks = jax.random.split(key, 32)
    nrm = jax.random.normal
    D = D_MODEL
    u_lam = jax.random.uniform(ks[20], (DEPTH, D_B), minval=0.9, maxval=0.999)
    a_base = u_lam ** (1.0 / LRU_C)
    lam = jnp.log(a_base) - jnp.log1p(-a_base)
    b_if = jnp.concatenate([-1.0 + 0.1 * nrm(ks[21], (DEPTH, H_A)),
                            3.0 + 0.5 * nrm(ks[22], (DEPTH, H_A))], axis=-1)
    return {
        "x_prompt": nrm(ks[0], (BATCH, SEQ, D)),
        "x_sample": nrm(ks[1], (DEC_BATCH, DEC_SEQ, D)),
        "c_prompt": nrm(ks[2], (BATCH, D)),
        "c_sample": nrm(ks[3], (DEC_BATCH, D)),
        "state_C": 0.1 * nrm(ks[4], (DEPTH, DEC_BATCH, H_A, DH_A, DH_A)),
        "state_n": 0.1 * nrm(ks[5], (DEPTH, DEC_BATCH, H_A, DH_A)),
        "state_m": nrm(ks[6], (DEPTH, DEC_BATCH, H_A)),
        "state_h": 0.5 * nrm(ks[7], (DEPTH, DEC_BATCH, D_B)),
        "state_conv": nrm(ks[8], (DEPTH, DEC_BATCH, CONV_W - 1, D_B)),
        "w_mod": 0.5 * D ** -0.5 * nrm(ks[9], (DEPTH, D, 3 * D)),
        "b_mod": 0.01 * nrm(ks[10], (DEPTH, 3 * D)),
        "g_norm": 1.0 + 0.01 * nrm(ks[11], (DEPTH, D)),
        "w_in": D ** -0.5 * nrm(ks[12], (DEPTH, D, D_IN)),
        "b_if": b_if,
        "g_head": 1.0 + 0.01 * nrm(ks[13], (DEPTH, D_A)),
        "conv_w": CONV_W ** -0.5 * nrm(ks[14], (DEPTH, CONV_W, D_B)),
        "conv_b": 0.01 * nrm(ks[15], (DEPTH, D_B)),
        "w_ra": BS_B ** -0.5 * nrm(ks[16], (DEPTH, NB_B, BS_B, BS_B)),
        "b_ra": 0.01 * nrm(ks[17], (DEPTH, D_B)),
        "w_rx": BS_B ** -0.5 * nrm(ks[18], (DEPTH, NB_B, BS_B, BS_B)),
        "b_rx": 0.01 * nrm(ks[19], (DEPTH, D_B)),
        "lam": lam,
        "w_gate": D ** -0.5 * nrm(ks[23], (DEPTH, D, 2 * D)),
        "b_gate": 0.01 * nrm(ks[24], (DEPTH, 2 * D)),
        "w_out_a": D_A ** -0.5 * nrm(ks[25], (DEPTH, D_A, D)),
        "w_out_b": D_B ** -0.5 * nrm(ks[26], (DEPTH, D_B, D)),
        "w_o": D ** -0.5 * nrm(ks[27], (DEPTH, D, D)),
        "g_final": 1.0 + 0.01 * nrm(ks[28], (D,)),
    }


def reference(x_prompt, x_sample, c_prompt, c_sample, state_C, state_n, state_m, state_h, state_conv,
              w_mod, b_mod, g_norm, w_in, b_if, g_head, conv_w, conv_b, w_ra, b_ra, w_rx, b_rx, lam,
              w_gate, b_gate, w_out_a, w_out_b, w_o, g_final):
    f32 = jnp.float32
    Bp = x_prompt.shape[0]
    zero_state = (jnp.zeros((Bp, H_A, DH_A, DH_A), f32), jnp.zeros((Bp, H_A, DH_A), f32),
                  jnp.zeros((Bp, H_A), f32), jnp.zeros((Bp, D_B), x_prompt.dtype),
                  jnp.zeros((Bp, CONV_W - 1, D_B), x_prompt.dtype))
    xp, xs = x_prompt, x_sample
    new_p = [[], [], [], [], []]
    new_s = [[], [], [], [], []]
    for l in range(DEPTH):
        lw = (w_mod[l], b_mod[l], g_norm[l], w_in[l], b_if[l], g_head[l], conv_w[l], conv_b[l],
              w_ra[l], b_ra[l], w_rx[l], b_rx[l], lam[l], w_gate[l], b_gate[l],
              w_out_a[l], w_out_b[l], w_o[l])
        xp, sp = _layer(xp, c_prompt, zero_state, lw, 0, CHUNK)
        xs, ss = _layer(xs, c_sample, (state_C[l], state_n[l], state_m[l], state_h[l], state_conv[l]),
                        lw, PAST_LEN, x_sample.shape[1])
        for j in range(5):
            new_p[j].append(sp[j])
            new_s[j].append(ss[j])
    y_prompt = _rmsnorm(xp, g_final)
    y_sample = _rmsnorm(xs, g_final)
    C_p, n_p, m_p, h_p, conv_p = [jnp.stack(t, axis=0) for t in new_p]
    C_s, n_s, m_s, h_s, conv_s = [jnp.stack(t, axis=0) for t in new_s]
    return (y_prompt, y_sample, C_p, n_p, m_p, h_p, conv_p, C_s, n_s, m_s, h_s, conv_s)
```

```python
import numpy as np
from contextlib import ExitStack
import concourse.bass as bass
import concourse.mybir as mybir
from concourse.bass_utils import run_bass_kernel_spmd

F32 = mybir.dt.float32
BF16 = mybir.dt.bfloat16
U8 = mybir.dt.uint8
AF = mybir.ActivationFunctionType
ALU = mybir.AluOpType
AX = mybir.AxisListType

D = 1024
H = 4
DH = 256
SEQ = 2048
DEC = 16
NTOK = 2 * SEQ + DEC
TT = 1024
EPS = 1e-6
NCH_W = 88
NSLOT_LOADS = NCH_W // 4
NS = 3
PV_BMOD, PV_GN, PV_GH, PV_CW, PV_CB, PV_BRA, PV_BRX, PV_LAM, PV_BG = 0, 24, 32, 40, 72, 80, 88, 96, 104
NPV = 120
import os
DBG_SUB = int(os.environ.get('DBG_SUB', '0'))
OPT_B = int(os.environ.get('OPT_B', '0'))
OPT_SQ = int(os.environ.get('OPT_SQ', '0'))
STQ = os.environ.get('STQ', 'sp')


class Buf:
    __slots__ = ("w", "r", "excl")

    def __init__(self, excl=False):
        self.w = None
        self.r = {}
        self.excl = excl


def bufs(n, excl=False):
    return [Buf(excl) for _ in range(n)]


class TR:
    def __init__(self, nc, es):
        self.nc = nc
        self.es = es
        self.eng = {"pe": nc.tensor, "act": nc.scalar, "dve": nc.vector, "pool": nc.gpsimd, "sp": nc.sync}
        self.csem = {k: es.enter_context(nc.semaphore("c_" + k)) for k in ("pe", "act", "dve", "pool")}
        self.cnt = {k: 0 for k in self.csem}
        self.known = {k: {} for k in self.eng}
        self.dsem = {}
        self.pending = []
        self.deferring = False
        self.deferred = []
        self.cur_bundle = []

    def _wait(self, e, tok):
        if tok is None:
            return
        key, sem, val = tok
        if e == "pe" and key == "pe":
            return
        if self.known[e].get(key, 0) >= val:
            return
        self.eng[e].wait_ge(sem, val)
        self.known[e][key] = val

    def _deps(self, e, reads, writes):
        for b in reads:
            self._wait(e, b.w)
            if b.excl:
                for key, (sem, val) in list(b.r.items()):
                    self._wait(e, (key, sem, val))
        for b in writes:
            self._wait(e, b.w)
            for key, (sem, val) in list(b.r.items()):
                self._wait(e, (key, sem, val))

    @staticmethod
    def _mark(tok, reads, writes):
        key, sem, val = tok
        for b in reads:
            if b.excl:
                b.w = tok
                b.r = {}
            else:
                b.r[key] = (sem, val)
        for b in writes:
            b.w = tok
            b.r = {}

    def op(self, e, fn, reads=(), writes=(), signal=True):
        if self.deferring:
            self.cur_bundle.append((e, fn, list(reads), list(writes), signal))
            if signal:
                self.deferred.append(self.cur_bundle)
                self.cur_bundle = []
            return None
        return self._emit(e, fn, reads, writes, signal)

    def flush(self, n=None):
        was = self.deferring
        self.deferring = False
        k = 0
        while self.deferred and (n is None or k < n):
            for args in self.deferred.pop(0):
                self._emit(*args)
            k += 1
        self.deferring = was

    def _emit(self, e, fn, reads=(), writes=(), signal=True):
        self._deps(e, reads, writes)
        inst = fn(self.eng[e])
        if not signal:
            assert e == "pe"
            self.pending.append((reads, writes))
            return None
        self.cnt[e] += 1
        inst.then_inc(self.csem[e], 1)
        tok = (e, self.csem[e], self.cnt[e])
        if e == "pe" and self.pending:
            for r, w in self.pending:
                self._mark(tok, r, w)
            self.pending = []
        self._mark(tok, reads, writes)
        return tok

    def dma(self, q, out, in_, reads, writes, slot, **kw):
        self._deps(q, reads, writes)
        if slot not in self.dsem:
            self.dsem[slot] = [self.es.enter_context(self.nc.semaphore("d_" + slot)), 0]
        ent = self.dsem[slot]
        ent[1] += 16
        self.eng[q].dma_start(out=out, in_=in_, **kw).then_inc(ent[0], 16)
        tok = ("d_" + slot, ent[0], ent[1])
        self._mark(tok, reads, writes)
        return tok

    def barrier(self, engines=None):
        assert not self.pending
        for e in (engines or self.eng):
            for k in self.csem:
                if self.cnt[k] > 0:
                    self._wait(e, (k, self.csem[k], self.cnt[k]))
            for slot, (sem, c) in self.dsem.items():
                if c > 0:
                    self._wait(e, ("d_" + slot, sem, c))


class Arena:
    def __init__(self, ap, size):
        self.ap = ap
        self.size = size
        self.off = 0

    def alloc(self, nelem, dt):
        esz = 4 if dt == F32 else 2
        off = (self.off + 63) // 64 * 64
        self.off = off + nelem * esz
        assert self.off <= self.size, ("arena overflow", self.off, self.size)
        return self.ap[:, off:off + nelem * esz].bitcast(dt)


class StopBuild(Exception):
    pass


def build_program(maxphase=10 ** 9):
    nc = bass.Bass("TRN2", target_bir_lowering=False)
    phase_ctr = [0]

    def gate():
        phase_ctr[0] += 1
        if phase_ctr[0] > maxphase:
            raise StopBuild()

    def din(name, shape):
        return nc.dram_tensor(name, shape, F32, kind="ExternalInput").ap()

    def dout(name, shape):
        return nc.dram_tensor(name, shape, F32, kind="ExternalOutput").ap()

    x_d = din("x", [NTOK, D])
    cT_d = din("cT", [128, 8, 3])
    wmod_d = din("wmod", [128, 8, 3 * D])
    pv_d = din("pv", [128, NPV])
    bif_d = din("bif", [4, 2])
    gfin_d = din("gfin", [1, D])
    wst_d = din("wst", [NCH_W, 128, D])
    wif_d = din("wif", [128, 64])
    wr_d = din("wr", [128, 2048])
    wo_d = din("wo", [128, 8 * D])
    sCT_d = din("sCT", [128, 4 * 2 * 257])
    sm_d = din("sm", [4, 1])
    sh_d = din("sh", [128, 8])
    sconv_d = din("sconv", [128, 24])
    wscr_d = nc.dram_tensor("wscr", [NSLOT_LOADS, 128, 4 * D], BF16, kind="Internal").ap()
    y_d = dout("y", [NTOK, D])
    oCT_d = dout("oCT", [3, 128, 4 * 2 * 257])
    om_d = dout("om", [3, 4, 1])
    oh_d = dout("oh", [3, 128, 8])
    oconv_d = dout("oconv", [3, 128, 24])
    DBG_DUMP = int(os.environ.get("DBG_DUMP", "0"))
    if DBG_DUMP:
        dbg_d = dout("dbg", [3, 128, 8 * TT])

    with ExitStack() as es:
        ARENA = 207 * 1024
        arena_t = es.enter_context(nc.sbuf_tensor("arena", [128, ARENA], U8))
        PB = [es.enter_context(nc.psum_tensor("pb%d" % i, [128, 512], F32)) for i in range(8)]
        T = TR(nc, es)
        A = Arena(arena_t, ARENA)
        out_toks = {}
        fold_wo = [False]

        wo_bf = A.alloc(8 * D, BF16).rearrange("p (k n) -> p k n", k=8)
        wr_bf = A.alloc(2048, F32).rearrange("p (g n j) -> p g n j", g=2, n=8)
        wif_bf = A.alloc(64, BF16).rearrange("p (k m) -> p k m", k=8)
        ident_bf = A.alloc(128, BF16)
        ident_f = A.alloc(128, F32)
        ones_f = A.alloc(128, F32)
        maskT = A.alloc(128, BF16)
        pv = A.alloc(NPV, F32)
        bif = A.alloc(2, F32)
        nbf = A.alloc(1, F32)
        cT = A.alloc(24, F32).rearrange("p (k s) -> p k s", k=8)
        modT = A.alloc(72, F32).rearrange("p (m s) -> p m s", m=24)
        gs = A.alloc(24, F32).rearrange("p (m s) -> p m s", m=8)
        cfeat = A.alloc(8, F32)
        lamt = A.alloc(8, F32)
        gfin_bc = A.alloc(D, F32)
        gate_bc = A.alloc(D, F32)
        gdiag = None
        Cst = A.alloc(4 * 2 * 257, F32).rearrange("p (h c v) -> p h c v", h=4, c=2)
        Cbf = A.alloc(4 * 2 * 258, BF16).rearrange("p (h c v) -> p h c v", h=4, c=2)
        mstate = A.alloc(1, F32)
        hcar = A.alloc(8, F32)
        convcar = A.alloc(24, F32).rearrange("p (f j) -> p f j", f=8)
        rmask = A.alloc(TT, F32)
        uT = A.alloc(8 * TT, BF16).rearrange("p (f t) -> p f t", f=8)
        yaT = A.alloc(8 * TT, BF16).rearrange("p (f t) -> p f t", f=8)
        ybT = A.alloc(8 * TT, BF16).rearrange("p (f t) -> p f t", f=8)
        mgT = A.alloc(8 * TT, BF16).rearrange("p (f t) -> p f t", f=8)
        ring = [A.alloc(4 * D, BF16).rearrange("p (c k m) -> p c k m", c=4, k=8) for _ in range(NS)]
        PERSIST_END = A.off

        b_const = Buf()
        b_wc = Buf()
        b_wo = Buf()
        b_mod = Buf()
        b_gbc = Buf()
        b_ring = bufs(NS)
        b_uT = [bufs(8) for _ in range(8)]
        b_yaT = [bufs(8) for _ in range(8)]
        b_ybT = [bufs(2) for _ in range(8)]
        b_mgT = [bufs(2) for _ in range(8)]
        b_C = bufs(4)
        b_Cbf = bufs(4)
        b_ms = Buf()
        b_hcar = bufs(8)
        b_ccar = bufs(8)
        b_PB = bufs(8, excl=True)

        T.op("pool", lambda e: e.memset(ident_bf, 0.0), writes=[b_const])
        T.op("pool", lambda e: e.affine_select(out=ident_bf, in_=ident_bf, pattern=[[1, 128]],
                                                compare_op=ALU.not_equal, fill=1.0, base=0,
                                                channel_multiplier=-1), reads=[b_const], writes=[b_const])
        T.op("pool", lambda e: e.memset(ident_f, 0.0), writes=[b_const])
        T.op("pool", lambda e: e.affine_select(out=ident_f, in_=ident_f, pattern=[[1, 128]],
                                                compare_op=ALU.not_equal, fill=1.0, base=0,
                                                channel_multiplier=-1), reads=[b_const], writes=[b_const])
        T.op("pool", lambda e: e.memset(ones_f, 1.0), writes=[b_const])
        T.op("pool", lambda e: e.memset(maskT, 1.0), writes=[b_const])
        T.op("pool", lambda e: e.affine_select(out=maskT, in_=maskT, pattern=[[1, 128]],
                                                compare_op=ALU.is_ge, fill=0.0, base=0,
                                                channel_multiplier=-1), reads=[b_const], writes=[b_const])
        T.op("pool", lambda e: e.memset(rmask[0:4, :], 1.0), writes=[b_const])
        T.op("pool", lambda e: e.memset(rmask[0:4, :].rearrange("p (c l) -> p c l", l=128)[:, :, 0:1], 0.0),
             reads=[b_const], writes=[b_const])
        T.dma("sp", pv, pv_d, [], [b_const], "s_pv")
        T.dma("sp", bif[0:4, :], bif_d, [], [b_const], "s_bif")
        T.dma("sp", cT.rearrange("p k s -> p (k s)"), cT_d.rearrange("p k s -> p (k s)"), [], [b_const], "s_cT")
        T.dma("sp", gfin_bc, gfin_d.to_broadcast([128, D]), [], [b_const], "s_gfin")
        T.op("act", lambda e: e.activation(out=lamt, in_=pv[:, PV_LAM:PV_LAM + 8], func=AF.Exp, scale=-1.0),
             reads=[b_const], writes=[b_const])
        T.op("act", lambda e: e.activation(out=lamt, in_=lamt, func=AF.Ln, bias=1.0),
             reads=[b_const], writes=[b_const])
        T.op("dve", lambda e: e.tensor_scalar(out=cfeat, in0=lamt, scalar1=-8.0, scalar2=None, op0=ALU.mult),
             reads=[b_const], writes=[b_const])
        T.op("dve", lambda e: e.tensor_scalar(out=nbf[0:4, :], in0=bif[0:4, 1:2], scalar1=-1.0, scalar2=None,
                                              op0=ALU.mult), reads=[b_const], writes=[b_const])

        wm = A.alloc(8 * 3072, BF16).rearrange("p (k n) -> p k n", k=8)
        cTb = A.alloc(24, BF16).rearrange("p (k s) -> p k s", k=8)
        b_wm = bufs(4)
        for q4 in range(4):
            T.dma("pool", wm[:, :, q4 * 768:(q4 + 1) * 768], wmod_d[:, :, q4 * 768:(q4 + 1) * 768], [], [b_wm[q4]],
                  "s_wm%d" % q4)
        T.dma("pool", wif_bf.rearrange("p k m -> p (k m)"), wif_d, [], [b_wc], "s_wif")
        T.dma("sp", wr_bf.rearrange("p g n j -> p (g n j)"), wr_d, [], [b_wc], "s_wr")
        T.dma("pool", wo_bf.rearrange("p k n -> p (k n)"), wo_d, [], [b_wo], "s_wo")
        T.op("dve", lambda e: e.tensor_copy(out=cTb, in_=cT), reads=[b_const], writes=[b_const])
        for m in range(24):
            for kc in range(8):
                T.op("pe", lambda e, m=m, kc=kc: e.matmul(
                    PB[0][:, 4 * m:4 * m + 3], lhsT=wm[:, kc, m * 128:(m + 1) * 128], rhs=cTb[:, kc, :],
                    start=(kc == 0), stop=(kc == 7)),
                    reads=[b_wm[m // 6], b_const], writes=[b_PB[0]], signal=(kc == 7))
        T.op("dve", lambda e: e.tensor_tensor(
            out=modT, in0=PB[0][:, 0:96].rearrange("p (m s) -> p m s", s=4)[:, :, 0:3],
            in1=pv[:, PV_BMOD:PV_BMOD + 24].unsqueeze(2).to_broadcast([128, 24, 3]), op=ALU.add),
            reads=[b_PB[0], b_const], writes=[b_mod])
        T.op("dve", lambda e: e.tensor_scalar(out=gs, in0=modT[:, 8:16, :], scalar1=1.0, scalar2=None, op0=ALU.add),
             reads=[b_mod], writes=[b_mod])
        T.op("dve", lambda e: e.tensor_tensor(
            out=gs, in0=gs, in1=pv[:, PV_GN:PV_GN + 8].unsqueeze(2).to_broadcast([128, 8, 3]), op=ALU.mult),
            reads=[b_mod, b_const], writes=[b_mod])
        T.barrier()
        A.off = PERSIST_END

        ring_state = {"issued": 0}
        b_scr = bufs(NSLOT_LOADS)
        NS_S = 6
        SAMPLE_BASE = NSLOT_LOADS * 4
        RS_OFF = ARENA - NS_S * 8192
        ring_s = [arena_t[:, RS_OFF + i * 8192: RS_OFF + (i + 1) * 8192].bitcast(BF16)
                  .rearrange("p (c k m) -> p c k m", c=4, k=8) for i in range(NS_S)]
        b_ring_s = bufs(NS_S)

        def ring_slot(j):
            if j >= SAMPLE_BASE:
                i = (j - SAMPLE_BASE) % NS_S
                return ring_s[i], b_ring_s[i], "ringX%d" % i, NS_S
            return ring[j % NS], b_ring[j % NS], "ring%d" % (j % NS), NS

        def ring_issue_upto(k):
            while ring_state["issued"] <= k:
                j = ring_state["issued"]
                ci = (j % NSLOT_LOADS) * 4
                jj = j % NSLOT_LOADS
                rv, rb, rname, _ = ring_slot(j)
                if j < NSLOT_LOADS:
                    sl = j % NS
                    T.dma("pool", rv.rearrange("p c k m -> p c (k m)"),
                          wst_d[ci:ci + 4].rearrange("c p n -> p c n"), [], [rb], "ringS%d" % sl)
                    T.dma("sp", wscr_d[jj], rv.rearrange("p c k m -> p (c k m)"), [rb], [b_scr[jj]], "scr%d" % sl)
                else:
                    T.dma("sp", rv.rearrange("p c k m -> p (c k m)"), wscr_d[jj], [b_scr[jj]], [rb], rname)
                ring_state["issued"] += 1

        total_loads = [0]

        def ring_next(dual=False):
            k = total_loads[0]
            total_loads[0] += 1
            rv, rb, _, ns = ring_slot(k)
            look = ns - 2 if dual else ns - 1
            hi = k + look
            if k < SAMPLE_BASE:
                hi = min(hi, SAMPLE_BASE - 1)
            ring_issue_upto(min(hi, NLOADS_TOTAL - 1))
            return rv, rb

        sts = []
        for s in range(2):
            for part in range(SEQ // TT):
                sts.append(dict(seq=s, tok0=s * SEQ + part * TT, T=TT, tp=128, first=(part == 0),
                                last=(part == SEQ // TT - 1), prompt=True))
        sts.append(dict(seq=2, tok0=2 * SEQ, T=DEC, tp=DEC, first=True, last=True, prompt=False))
        NLOADS_TOTAL = NSLOT_LOADS * len(sts)

        psum_rr = [0]
        bank_pool = [list(range(8))]

        def next_bank():
            pool = bank_pool[0]
            i = pool[psum_rr[0] % len(pool)]
            psum_rr[0] += 1
            return i

        def proj(wv, bw, ci, src, bsrc, Tn, nn, evac):
            for nt in range(Tn // nn):
                bk = next_bank()
                for kc in range(8):
                    T.op("pe", lambda e, bk=bk, kc=kc, nt=nt: e.matmul(
                        PB[bk][:, 0:nn], lhsT=wv[:, ci, kc, :], rhs=src[:, kc, nt * nn:(nt + 1) * nn],
                        start=(kc == 0), stop=(kc == 7)),
                        reads=[bw] + bsrc(kc, nt), writes=[b_PB[bk]], signal=(kc == 7))
                evac(bk, nt)

        def p0_alloc():
            return dict(xin=[A.alloc(D, F32) for _ in range(2)], xn=[A.alloc(D, BF16) for _ in range(2)],
                        junk=A.alloc(D, BF16), stat=[A.alloc(4, F32) for _ in range(2)],
                        b_xin=bufs(2), b_xn=bufs(2), b_stat=bufs(2), b_junk=Buf())

        def p0_tile(st, j, Z, load_only=False, stages=("stats", "xn", "tr", "ev")):
            s, tp, tok0 = st["seq"], st["tp"], st["tok0"]
            p = j % 2
            xin, xn, junk, stat = Z["xin"][p], Z["xn"][p], Z["junk"], Z["stat"][p]
            b_xin, b_xn, b_stat, b_junk = Z["b_xin"][p], Z["b_xn"][p], Z["b_stat"][p], Z["b_junk"]
            if load_only:
                T.dma("sp", xin[0:tp, :], x_d[tok0 + j * tp: tok0 + (j + 1) * tp, :], [], [b_xin], "xin%d" % p)
                return
            if "stats" in stages:
                T.op("act", lambda e: e.activation(out=junk[0:tp, :], in_=xin[0:tp, :], func=AF.Square,
                                                   accum_out=stat[0:tp, 0:1]), reads=[b_xin], writes=[b_junk, b_stat])
                T.op("act", lambda e: e.activation(out=stat[0:tp, 1:2], in_=stat[0:tp, 0:1], func=AF.Ln,
                                                   scale=1.0 / D, bias=EPS), reads=[b_stat], writes=[b_stat])
                T.op("act", lambda e: e.activation(out=stat[0:tp, 2:3], in_=stat[0:tp, 1:2], func=AF.Exp,
                                                   scale=-0.5), reads=[b_stat], writes=[b_stat])
            if "xn" in stages:
                T.op("pool", lambda e: e.tensor_scalar(out=xn[0:tp, :], in0=xin[0:tp, :], scalar1=stat[0:tp, 2:3],
                                                       scalar2=0.0, op0=ALU.mult, op1=ALU.add),
                     reads=[b_xin, b_stat], writes=[b_xn])
            def bank_of(fc):
                if fc < 6:
                    return 2 * p, PB[2 * p][:, 0:384].bitcast(BF16).rearrange("p (f t) -> p f t", f=6)[:, fc, 0:tp]
                return 2 * p + 1, PB[2 * p + 1][:, 0:128].bitcast(BF16).rearrange("p (f t) -> p f t", f=2)[:, fc - 6, 0:tp]
            if "tr" in stages:
                for fc in range(8):
                    bk, pv_ = bank_of(fc)
                    T.op("pe", lambda e, fc=fc, pv_=pv_: e.transpose(
                        out=pv_, in_=xn[0:tp, fc * 128:(fc + 1) * 128], identity=ident_bf[0:tp, 0:tp]),
                        reads=[b_xn, b_const], writes=[b_PB[bk]], signal=(fc in (5, 7)))
            if "ev" in stages:
                for fc in (0, 6, 1, 2, 7, 3, 4, 5):
                    bk, pv_ = bank_of(fc)
                    dst = uT[:, fc, j * tp:(j + 1) * tp]
                    if fc < 6:
                        T.op("dve", lambda e, fc=fc, dst=dst, pv_=pv_: e.tensor_scalar(
                            out=dst, in0=pv_, scalar1=gs[:, fc, s:s + 1], scalar2=modT[:, fc, s:s + 1],
                            op0=ALU.mult, op1=ALU.add), reads=[b_PB[bk], b_mod], writes=[b_uT[fc][j]])
                    else:
                        T.op("act", lambda e, fc=fc, dst=dst, pv_=pv_: e.activation(
                            out=dst, in_=pv_, func=AF.Identity, scale=gs[:, fc, s:s + 1],
                            bias=modT[:, fc, s:s + 1]), reads=[b_PB[bk], b_mod], writes=[b_uT[fc][j]])

        def pd_alloc(nb=2):
            one = lambda mk: [mk() for _ in range(nb)] * (2 // nb)
            return dict(xr=one(lambda: A.alloc(D, F32)), yt=one(lambda: A.alloc(D, F32)),
                        ot=one(lambda: A.alloc(D, F32)), junkd=A.alloc(D, BF16),
                        std=one(lambda: A.alloc(4, F32)),
                        b_xr=one(Buf), b_yt=one(Buf), b_ot=one(Buf), b_std=one(Buf), b_junkd=Buf())

        def pd_tile(st, j, Z, load_only=False, stages=("pe", "mul", "add", "stats", "fin")):
            tp, tok0 = st["tp"], st["tok0"]
            nn = min(512, st["T"])
            p = j % 2
            xr, yt, ot, junkd, std = Z["xr"][p], Z["yt"][p], Z["ot"][p], Z["junkd"], Z["std"][p]
            b_xr, b_yt, b_ot, b_std, b_junkd = Z["b_xr"][p], Z["b_yt"][p], Z["b_ot"][p], Z["b_std"][p], Z["b_junkd"]
            ntj = (j * tp) // nn
            rows = slice(tok0 + j * tp, tok0 + (j + 1) * tp)
            if load_only:
                if not os.environ.get("DIAG_NOXR"):
                    T.dma("sp", xr[0:tp, :], x_d[rows, :], [], [b_xr], "xr%d" % p)
                return
            for half in range(2):
                bk = 4 + 2 * p + half
                if "pe" in stages:
                    for kc in range(8):
                        T.op("pe", lambda e, bk=bk, kc=kc, half=half: e.matmul(
                            PB[bk][0:tp, :], lhsT=mgT[:, kc, j * tp:(j + 1) * tp], rhs=wo_bf[:, kc, half * 512:(half + 1) * 512],
                            start=(kc == 0), stop=(kc == 7)),
                            reads=[b_mgT[kc][ntj], b_wo], writes=[b_PB[bk]], signal=(kc == 7))
                if "mul" in stages:
                    T.op("dve", lambda e, bk=bk, half=half: e.tensor_tensor(
                        out=yt[0:tp, half * 512:(half + 1) * 512], in0=PB[bk][0:tp, :],
                        in1=xr[0:tp, half * 512:(half + 1) * 512], op=ALU.add),
                        reads=[b_PB[bk], b_xr], writes=[b_yt])
            if "stats" in stages:
                T.op("act", lambda e: e.activation(out=junkd[0:tp, :], in_=yt[0:tp, :], func=AF.Square,
                                                   accum_out=std[0:tp, 0:1]), reads=[b_yt], writes=[b_junkd, b_std])
                T.op("act", lambda e: e.activation(out=std[0:tp, 1:2], in_=std[0:tp, 0:1], func=AF.Ln,
                                                   scale=1.0 / D, bias=EPS), reads=[b_std], writes=[b_std])
                T.op("act", lambda e: e.activation(out=std[0:tp, 2:3], in_=std[0:tp, 1:2], func=AF.Exp,
                                                   scale=-0.5), reads=[b_std], writes=[b_std])
            if "fin" in stages:
                T.op("dve", lambda e: e.scalar_tensor_tensor(out=ot[0:tp, :], in0=yt[0:tp, :], scalar=std[0:tp, 2:3],
                                                             in1=gfin_bc[0:tp, :], op0=ALU.mult, op1=ALU.mult),
                     reads=[b_yt, b_std, b_const], writes=[b_ot])
                out_toks["y%d" % p] = T.dma(STQ, y_d[rows, :], ot[0:tp, :], [b_ot], [], "yo%d" % p)

        def run_st(idx):
            st = sts[idx]
            s = st["seq"]
            if not st["prompt"]:
                A.size = RS_OFF
            Tn = st["T"]
            tp = st["tp"]
            L = tp
            nch = Tn // L
            nn = min(512, Tn)
            nnt = Tn // nn
            tpn = nn // tp
            tok0 = st["tok0"]

            def uT_bufs(kc, nt):
                return [b_uT[kc][nt * tpn + i] for i in range(tpn)]

            if st["first"]:
                if st["prompt"]:
                    for h in range(4):
                        T.op("pool", lambda e, h=h: e.memset(Cst[:, h].rearrange("p c v -> p (c v)"), 0.0),
                             writes=[b_C[h]])
                    T.op("pool", lambda e: e.memset(mstate[0:4, :], 0.0), writes=[b_ms])
                    T.op("pool", lambda e: e.memset(hcar, 0.0), writes=b_hcar)
                    T.op("pool", lambda e: e.memset(convcar.rearrange("p f j -> p (f j)"), 0.0), writes=b_ccar)
                else:
                    T.dma("sp", Cst.rearrange("p h c v -> p (h c v)"), sCT_d, [], b_C, "s_sCT")
                    T.dma("sp", mstate[0:4, :], sm_d, [], [b_ms], "s_sm")
                    T.dma("sp", hcar, sh_d, [], b_hcar, "s_sh")
                    T.dma("sp", convcar.rearrange("p f j -> p (f j)"), sconv_d, [], b_ccar, "s_sconv")
                gd = A.alloc(D, F32).rearrange("p (f j) -> p f j", f=8)
                b_gd = Buf()
                T.op("dve", lambda e: e.tensor_tensor(
                    out=gd, in0=modT[:, 16:24, s:s + 1].to_broadcast([128, 8, 128]),
                    in1=ident_f.unsqueeze(1).to_broadcast([128, 8, 128]), op=ALU.mult),
                    reads=[b_mod, b_const], writes=[b_gd])
                for half in range(2):
                    T.op("pe", lambda e, half=half: e.matmul(
                        PB[half][:, :], lhsT=ones_f, rhs=gd.rearrange("p f j -> p (f j)")[:, half * 512:(half + 1) * 512],
                        start=True, stop=True), reads=[b_gd, b_const], writes=[b_PB[half]])
                    T.op("act", lambda e, half=half: e.activation(
                        out=gate_bc[:, half * 512:(half + 1) * 512], in_=PB[half][:, :], func=AF.Copy),
                        reads=[b_PB[half]], writes=[b_gbc])
                T.barrier()
                A.off = PERSIST_END
                if s > 0:
                    T.dma("pool", wo_bf.rearrange("p k n -> p (k n)"), wo_d, [], [b_wo], "s_wo")
                fold_wo[0] = True

            if idx == 0:
                gate()
                Z0 = p0_alloc()
                p0_tile(st, 0, Z0, load_only=True)
                for j in range(Tn // tp):
                    if j + 1 < Tn // tp:
                        p0_tile(st, j + 1, Z0, load_only=True)
                    p0_tile(st, j, Z0)
                T.barrier()
                A.off = PERSIST_END

            gate()
            igb = A.alloc(Tn, F32)[0:4, :]
            cs = A.alloc(Tn, F32)[0:4, :]
            gg = A.alloc(Tn, F32)[0:4, :]
            esc = A.alloc(Tn, F32)[0:4, :]
            thr = A.alloc(Tn, F32)[0:4, :]
            sm_ = A.alloc(64, F32)[0:4, :]
            gmax, Ac, Bc, mall, Mc, mprev, dec = (sm_[:, i * 8:i * 8 + nch] for i in range(7))
            decd = A.alloc(32, F32)[0:4, 0:4 * nch].rearrange("p (a c) -> p a c", a=4)
            escT = A.alloc(64, F32)[:, 0:nch * 8].rearrange("p (c j) -> p c j", j=8)
            decbc = A.alloc(32, F32)[:, 0:4 * nch].rearrange("p (a c) -> p a c", a=4)
            b_g = Buf()
            b_escT = Buf()
            b_decbc = Buf()
            T.deferring = True
            for nt in range(nnt):
                sl = slice(nt * nn, (nt + 1) * nn)
                bi, bf_ = 6, 7
                for (bk, c0) in ((bi, 0), (bf_, 4)):
                    for kc in range(8):
                        T.op("pe", lambda e, bk=bk, c0=c0, kc=kc, sl=sl: e.matmul(
                            PB[bk][0:4, 0:nn], lhsT=wif_bf[:, kc, c0:c0 + 4], rhs=uT[:, kc, sl],
                            start=(kc == 0), stop=(kc == 7)),
                            reads=[b_wc] + uT_bufs(kc, nt), writes=[b_PB[bk]], signal=(kc == 7))
                T.op("dve", lambda e, bi=bi, sl=sl: e.tensor_scalar(
                    out=igb[:, sl], in0=PB[bi][0:4, 0:nn], scalar1=bif[0:4, 0:1], scalar2=None, op0=ALU.add),
                    reads=[b_PB[bi], b_const], writes=[b_g])
                T.op("act", lambda e, bf_=bf_, sl=sl: e.activation(
                    out=gg[:, sl], in_=PB[bf_][0:4, 0:nn], func=AF.Exp, scale=-1.0, bias=nbf[0:4, :]),
                    reads=[b_PB[bf_], b_const], writes=[b_g])
                T.op("act", lambda e, sl=sl: e.activation(out=gg[:, sl], in_=gg[:, sl], func=AF.Ln, bias=1.0),
                     reads=[b_g], writes=[b_g])
            T.op("dve", lambda e: e.tensor_tensor_scan(out=cs, data0=rmask[0:4, 0:Tn], data1=gg, initial=0.0,
                                                       op0=ALU.mult, op1=ALU.add), reads=[b_g, b_const], writes=[b_g])
            T.op("dve", lambda e: e.tensor_tensor(out=gg, in0=igb, in1=cs, op=ALU.add), reads=[b_g], writes=[b_g])
            T.op("dve", lambda e: e.tensor_reduce(out=gmax, in_=gg.rearrange("p (c l) -> p c l", l=L), axis=AX.X,
                                                  op=ALU.max), reads=[b_g], writes=[b_g])
            T.op("dve", lambda e: e.tensor_scalar(out=Ac, in0=cs.rearrange("p (c l) -> p c l", l=L)[:, :, L - 1],
                                                  scalar1=-1.0, scalar2=None, op0=ALU.mult), reads=[b_g], writes=[b_g])
            T.op("dve", lambda e: e.tensor_tensor(out=Bc, in0=gmax, in1=Ac, op=ALU.add), reads=[b_g], writes=[b_g])
            T.op("dve", lambda e: e.tensor_tensor_scan(out=mall, data0=Ac, data1=Bc, initial=mstate[0:4, :],
                                                       op0=ALU.add, op1=ALU.max), reads=[b_g, b_ms], writes=[b_g])
            T.op("dve", lambda e: e.tensor_tensor(out=Mc, in0=mall, in1=Ac, op=ALU.subtract), reads=[b_g], writes=[b_g])
            T.op("dve", lambda e: e.tensor_copy(out=mprev[:, 0:1], in_=mstate[0:4, :]), reads=[b_ms], writes=[b_g])
            if nch > 1:
                T.op("dve", lambda e: e.tensor_copy(out=mprev[:, 1:nch], in_=mall[:, 0:nch - 1]),
                     reads=[b_g], writes=[b_g])
            T.op("dve", lambda e: e.tensor_copy(out=mstate[0:4, :], in_=mall[:, nch - 1:nch]),
                 reads=[b_g], writes=[b_ms])
            T.op("dve", lambda e: e.tensor_tensor(out=dec, in0=mprev, in1=Mc, op=ALU.subtract), reads=[b_g], writes=[b_g])
            T.op("act", lambda e: e.activation(out=dec, in_=dec, func=AF.Exp), reads=[b_g], writes=[b_g])
            Mcb = Mc.unsqueeze(2).to_broadcast([4, nch, L])
            T.op("dve", lambda e: e.tensor_tensor(out=esc.rearrange("p (c l) -> p c l", l=L),
                                                  in0=gg.rearrange("p (c l) -> p c l", l=L), in1=Mcb, op=ALU.subtract),
                 reads=[b_g], writes=[b_g])
            T.op("act", lambda e: e.activation(out=esc, in_=esc, func=AF.Exp), reads=[b_g], writes=[b_g])
            T.op("dve", lambda e: e.tensor_tensor(out=thr.rearrange("p (c l) -> p c l", l=L),
                                                  in0=cs.rearrange("p (c l) -> p c l", l=L), in1=Mcb, op=ALU.subtract),
                 reads=[b_g], writes=[b_g])
            T.op("act", lambda e: e.activation(out=thr, in_=thr, func=AF.Exp), reads=[b_g], writes=[b_g])
            bkE = 6
            psE = PB[bkE][:, 0:nch * 8].rearrange("p (c j) -> p c j", j=8)
            for c in range(nch):
                for (srcv, c0) in ((esc, 0), (thr, 4)):
                    T.op("pe", lambda e, c=c, srcv=srcv, c0=c0: e.transpose(
                        out=psE[0:L, c, c0:c0 + 4], in_=srcv[:, c * L:(c + 1) * L], identity=ident_f[0:4, 0:4]),
                        reads=[b_g, b_const], writes=[b_PB[bkE]], signal=(c == nch - 1 and c0 == 4))
            T.op("dve", lambda e: e.tensor_copy(out=escT[0:L], in_=psE[0:L]), reads=[b_PB[bkE]], writes=[b_escT])
            T.op("dve", lambda e: e.tensor_tensor(
                out=decd, in0=dec.unsqueeze(1).to_broadcast([4, 4, nch]),
                in1=ident_f[0:4, 0:4].unsqueeze(2).to_broadcast([4, 4, nch]), op=ALU.mult),
                reads=[b_g, b_const], writes=[b_g])
            bkD = 7
            T.op("pe", lambda e: e.matmul(PB[bkD][:, 0:4 * nch], lhsT=ones_f[0:4, :],
                                          rhs=decd.rearrange("p a c -> p (a c)"), start=True, stop=True),
                 reads=[b_g, b_const], writes=[b_PB[bkD]])
            T.op("dve", lambda e: e.tensor_copy(out=decbc.rearrange("p a c -> p (a c)"), in_=PB[bkD][:, 0:4 * nch]),
                 reads=[b_PB[bkD]], writes=[b_decbc])
            if fold_wo[0]:
                fold_wo[0] = False
                for kc in range(8):
                    T.op("dve", lambda e, kc=kc: e.tensor_tensor(out=wo_bf[:, kc, :], in0=wo_bf[:, kc, :], in1=gate_bc,
                                                                 op=ALU.mult), reads=[b_wo, b_gbc], writes=[b_wo])
            T.deferring = False
            A1_END = A.off

            for g in range(2):
                gate()
                if g == 0:
                    A.off = A1_END
                    qT = A.alloc(4 * Tn, BF16).rearrange("p (i t) -> p i t", i=4)
                    kT = A.alloc(4 * Tn, BF16).rearrange("p (i t) -> p i t", i=4)
                    vT = A.alloc(4 * Tn, BF16).rearrange("p (i t) -> p i t", i=4)
                    gaT = A.alloc(4 * Tn, BF16).rearrange("p (i t) -> p i t", i=4)
                    szt = [A.alloc(nn, F32) for _ in range(2)]
                    tzt = [A.alloc(nn, F32) for _ in range(2)]
                    b_szt, b_tzt = bufs(2), bufs(2)
                    b_q = [bufs(nnt) for _ in range(4)]
                    b_k = [bufs(nnt) for _ in range(4)]
                    b_v = [bufs(nnt) for _ in range(4)]
                    b_ga = [bufs(nnt) for _ in range(4)]
                ecnt = [0]

                def ev_copy(dstT, bdst, i):
                    def f(bk, nt):
                        ecnt[0] += 1
                        dst = dstT[:, i, nt * nn:(nt + 1) * nn]
                        if ecnt[0] % 2 == 0:
                            T.op("act", lambda e: e.activation(out=dst, in_=PB[bk][:, 0:nn], func=AF.Copy),
                                 reads=[b_PB[bk]], writes=[bdst[i][nt]])
                        else:
                            T.op("dve", lambda e: e.tensor_copy(out=dst, in_=PB[bk][:, 0:nn]),
                                 reads=[b_PB[bk]], writes=[bdst[i][nt]])
                    return f

                def ev_k(i):
                    def f(bk, nt):
                        dst = kT[:, i, nt * nn:(nt + 1) * nn]
                        T.op("dve", lambda e: e.tensor_scalar(out=dst, in0=PB[bk][:, 0:nn], scalar1=DH ** -0.5,
                                                              scalar2=None, op0=ALU.mult),
                             reads=[b_PB[bk]], writes=[b_k[i][nt]])
                    return f

                def ev_o(i):
                    def f(bk, nt):
                        dst = gaT[:, i, nt * nn:(nt + 1) * nn]
                        T.op("act", lambda e: e.activation(out=dst, in_=PB[bk][:, 0:nn], func=AF.Sigmoid),
                             reads=[b_PB[bk]], writes=[b_ga[i][nt]])
                    return f

                def ev_z(i):
                    def f(bk, nt):
                        ecnt[0] += 1
                        p = ecnt[0] % 2
                        dst = gaT[:, i, nt * nn:(nt + 1) * nn]
                        fcg = g * 4 + i
                        T.op("act", lambda e: e.activation(out=szt[p], in_=PB[bk][:, 0:nn], func=AF.Sigmoid),
                             reads=[b_PB[bk]], writes=[b_szt[p]])
                        T.op("dve", lambda e: e.scalar_tensor_tensor(
                            out=tzt[p], in0=PB[bk][:, 0:nn], scalar=pv[:, PV_GH + fcg:PV_GH + fcg + 1], in1=szt[p],
                            op0=ALU.mult, op1=ALU.mult), reads=[b_PB[bk], b_szt[p], b_const], writes=[b_tzt[p]])
                        T.op("dve", lambda e: e.tensor_tensor(out=dst, in0=tzt[p], in1=dst, op=ALU.mult),
                             reads=[b_tzt[p], b_ga[i][nt]], writes=[b_ga[i][nt]])
                    return f

                if g == 0:
                    bank_pool[0] = [0, 1, 2, 3, 4, 5]
                    nfl = -(-len(T.deferred) // 18)
                for (kind, evf) in (("q", lambda i: ev_copy(qT, b_q, i)), ("k", ev_k),
                                    ("v", lambda i: ev_copy(vT, b_v, i)), ("o", ev_o), ("z", ev_z)):
                    wv, bw = ring_next()
                    for i in range(4):
                        proj(wv, bw, i, uT, uT_bufs, Tn, nn, evf(i))
                        if g == 0:
                            T.flush(nfl)
                if g == 0:
                    T.flush()
                    bank_pool[0] = list(range(8))

                gate()
                NP3 = 3
                if g == 0:
                    ktok = [A.alloc(256, BF16) for _ in range(NP3)]
                    vpp = [A.alloc(258, BF16) for _ in range(NP3)]
                    SmT = [A.alloc(128, BF16) for _ in range(NP3)]
                    hn = [A.alloc(512, BF16) for _ in range(2)]
                    st3 = [A.alloc(8, F32) for _ in range(NP3)]
                    junk3 = A.alloc(256, BF16)
                    b_ktok, b_vpp, b_SmT, b_hn, b_st3 = bufs(NP3), bufs(NP3), bufs(NP3), bufs(2), bufs(NP3)
                    b_junk3 = Buf()
                iters = [(c, hh) for c in range(nch) for hh in range(2)]

                def views(it):
                    c, hh = iters[it]
                    p = it % NP3
                    bankT = PB[p][:, 0:256].bitcast(BF16)
                    return dict(c=c, hh=hh, h=2 * g + hh, p=p, tsl=slice(c * L, (c + 1) * L), ntc=(c * L) // nn,
                                i0=hh * 2, pk=bankT[:, 0:256], pvv=bankT[:, 256:512], psS=PB[p][:, 256:384],
                                pdn=PB[p][:, 384:386], psN=PB[3 + p][:, 0:257],
                                psD=PB[6][:, :].rearrange("p (c v) -> p c v", c=2),
                                bT=b_PB[p], bN=b_PB[3 + p], bD=b_PB[6])

                def pe1(it):
                    V = views(it)
                    i0, tsl, ntc, p = V["i0"], V["tsl"], V["ntc"], V["p"]
                    for dc in range(2):
                        T.op("pe", lambda e, dc=dc: e.transpose(out=V["pk"][0:L, dc * 128:(dc + 1) * 128],
                                                                in_=kT[:, i0 + dc, tsl], identity=ident_bf),
                             reads=[b_k[i0 + dc][ntc], b_const], writes=[V["bT"]], signal=False)
                    for dc in range(2):
                        T.op("pe", lambda e, dc=dc: e.transpose(out=V["pvv"][0:L, dc * 128:(dc + 1) * 128],
                                                                in_=vT[:, i0 + dc, tsl], identity=ident_bf),
                             reads=[b_v[i0 + dc][ntc], b_const], writes=[V["bT"]], signal=False)
                    for dc in range(2):
                        T.op("pe", lambda e, dc=dc: e.matmul(V["psS"][0:L, 0:L], lhsT=kT[:, i0 + dc, tsl],
                                                             rhs=qT[:, i0 + dc, tsl], start=(dc == 0), stop=(dc == 1)),
                             reads=[b_k[i0 + dc][ntc], b_q[i0 + dc][ntc]], writes=[V["bT"]], signal=(dc == 1))

                def ev1(it):
                    V = views(it)
                    c, h, p = V["c"], V["h"], V["p"]
                    T.op("dve", lambda e: e.tensor_copy(out=ktok[p][0:L, :], in_=V["pk"][0:L, :]),
                         reads=[V["bT"]], writes=[b_ktok[p]])
                    T.op("dve", lambda e: e.tensor_scalar(out=vpp[p][0:L, 0:256], in0=V["pvv"][0:L, :],
                                                          scalar1=escT[0:L, c, h:h + 1], scalar2=None, op0=ALU.mult),
                         reads=[V["bT"], b_escT], writes=[b_vpp[p]])
                    T.op("dve", lambda e: e.tensor_copy(out=vpp[p][0:L, 256:257], in_=escT[0:L, c, h:h + 1]),
                         reads=[b_escT, b_vpp[p]], writes=[b_vpp[p]])
                    T.op("dve", lambda e: e.tensor_tensor(out=SmT[p][0:L, 0:L], in0=V["psS"][0:L, 0:L],
                                                          in1=maskT[0:L, 0:L], op=ALU.mult),
                         reads=[V["bT"], b_const], writes=[b_SmT[p]])

                def cbf(it):
                    V = views(it)
                    c, h = V["c"], V["h"]
                    T.op("act", lambda e: e.activation(out=Cbf[:, h, :, 0:257], in_=Cst[:, h], func=AF.Identity,
                                                       scale=decbc[:, h, c:c + 1]),
                         reads=[b_C[h], b_decbc], writes=[b_Cbf[h]])

                def pe2(it):
                    V = views(it)
                    c, h, p, i0, tsl, ntc = V["c"], V["h"], V["p"], V["i0"], V["tsl"], V["ntc"]
                    psN, psD, pdn = V["psN"], V["psD"], V["pdn"]
                    for dc in range(2):
                        T.op("pe", lambda e, dc=dc: e.matmul(psN[0:L, :], lhsT=qT[:, i0 + dc, tsl],
                                                             rhs=Cbf[:, h, dc, 0:257], start=(dc == 0), stop=False),
                             reads=[b_q[i0 + dc][ntc], b_Cbf[h]], writes=[V["bN"]], signal=False)
                    T.op("pe", lambda e: e.matmul(psN[0:L, :], lhsT=SmT[p][0:L, 0:L], rhs=vpp[p][0:L, 0:257],
                                                  start=False, stop=True),
                         reads=[b_SmT[p], b_vpp[p]], writes=[V["bN"]])
                    for dc in range(2):
                        T.op("pe", lambda e, dc=dc: e.matmul(psD[:, dc, :], lhsT=ktok[p][0:L, dc * 128:(dc + 1) * 128],
                                                             rhs=vpp[p][0:L, 0:256], start=True, stop=True),
                             reads=[b_ktok[p], b_vpp[p]], writes=[V["bD"]], signal=(dc == 1))
                    for dc in range(2):
                        T.op("pe", lambda e, dc=dc: e.matmul(pdn[:, dc:dc + 1], lhsT=ktok[p][0:L, dc * 128:(dc + 1) * 128],
                                                             rhs=vpp[p][0:L, 256:257], start=True, stop=True),
                             reads=[b_ktok[p], b_vpp[p]], writes=[V["bT"]], signal=(dc == 1))

                def upd(it):
                    V = views(it)
                    c, h = V["c"], V["h"]
                    T.op("dve", lambda e: e.scalar_tensor_tensor(
                        out=Cst[:, h, :, 0:256], in0=Cst[:, h, :, 0:256], scalar=decbc[:, h, c:c + 1], in1=V["psD"],
                        op0=ALU.mult, op1=ALU.add), reads=[b_C[h], b_decbc, V["bD"]], writes=[b_C[h]])
                    T.op("dve", lambda e: e.scalar_tensor_tensor(
                        out=Cst[:, h, :, 256], in0=Cst[:, h, :, 256], scalar=decbc[:, h, c:c + 1], in1=V["pdn"],
                        op0=ALU.mult, op1=ALU.add), reads=[b_C[h], b_decbc, V["bT"]], writes=[b_C[h]])

                def normA(it):
                    V = views(it)
                    p, sv = V["p"], st3[V["p"]]
                    T.op("act", lambda e: e.activation(out=sv[0:L, 5:6], in_=V["psN"][0:L, 256:257], func=AF.Abs),
                         reads=[V["bN"], b_st3[p]], writes=[b_st3[p]])

                def normB(it):
                    V = views(it)
                    c, h, p, sv = V["c"], V["h"], V["p"], st3[V["p"]]
                    T.op("dve", lambda e: e.tensor_tensor(out=sv[0:L, 1:2], in0=sv[0:L, 5:6],
                                                          in1=escT[0:L, c, 4 + h:5 + h], op=ALU.max),
                         reads=[b_escT, b_st3[p]], writes=[b_st3[p]])
                    T.op("dve", lambda e: e.reciprocal(out=sv[0:L, 2:3], in_=sv[0:L, 1:2]),
                         reads=[b_st3[p]], writes=[b_st3[p]])

                def normC(it):
                    V = views(it)
                    p, sv = V["p"], st3[V["p"]]
                    T.op("act", lambda e: e.activation(out=junk3[0:L, :], in_=V["psN"][0:L, 0:256], func=AF.Square,
                                                       scale=sv[0:L, 2:3], accum_out=sv[0:L, 0:1]),
                         reads=[V["bN"], b_st3[p]], writes=[b_junk3, b_st3[p]])
                    T.op("act", lambda e: e.activation(out=sv[0:L, 3:4], in_=sv[0:L, 0:1], func=AF.Ln,
                                                       scale=1.0 / DH, bias=EPS),
                         reads=[b_st3[p]], writes=[b_st3[p]])
                    T.op("act", lambda e: e.activation(out=sv[0:L, 6:7], in_=sv[0:L, 3:4], func=AF.Exp, scale=-0.5),
                         reads=[b_st3[p]], writes=[b_st3[p]])

                def normD(it):
                    V = views(it)
                    p, sv = V["p"], st3[V["p"]]
                    T.op("dve", lambda e: e.tensor_tensor(out=sv[0:L, 4:5], in0=sv[0:L, 6:7], in1=sv[0:L, 2:3],
                                                          op=ALU.mult),
                         reads=[b_st3[p]], writes=[b_st3[p]])

                def normE(it):
                    V = views(it)
                    c, hh, p, sv = V["c"], V["hh"], V["p"], st3[V["p"]]
                    hb = (c % 2)
                    T.op("act", lambda e: e.activation(out=hn[hb][0:L, hh * 256:(hh + 1) * 256], in_=V["psN"][0:L, 0:256],
                                                       func=AF.Identity, scale=sv[0:L, 4:5]),
                         reads=[V["bN"], b_st3[p]], writes=[b_hn[hb]])

                def chunk_end(c):
                    tsl = slice(c * L, (c + 1) * L)
                    ntc = (c * L) // nn
                    hb = (c % 2)
                    psH = PB[7][:, 0:256].bitcast(BF16).rearrange("p (i t) -> p i t", i=4)
                    for i in range(4):
                        T.op("pe", lambda e, i=i: e.transpose(out=psH[:, i, 0:L], in_=hn[hb][0:L, i * 128:(i + 1) * 128],
                                                              identity=ident_bf[0:L, 0:L]),
                             reads=[b_hn[hb], b_const], writes=[b_PB[7]], signal=(i == 3))
                    T.op("dve", lambda e: e.tensor_tensor(out=yaT[:, g * 4:(g + 1) * 4, tsl], in0=psH[:, :, 0:L],
                                                          in1=gaT[:, :, tsl], op=ALU.mult),
                         reads=[b_PB[7]] + [b_ga[i][ntc] for i in range(4)],
                         writes=[b_yaT[g * 4 + i][c] for i in range(4)])

                NI = len(iters)

                def okk(t):
                    return 0 <= t < NI

                pe1(0)
                ev1(0)
                for t in range(NI + 3):
                    if okk(t - 3) and iters[t - 3][1] == 1:
                        chunk_end(iters[t - 3][0])
                    if okk(t):
                        cbf(t)
                    if okk(t + 1):
                        pe1(t + 1)
                    if okk(t - 1):
                        normB(t - 1)
                    if okk(t - 2):
                        normD(t - 2)
                    if okk(t + 1):
                        ev1(t + 1)
                    if okk(t):
                        pe2(t)
                        upd(t)
                    if okk(t - 1):
                        normC(t - 1)
                    if okk(t - 2):
                        normE(t - 2)
                    if okk(t):
                        normA(t)
            T.barrier()
            A.off = PERSIST_END

            gate()
            s1 = [A.alloc(nn, F32) for _ in range(2)]
            s2 = [A.alloc(nn, F32) for _ in range(2)]
            depth = dict(xbp=2, szb=8, xc=5, rr=5, ii=5, mu=3)
            pools = {}
            for kname, dep in depth.items():
                n_el = nn + 4 if kname == "xbp" else nn
                dt_ = BF16 if kname == "xcb" else F32
                pools[kname] = [(A.alloc(n_el, dt_), Buf()) for _ in range(dep)]

            def PBf(kname, u):
                return pools[kname][u % depth[kname]]

            units = [(fc, nt) for fc in range(8) for nt in range(nnt)]
            NU = len(units)
            slotw = {}
            zbank = {}

            def pe_proj(u):
                fc, nt = units[u]
                if fc % 2 == 0 and nt == 0:
                    slotw["cur"] = ring_next(dual=True)
                wv, bw = slotw["cur"]
                bks = []
                for kind in range(2):
                    bk = 2 * kind + (u % 2)
                    bks.append(bk)
                    for kc in range(8):
                        T.op("pe", lambda e, kc=kc: e.matmul(
                            PB[bk][:, 0:nn], lhsT=wv[:, (fc % 2) * 2 + kind, kc, :], rhs=uT[:, kc, nt * nn:(nt + 1) * nn],
                            start=(kc == 0), stop=(kc == 7)),
                            reads=[bw] + uT_bufs(kc, nt), writes=[b_PB[bk]], signal=(kc == 7))
                zbank[u] = bks

            def act_p(u):
                xbp, b_xbp = PBf("xbp", u)
                szb, b_szb = PBf("szb", u)
                bx, bz = zbank[u]
                T.op("act", lambda e: e.activation(out=xbp[:, 3:3 + nn], in_=PB[bx][:, 0:nn], func=AF.Copy),
                     reads=[b_PB[bx]], writes=[b_xbp])
                T.op("act", lambda e: e.activation(out=szb, in_=PB[bz][:, 0:nn], func=AF.Sigmoid),
                     reads=[b_PB[bz]], writes=[b_szb])

            def dve_silu(u):
                szb, b_szb = PBf("szb", u)
                bz = zbank[u][1]
                T.op("dve", lambda e: e.tensor_tensor(out=szb, in0=PB[bz][:, 0:nn], in1=szb, op=ALU.mult),
                     reads=[b_PB[bz], b_szb], writes=[b_szb])

            def dve_conv(u):
                fc, nt = units[u]
                xbp, b_xbp = PBf("xbp", u)
                xc, b_xc = PBf("xc", u)
                cw = lambda j: pv[:, PV_CW + j * 8 + fc: PV_CW + j * 8 + fc + 1]
                ce = "pool"
                T.op(ce, lambda e: e.tensor_copy(out=xbp[:, 0:3], in_=convcar[:, fc, :]),
                     reads=[b_ccar[fc]], writes=[b_xbp])
                T.op(ce, lambda e: e.tensor_copy(out=convcar[:, fc, :], in_=xbp[:, nn:nn + 3]),
                     reads=[b_xbp], writes=[b_ccar[fc]])
                if OPT_B & 2:
                    T.op("act", lambda e: e.activation(out=xc, in_=xbp[:, 0:nn], func=AF.Identity, scale=cw(0),
                                                       bias=pv[:, PV_CB + fc:PV_CB + fc + 1]),
                         reads=[b_xbp, b_const], writes=[b_xc])
                else:
                    T.op("dve", lambda e: e.scalar_tensor_tensor(
                        out=xc, in0=xbp[:, 0:nn], scalar=cw(0), in1=pv[:, PV_CB + fc:PV_CB + fc + 1].to_broadcast([128, nn]),
                        op0=ALU.mult, op1=ALU.add), reads=[b_xbp, b_const], writes=[b_xc])
                for j in range(1, 4):
                    T.op("dve", lambda e, j=j: e.scalar_tensor_tensor(out=xc, in0=xbp[:, j:j + nn], scalar=cw(j),
                                                                      in1=xc, op0=ALU.mult, op1=ALU.add),
                         reads=[b_xbp, b_xc, b_const], writes=[b_xc])

            def pool_cast(u):
                return

            ribank = {}

            def pe_ri(u):
                fc, nt = units[u]
                xcb, b_xcb = PBf("xc", u)
                br_, bi_ = 4, 5
                ribank[u] = (br_, bi_)
                T.op("pe", lambda e: e.matmul(PB[br_][:, 0:nn], lhsT=wr_bf[:, 0, fc, :], rhs=xcb,
                                              start=True, stop=True), reads=[b_xcb, b_wc], writes=[b_PB[br_]])
                T.op("pe", lambda e: e.matmul(PB[bi_][:, 0:nn], lhsT=wr_bf[:, 1, fc, :], rhs=xcb,
                                              start=True, stop=True), reads=[b_xcb, b_wc], writes=[b_PB[bi_]])

            def act_ri(u):
                fc, nt = units[u]
                rr, b_rr = PBf("rr", u)
                ii, b_ii = PBf("ii", u)
                br_, bi_ = ribank[u]
                T.op("act", lambda e: e.activation(out=rr, in_=PB[br_][:, 0:nn], func=AF.Sigmoid,
                                                   bias=pv[:, PV_BRA + fc:PV_BRA + fc + 1]),
                     reads=[b_PB[br_], b_const], writes=[b_rr])
                T.op("act", lambda e: e.activation(out=ii, in_=PB[bi_][:, 0:nn], func=AF.Sigmoid,
                                                   bias=pv[:, PV_BRX + fc:PV_BRX + fc + 1]),
                     reads=[b_PB[bi_], b_const], writes=[b_ii])

            def act_exp(u):
                fc, nt = units[u]
                rr, b_rr = PBf("rr", u)
                mu, b_mu = PBf("mu", u)
                T.op("act", lambda e: e.activation(out=rr, in_=rr, func=AF.Exp, scale=cfeat[:, fc:fc + 1]),
                     reads=[b_rr, b_const], writes=[b_rr])
                if OPT_SQ:
                    T.op("pool", lambda e: e.tensor_tensor(out=mu, in0=rr, in1=rr, op=ALU.mult), reads=[b_rr], writes=[b_mu])
                else:
                    T.op("act", lambda e: e.activation(out=mu, in_=rr, func=AF.Square), reads=[b_rr], writes=[b_mu])
                T.op("act", lambda e: e.activation(out=mu, in_=mu, func=AF.Ln, scale=-1.0, bias=1.0),
                     reads=[b_mu], writes=[b_mu])
                T.op("act", lambda e: e.activation(out=mu, in_=mu, func=AF.Exp, scale=0.5), reads=[b_mu], writes=[b_mu])

            def pool_mul(u):
                ii, b_ii = PBf("ii", u)
                xc, b_xc = PBf("xc", u)
                T.op("pool", lambda e: e.tensor_tensor(out=ii, in0=ii, in1=xc, op=ALU.mult),
                     reads=[b_ii, b_xc], writes=[b_ii])

            def dve_r(u):
                fc, nt = units[u]
                rr, b_rr = PBf("rr", u)
                ii, b_ii = PBf("ii", u)
                mu, b_mu = PBf("mu", u)
                if st["prompt"] and st["first"] and nt == 0:
                    T.op("dve", lambda e: e.memset(mu[:, 0:1], 1.0), reads=[b_mu], writes=[b_mu])
                T.op("dve", lambda e: e.tensor_tensor(out=ii, in0=ii, in1=mu, op=ALU.mult),
                     reads=[b_ii, b_mu], writes=[b_ii])
                T.op("dve", lambda e: e.tensor_tensor_scan(out=mu, data0=rr, data1=ii,
                                                           initial=hcar[:, fc:fc + 1], op0=ALU.mult, op1=ALU.add),
                     reads=[b_rr, b_ii, b_hcar[fc], b_mu], writes=[b_mu])
                T.op("pool", lambda e: e.tensor_copy(out=hcar[:, fc:fc + 1], in_=mu[:, nn - 1:nn]),
                     reads=[b_mu], writes=[b_hcar[fc]])

            def pool_y(u):
                fc, nt = units[u]
                szb, b_szb = PBf("szb", u)
                mu, b_mu = PBf("mu", u)
                T.op("pool", lambda e: e.tensor_tensor(out=ybT[:, fc, nt * nn:(nt + 1) * nn], in0=szb, in1=mu,
                                                       op=ALU.mult),
                     reads=[b_szb, b_mu], writes=[b_ybT[fc][nt]])

            def ok(u):
                return 0 <= u < NU

            b_s1, b_s2 = bufs(2), bufs(2)
            slotc = {}

            def cprime(gi):
                m, nt = divmod(gi, nnt)
                if m % 2 == 0 and nt == 0:
                    slotc["cur"] = ring_next(dual=True)
                wv, bw = slotc["cur"]
                base = (m % 2) * 2
                sl = slice(nt * nn, (nt + 1) * nn)
                p = gi % 2
                for (ci, srcT, rb, bk) in ((base, yaT, lambda kc: [b_yaT[kc][nt * tpn + i] for i in range(tpn)], 6),
                                           (base + 1, uT, lambda kc: uT_bufs(kc, nt), 7)):
                    for kc in range(8):
                        T.op("pe", lambda e, ci=ci, kc=kc, srcT=srcT, bk=bk: e.matmul(
                            PB[bk][:, 0:nn], lhsT=wv[:, ci, kc, :], rhs=srcT[:, kc, sl],
                            start=(kc == 0), stop=(kc == 7)),
                            reads=[bw] + rb(kc), writes=[b_PB[bk]], signal=(kc == 7))
                T.op("act", lambda e: e.activation(out=s1[p], in_=PB[7][:, 0:nn], func=AF.Sigmoid,
                                                   bias=pv[:, PV_BG + m:PV_BG + m + 1]),
                     reads=[b_PB[7], b_const], writes=[b_s1[p]])
                T.op("dve", lambda e: e.tensor_tensor(out=mgT[:, m, sl], in0=PB[6][:, 0:nn], in1=s1[p], op=ALU.mult),
                     reads=[b_PB[6], b_s1[p]], writes=[b_mgT[m][nt]])

            NG = 8 * nnt
            for s_ in range(NU + 8):
                if ok(s_ - 4):
                    act_ri(s_ - 4)
                if ok(s_):
                    pe_proj(s_)
                if ok(s_ - 3):
                    pe_ri(s_ - 3)
                if s_ % 2 == 1:
                    for u in (s_ - 6, s_ - 5):
                        if ok(u):
                            act_exp(u)
                if ok(s_ - 1):
                    act_p(s_ - 1)
                if s_ % 2 == 0:
                    for u in (s_ - 7, s_ - 6):
                        if ok(u):
                            dve_r(u)
                if ok(s_ - 2):
                    dve_conv(s_ - 2)
                if ok(s_ - 1):
                    dve_silu(s_ - 1)
                if ok(s_ - 2):
                    pool_cast(s_ - 2)
                if ok(s_ - 5):
                    pool_mul(s_ - 5)
                if s_ % 2 == 0:
                    for u in (s_ - 7, s_ - 6):
                        if ok(u):
                            pool_y(u)
                if s_ < NG:
                    cprime(s_)

            gate()
            cc = 0
            for m in range(8):
                if m % 2 == 0:
                    wv, bw = ring_next()
                base = (m % 2) * 2
                for nt in range(nnt):
                    sl = slice(nt * nn, (nt + 1) * nn)
                    p = cc % 2
                    cc += 1
                    bkb, bkg = next_bank(), next_bank()
                    for (ci, srcT, rb, bk) in ((base, ybT, lambda kc: [b_ybT[kc][0], b_ybT[kc][1]], bkb),
                                               (base + 1, uT, lambda kc: uT_bufs(kc, nt), bkg)):
                        for kc in range(8):
                            T.op("pe", lambda e, ci=ci, kc=kc, srcT=srcT, bk=bk: e.matmul(
                                PB[bk][:, 0:nn], lhsT=wv[:, ci, kc, :], rhs=srcT[:, kc, sl],
                                start=(kc == 0), stop=(kc == 7)),
                                reads=[bw] + rb(kc), writes=[b_PB[bk]], signal=(kc == 7))
                    T.op("act", lambda e: e.activation(out=s2[p], in_=PB[bkg][:, 0:nn], func=AF.Sigmoid,
                                                       bias=pv[:, PV_BG + 8 + m:PV_BG + 8 + m + 1]),
                         reads=[b_PB[bkg], b_const], writes=[b_s2[p]])
                    T.op("dve", lambda e: e.tensor_tensor(out=s2[p], in0=PB[bkb][:, 0:nn], in1=s2[p], op=ALU.mult),
                         reads=[b_PB[bkb], b_s2[p]], writes=[b_s2[p]])
                    T.op("pool", lambda e: e.tensor_tensor(out=mgT[:, m, sl], in0=mgT[:, m, sl], in1=s2[p], op=ALU.add),
                         reads=[b_s2[p], b_mgT[m][nt]], writes=[b_mgT[m][nt]])
            T.barrier()
            A.off = PERSIST_END
            if DBG_DUMP and st["tok0"] == 0:
                dtmp = A.alloc(8 * TT, F32)
                b_dtmp = Buf()
                for di, srcT in enumerate((yaT, ybT, mgT)):
                    T.op("dve", lambda e, srcT=srcT: e.tensor_copy(out=dtmp, in_=srcT.rearrange("p f t -> p (f t)")),
                         writes=[b_dtmp])
                    T.dma("sp", dbg_d[di], dtmp, [b_dtmp], [], "dbgo")
                T.barrier()
                A.off = PERSIST_END

            gate()
            ZD = pd_alloc(2 if st["prompt"] else 1)
            nxt = sts[idx + 1] if idx + 1 < len(sts) else None
            nD = Tn // tp
            nP = (nxt["T"] // nxt["tp"]) if nxt is not None else 0
            Z0 = p0_alloc() if nxt is not None else None
            if nxt is not None and nxt["prompt"]:
                ring_issue_upto(min(total_loads[0] + NS - 1, NLOADS_TOTAL - 1))
            pd_tile(st, 0, ZD, load_only=True)
            if nP > 0:
                p0_tile(nxt, 0, Z0, load_only=True)
            for j in range(max(nD, nP)):
                if j + 1 < nD:
                    pd_tile(st, j + 1, ZD, load_only=True)
                if j + 1 < nP:
                    p0_tile(nxt, j + 1, Z0, load_only=True)
                hd, hp = j < nD, j < nP
                if hd:
                    pd_tile(st, j, ZD, stages=("pe",))
                if hp:
                    p0_tile(nxt, j, Z0, stages=("stats", "xn", "tr"))
                if hd:
                    pd_tile(st, j, ZD, stages=("mul", "add"))
                if hp:
                    p0_tile(nxt, j, Z0, stages=("ev",))
                if hd:
                    pd_tile(st, j, ZD, stages=("stats", "fin"))
            if st["last"]:
                out_toks["oCT%d" % s] = T.dma("sp", oCT_d[s], Cst.rearrange("p h c v -> p (h c v)"), b_C, [], "o_CT")
                out_toks["om%d" % s] = T.dma("sp", om_d[s], mstate[0:4, :], [b_ms], [], "o_m")
                out_toks["oh%d" % s] = T.dma("sp", oh_d[s], hcar, b_hcar, [], "o_h")
                out_toks["oc%d" % s] = T.dma("sp", oconv_d[s], convcar.rearrange("p f j -> p (f j)"), b_ccar, [], "o_c")
            T.barrier()
            A.off = PERSIST_END

        try:
            for i_ in range(len(sts)):
                run_st(i_)
        except StopBuild:
            T.pending = []

        T.barrier()
        T.barrier(engines=["sp"])
    return nc


_CACHE = {}


def _f32(a):
    return np.ascontiguousarray(np.asarray(a, dtype=np.float32))


def kernel(x_prompt, x_sample, c_prompt, c_sample, state_C, state_n, state_m, state_h, state_conv,
           w_mod, b_mod, g_norm, w_in, b_if, g_head, conv_w, conv_b, w_ra, b_ra, w_rx, b_rx, lam,
           w_gate, b_gate, w_out_a, w_out_b, w_o, g_final):
    n = 8
    x_prompt, x_sample = _f32(x_prompt), _f32(x_sample)
    c_prompt, c_sample = _f32(c_prompt), _f32(c_sample)
    w_in0 = _f32(w_in)[0]
    def fm(v, nchunk):
        return np.ascontiguousarray(_f32(v).reshape(nchunk, 128).T)
    pv = np.zeros((128, NPV), np.float32)
    pv[:, PV_BMOD:PV_BMOD + 24] = fm(b_mod[0], 24)
    pv[:, PV_GN:PV_GN + 8] = fm(g_norm[0], 8)
    pv[:, PV_GH:PV_GH + 8] = fm(g_head[0], 8)
    for j in range(4):
        pv[:, PV_CW + j * 8:PV_CW + j * 8 + 8] = fm(_f32(conv_w)[0, j], 8)
    pv[:, PV_CB:PV_CB + 8] = fm(conv_b[0], 8)
    pv[:, PV_BRA:PV_BRA + 8] = fm(b_ra[0], 8)
    pv[:, PV_BRX:PV_BRX + 8] = fm(b_rx[0], 8)
    pv[:, PV_LAM:PV_LAM + 8] = fm(lam[0], 8)
    pv[:, PV_BG:PV_BG + 16] = fm(b_gate[0], 16)
    bif = np.ascontiguousarray(_f32(b_if)[0].reshape(2, 4).T)
    gfin = _f32(g_final).reshape(1, D)
    wmod = np.ascontiguousarray(_f32(w_mod)[0].reshape(8, 128, 3 * D).transpose(1, 0, 2))

    def chunk(W, col0):
        return W[:, col0:col0 + 128].reshape(8, 128, 128).transpose(1, 0, 2).reshape(128, 1024)
    base = {"q": 0, "k": 1024, "v": 2048, "o": 3072, "z": 4096}
    chunks = []
    for g in range(2):
        for name in ("q", "k", "v", "o", "z"):
            for hh in range(2):
                for dc in range(2):
                    chunks.append(chunk(w_in0, base[name] + (2 * g + hh) * 256 + dc * 128))
    wg0, wa0, wb0 = _f32(w_gate)[0], _f32(w_out_a)[0], _f32(w_out_b)[0]
    for k in range(4):
        for fc in (2 * k, 2 * k + 1):
            chunks.append(chunk(w_in0, 5128 + fc * 128))
            chunks.append(chunk(w_in0, 6152 + fc * 128))
        for m in (2 * k, 2 * k + 1):
            chunks.append(chunk(wa0, m * 128))
            chunks.append(chunk(wg0, m * 128))
    for k in range(4):
        for m in (2 * k, 2 * k + 1):
            chunks.append(chunk(wb0, m * 128))
            chunks.append(chunk(wg0, 1024 + m * 128))
    wst = np.ascontiguousarray(np.stack(chunks, 0))
    assert wst.shape == (NCH_W, 128, 1024)
    wif = np.ascontiguousarray(w_in0[:, 5120:5128].reshape(8, 128, 8).transpose(1, 0, 2).reshape(128, 64))
    wr = np.ascontiguousarray(np.stack([_f32(w_ra)[0].transpose(1, 0, 2), _f32(w_rx)[0].transpose(1, 0, 2)], 1)
                              .reshape(128, 2048))
    wo = np.ascontiguousarray(_f32(w_o)[0].reshape(8, 128, D).transpose(1, 0, 2).reshape(128, 8 * D))
    sC, sn, sm_, sh_, sc_ = _f32(state_C)[0], _f32(state_n)[0], _f32(state_m)[0], _f32(state_h)[0], _f32(state_conv)[0]

    in_maps = []
    for c in range(n):
        xs = np.concatenate([x_prompt[2 * c].reshape(SEQ, D), x_prompt[2 * c + 1].reshape(SEQ, D),
                             x_sample[c].reshape(DEC, D)], 0)
        cs_ = np.stack([c_prompt[2 * c], c_prompt[2 * c + 1], c_sample[c]], 0)
        cT = np.ascontiguousarray(cs_.reshape(3, 8, 128).transpose(2, 1, 0))
        CT = sC[c].reshape(4, 256, 2, 128).transpose(3, 0, 2, 1)
        nT = sn[c].reshape(4, 2, 128).transpose(2, 0, 1)[..., None]
        sCT = np.ascontiguousarray(np.concatenate([CT, nT], -1).reshape(128, 4 * 2 * 257))
        in_maps.append({
            "x": np.ascontiguousarray(xs), "cT": cT, "wmod": wmod, "pv": pv, "bif": bif, "gfin": gfin,
            "wst": wst, "wif": wif, "wr": wr, "wo": wo, "sCT": sCT,
            "sm": np.ascontiguousarray(sm_[c].reshape(4, 1)),
            "sh": np.ascontiguousarray(sh_[c].reshape(8, 128).T),
            "sconv": np.ascontiguousarray(sc_[c].reshape(3, 8, 128).transpose(2, 1, 0).reshape(128, 24)),
        })
    if "nc" not in _CACHE:
        _CACHE["nc"] = build_program()
    res = run_bass_kernel_spmd(_CACHE["nc"], in_maps, core_ids=list(range(n)))
    R = res.results
    y_p = np.zeros((16, SEQ, D), np.float32)
    y_s = np.zeros((8, DEC, D), np.float32)
    Cs = np.zeros((24, 4, 256, 256), np.float32)
    ns = np.zeros((24, 4, 256), np.float32)
    ms = np.zeros((24, 4), np.float32)
    hs = np.zeros((24, D), np.float32)
    cvs = np.zeros((24, 3, D), np.float32)
    for c in range(n):
        r = R[c]
        y_p[2 * c] = r["y"][0:SEQ]
        y_p[2 * c + 1] = r["y"][SEQ:2 * SEQ]
        y_s[c] = r["y"][2 * SEQ:]
        for s, gi in enumerate((2 * c, 2 * c + 1, 16 + c)):
            o = r["oCT"][s].reshape(128, 4, 2, 257)
            Cs[gi] = o[..., 0:256].transpose(1, 3, 2, 0).reshape(4, 256, 256)
            ns[gi] = o[..., 256].transpose(1, 2, 0).reshape(4, 256)
            ms[gi] = r["om"][s].reshape(4)
            hs[gi] = r["oh"][s].T.reshape(D)
            cvs[gi] = r["oconv"][s].reshape(128, 8, 3).transpose(2, 1, 0).reshape(3, D)
    return (y_p, y_s, Cs[None, :16], ns[None, :16], ms[None, :16], hs[None, :16], cvs[None, :16],
            Cs[None, 16:], ns[None, 16:], ms[None, 16:], hs[None, 16:], cvs[None, 16:])
```

```python
import numpy as np
from contextlib import ExitStack
import concourse.bass as bass
import concourse.mybir as mybir
from concourse.bass_utils import run_bass_kernel_spmd

F32 = mybir.dt.float32
BF16 = mybir.dt.bfloat16
U8 = mybir.dt.uint8
AF = mybir.ActivationFunctionType
ALU = mybir.AluOpType
AX = mybir.AxisListType

D = 1024
H = 4
DH = 256
SEQ = 2048
DEC = 16
NTOK = 2 * SEQ + DEC
TT = 1024
EPS = 1e-6
NCH_W = 88
NSLOT_LOADS = NCH_W // 4
NS = 3
PV_BMOD, PV_GN, PV_GH, PV_CW, PV_CB, PV_BRA, PV_BRX, PV_LAM, PV_BG = 0, 24, 32, 40, 72, 80, 88, 96, 104
NPV = 120
DBG_SUB = 0
OPT_B = 0
OPT_SQ = 0
STQ = 'sp'


class Buf:
    __slots__ = ("w", "r", "excl")

    def __init__(self, excl=False):
        self.w = None
        self.r = {}
        self.excl = excl


def bufs(n, excl=False):
    return [Buf(excl) for _ in range(n)]


class TR:
    def __init__(self, nc, es):
        self.nc = nc
        self.es = es
        self.eng = {"pe": nc.tensor, "act": nc.scalar, "dve": nc.vector, "pool": nc.gpsimd, "sp": nc.sync}
        self.csem = {k: es.enter_context(nc.semaphore("c_" + k)) for k in ("pe", "act", "dve", "pool")}
        self.cnt = {k: 0 for k in self.csem}
        self.known = {k: {} for k in self.eng}
        self.dsem = {}
        self.pending = []
        self.deferring = False
        self.deferred = []
        self.cur_bundle = []

    def _wait(self, e, tok):
        if tok is None:
            return
        key, sem, val = tok
        if e == "pe" and key == "pe":
            return
        if self.known[e].get(key, 0) >= val:
            return
        self.eng[e].wait_ge(sem, val)
        self.known[e][key] = val

    def _deps(self, e, reads, writes):
        for b in reads:
            self._wait(e, b.w)
            if b.excl:
                for key, (sem, val) in list(b.r.items()):
                    self._wait(e, (key, sem, val))
        for b in writes:
            self._wait(e, b.w)
            for key, (sem, val) in list(b.r.items()):
                self._wait(e, (key, sem, val))

    @staticmethod
    def _mark(tok, reads, writes):
        key, sem, val = tok
        for b in reads:
            if b.excl:
                b.w = tok
                b.r = {}
            else:
                b.r[key] = (sem, val)
        for b in writes:
            b.w = tok
            b.r = {}

    def op(self, e, fn, reads=(), writes=(), signal=True):
        if self.deferring:
            self.cur_bundle.append((e, fn, list(reads), list(writes), signal))
            if signal:
                self.deferred.append(self.cur_bundle)
                self.cur_bundle = []
            return None
        return self._emit(e, fn, reads, writes, signal)

    def flush(self, n=None):
        was = self.deferring
        self.deferring = False
        k = 0
        while self.deferred and (n is None or k < n):
            for args in self.deferred.pop(0):
                self._emit(*args)
            k += 1
        self.deferring = was

    def _emit(self, e, fn, reads=(), writes=(), signal=True):
        self._deps(e, reads, writes)
        inst = fn(self.eng[e])
        if not signal:
            assert e == "pe"
            self.pending.append((reads, writes))
            return None
        self.cnt[e] += 1
        inst.then_inc(self.csem[e], 1)
        tok = (e, self.csem[e], self.cnt[e])
        if e == "pe" and self.pending:
            for r, w in self.pending:
                self._mark(tok, r, w)
            self.pending = []
        self._mark(tok, reads, writes)
        return tok

    def dma(self, q, out, in_, reads, writes, slot, **kw):
        self._deps(q, reads, writes)
        if slot not in self.dsem:
            self.dsem[slot] = [self.es.enter_context(self.nc.semaphore("d_" + slot)), 0]
        ent = self.dsem[slot]
        ent[1] += 16
        self.eng[q].dma_start(out=out, in_=in_, **kw).then_inc(ent[0], 16)
        tok = ("d_" + slot, ent[0], ent[1])
        self._mark(tok, reads, writes)
        return tok

    def barrier(self, engines=None):
        assert not self.pending
        for e in (engines or self.eng):
            for k in self.csem:
                if self.cnt[k] > 0:
                    self._wait(e, (k, self.csem[k], self.cnt[k]))
            for slot, (sem, c) in self.dsem.items():
                if c > 0:
                    self._wait(e, ("d_" + slot, sem, c))


class Arena:
    def __init__(self, ap, size):
        self.ap = ap
        self.size = size
        self.off = 0

    def alloc(self, nelem, dt):
        esz = 4 if dt == F32 else 2
        off = (self.off + 63) // 64 * 64
        self.off = off + nelem * esz
        assert self.off <= self.size, ("arena overflow", self.off, self.size)
        return self.ap[:, off:off + nelem * esz].bitcast(dt)


class StopBuild(Exception):
    pass


def build_program(maxphase=10 ** 9):
    nc = bass.Bass("TRN2", target_bir_lowering=False)
    phase_ctr = [0]

    def gate():
        phase_ctr[0] += 1
        if phase_ctr[0] > maxphase:
            raise StopBuild()

    def din(name, shape):
        return nc.dram_tensor(name, shape, F32, kind="ExternalInput").ap()

    def dout(name, shape):
        return nc.dram_tensor(name, shape, F32, kind="ExternalOutput").ap()

    x_d = din("x", [NTOK, D])
    cT_d = din("cT", [128, 8, 3])
    wmod_d = din("wmod", [128, 8, 3 * D])
    pv_d = din("pv", [128, NPV])
    bif_d = din("bif", [4, 2])
    gfin_d = din("gfin", [1, D])
    wst_d = din("wst", [NCH_W, 128, D])
    wif_d = din("wif", [128, 64])
    wr_d = din("wr", [128, 2048])
    wo_d = din("wo", [128, 8 * D])
    sCT_d = din("sCT", [128, 4 * 2 * 257])
    sm_d = din("sm", [4, 1])
    sh_d = din("sh", [128, 8])
    sconv_d = din("sconv", [128, 24])
    wscr_d = nc.dram_tensor("wscr", [NSLOT_LOADS, 128, 4 * D], BF16, kind="Internal").ap()
    y_d = dout("y", [NTOK, D])
    oCT_d = dout("oCT", [3, 128, 4 * 2 * 257])
    om_d = dout("om", [3, 4, 1])
    oh_d = dout("oh", [3, 128, 8])
    oconv_d = dout("oconv", [3, 128, 24])
    DBG_DUMP = 0
    if DBG_DUMP:
        dbg_d = dout("dbg", [3, 128, 8 * TT])

    with ExitStack() as es:
        ARENA = 207 * 1024
        arena_t = es.enter_context(nc.sbuf_tensor("arena", [128, ARENA], U8))
        PB = [es.enter_context(nc.psum_tensor("pb%d" % i, [128, 512], F32)) for i in range(8)]
        T = TR(nc, es)
        A = Arena(arena_t, ARENA)
        out_toks = {}
        fold_wo = [False]

        wo_bf = A.alloc(8 * D, BF16).rearrange("p (k n) -> p k n", k=8)
        wr_bf = A.alloc(2048, F32).rearrange("p (g n j) -> p g n j", g=2, n=8)
        wif_bf = A.alloc(64, BF16).rearrange("p (k m) -> p k m", k=8)
        ident_bf = A.alloc(128, BF16)
        ident_f = A.alloc(128, F32)
        ones_f = A.alloc(128, F32)
        maskT = A.alloc(128, BF16)
        pv = A.alloc(NPV, F32)
        bif = A.alloc(2, F32)
        nbf = A.alloc(1, F32)
        cT = A.alloc(24, F32).rearrange("p (k s) -> p k s", k=8)
        modT = A.alloc(72, F32).rearrange("p (m s) -> p m s", m=24)
        gs = A.alloc(24, F32).rearrange("p (m s) -> p m s", m=8)
        cfeat = A.alloc(8, F32)
        lamt = A.alloc(8, F32)
        gfin_bc = A.alloc(D, F32)
        gate_bc = A.alloc(D, F32)
        gdiag = None
        Cst = A.alloc(4 * 2 * 257, F32).rearrange("p (h c v) -> p h c v", h=4, c=2)
        Cbf = A.alloc(4 * 2 * 258, BF16).rearrange("p (h c v) -> p h c v", h=4, c=2)
        mstate = A.alloc(1, F32)
        hcar = A.alloc(8, F32)
        convcar = A.alloc(24, F32).rearrange("p (f j) -> p f j", f=8)
        rmask = A.alloc(TT, F32)
        uT = A.alloc(8 * TT, BF16).rearrange("p (f t) -> p f t", f=8)
        yaT = A.alloc(8 * TT, BF16).rearrange("p (f t) -> p f t", f=8)
        ybT = A.alloc(8 * TT, BF16).rearrange("p (f t) -> p f t", f=8)
        mgT = A.alloc(8 * TT, BF16).rearrange("p (f t) -> p f t", f=8)
        ring = [A.alloc(4 * D, BF16).rearrange("p (c k m) -> p c k m", c=4, k=8) for _ in range(NS)]
        PERSIST_END = A.off

        b_const = Buf()
        b_wc = Buf()
        b_wo = Buf()
        b_mod = Buf()
        b_gbc = Buf()
        b_ring = bufs(NS)
        b_uT = [bufs(8) for _ in range(8)]
        b_yaT = [bufs(8) for _ in range(8)]
        b_ybT = [bufs(2) for _ in range(8)]
        b_mgT = [bufs(2) for _ in range(8)]
        b_C = bufs(4)
        b_Cbf = bufs(4)
        b_ms = Buf()
        b_hcar = bufs(8)
        b_ccar = bufs(8)
        b_PB = bufs(8, excl=True)

        T.op("pool", lambda e: e.memset(ident_bf, 0.0), writes=[b_const])
        T.op("pool", lambda e: e.affine_select(out=ident_bf, in_=ident_bf, pattern=[[1, 128]],
                                                compare_op=ALU.not_equal, fill=1.0, base=0,
                                                channel_multiplier=-1), reads=[b_const], writes=[b_const])
        T.op("pool", lambda e: e.memset(ident_f, 0.0), writes=[b_const])
        T.op("pool", lambda e: e.affine_select(out=ident_f, in_=ident_f, pattern=[[1, 128]],
                                                compare_op=ALU.not_equal, fill=1.0, base=0,
                                                channel_multiplier=-1), reads=[b_const], writes=[b_const])
        T.op("pool", lambda e: e.memset(ones_f, 1.0), writes=[b_const])
        T.op("pool", lambda e: e.memset(maskT, 1.0), writes=[b_const])
        T.op("pool", lambda e: e.affine_select(out=maskT, in_=maskT, pattern=[[1, 128]],
                                                compare_op=ALU.is_ge, fill=0.0, base=0,
                                                channel_multiplier=-1), reads=[b_const], writes=[b_const])
        T.op("pool", lambda e: e.memset(rmask[0:4, :], 1.0), writes=[b_const])
        T.op("pool", lambda e: e.memset(rmask[0:4, :].rearrange("p (c l) -> p c l", l=128)[:, :, 0:1], 0.0),
             reads=[b_const], writes=[b_const])
        T.dma("sp", pv, pv_d, [], [b_const], "s_pv")
        T.dma("sp", bif[0:4, :], bif_d, [], [b_const], "s_bif")
        T.dma("sp", cT.rearrange("p k s -> p (k s)"), cT_d.rearrange("p k s -> p (k s)"), [], [b_const], "s_cT")
        T.dma("sp", gfin_bc, gfin_d.to_broadcast([128, D]), [], [b_const], "s_gfin")
        T.op("act", lambda e: e.activation(out=lamt, in_=pv[:, PV_LAM:PV_LAM + 8], func=AF.Exp, scale=-1.0),
             reads=[b_const], writes=[b_const])
        T.op("act", lambda e: e.activation(out=lamt, in_=lamt, func=AF.Ln, bias=1.0),
             reads=[b_const], writes=[b_const])
        T.op("dve", lambda e: e.tensor_scalar(out=cfeat, in0=lamt, scalar1=-8.0, scalar2=None, op0=ALU.mult),
             reads=[b_const], writes=[b_const])
        T.op("dve", lambda e: e.tensor_scalar(out=nbf[0:4, :], in0=bif[0:4, 1:2], scalar1=-1.0, scalar2=None,
                                              op0=ALU.mult), reads=[b_const], writes=[b_const])

        wm = A.alloc(8 * 3072, BF16).rearrange("p (k n) -> p k n", k=8)
        cTb = A.alloc(24, BF16).rearrange("p (k s) -> p k s", k=8)
        b_wm = bufs(4)
        for q4 in range(4):
            T.dma("pool", wm[:, :, q4 * 768:(q4 + 1) * 768], wmod_d[:, :, q4 * 768:(q4 + 1) * 768], [], [b_wm[q4]],
                  "s_wm%d" % q4)
        T.dma("pool", wif_bf.rearrange("p k m -> p (k m)"), wif_d, [], [b_wc], "s_wif")
        T.dma("sp", wr_bf.rearrange("p g n j -> p (g n j)"), wr_d, [], [b_wc], "s_wr")
        T.dma("pool", wo_bf.rearrange("p k n -> p (k n)"), wo_d, [], [b_wo], "s_wo")
        T.op("dve", lambda e: e.tensor_copy(out=cTb, in_=cT), reads=[b_const], writes=[b_const])
        for m in range(24):
            for kc in range(8):
                T.op("pe", lambda e, m=m, kc=kc: e.matmul(
                    PB[0][:, 4 * m:4 * m + 3], lhsT=wm[:, kc, m * 128:(m + 1) * 128], rhs=cTb[:, kc, :],
                    start=(kc == 0), stop=(kc == 7)),
                    reads=[b_wm[m // 6], b_const], writes=[b_PB[0]], signal=(kc == 7))
        T.op("dve", lambda e: e.tensor_tensor(
            out=modT, in0=PB[0][:, 0:96].rearrange("p (m s) -> p m s", s=4)[:, :, 0:3],
            in1=pv[:, PV_BMOD:PV_BMOD + 24].unsqueeze(2).to_broadcast([128, 24, 3]), op=ALU.add),
            reads=[b_PB[0], b_const], writes=[b_mod])
        T.op("dve", lambda e: e.tensor_scalar(out=gs, in0=modT[:, 8:16, :], scalar1=1.0, scalar2=None, op0=ALU.add),
             reads=[b_mod], writes=[b_mod])
        T.op("dve", lambda e: e.tensor_tensor(
            out=gs, in0=gs, in1=pv[:, PV_GN:PV_GN + 8].unsqueeze(2).to_broadcast([128, 8, 3]), op=ALU.mult),
            reads=[b_mod, b_const], writes=[b_mod])
        T.barrier()
        A.off = PERSIST_END

        ring_state = {"issued": 0}
        b_scr = bufs(NSLOT_LOADS)
        NS_S = 6
        SAMPLE_BASE = NSLOT_LOADS * 4
        RS_OFF = ARENA - NS_S * 8192
        ring_s = [arena_t[:, RS_OFF + i * 8192: RS_OFF + (i + 1) * 8192].bitcast(BF16)
                  .rearrange("p (c k m) -> p c k m", c=4, k=8) for i in range(NS_S)]
        b_ring_s = bufs(NS_S)

        def ring_slot(j):
            if j >= SAMPLE_BASE:
                i = (j - SAMPLE_BASE) % NS_S
                return ring_s[i], b_ring_s[i], "ringX%d" % i, NS_S
            return ring[j % NS], b_ring[j % NS], "ring%d" % (j % NS), NS

        def ring_issue_upto(k):
            while ring_state["issued"] <= k:
                j = ring_state["issued"]
                ci = (j % NSLOT_LOADS) * 4
                jj = j % NSLOT_LOADS
                rv, rb, rname, _ = ring_slot(j)
                if j < NSLOT_LOADS:
                    sl = j % NS
                    T.dma("pool", rv.rearrange("p c k m -> p c (k m)"),
                          wst_d[ci:ci + 4].rearrange("c p n -> p c n"), [], [rb], "ringS%d" % sl)
                    T.dma("sp", wscr_d[jj], rv.rearrange("p c k m -> p (c k m)"), [rb], [b_scr[jj]], "scr%d" % sl)
                else:
                    T.dma("sp", rv.rearrange("p c k m -> p (c k m)"), wscr_d[jj], [b_scr[jj]], [rb], rname)
                ring_state["issued"] += 1

        total_loads = [0]

        def ring_next(dual=False):
            k = total_loads[0]
            total_loads[0] += 1
            rv, rb, _, ns = ring_slot(k)
            look = ns - 2 if dual else ns - 1
            hi = k + look
            if k < SAMPLE_BASE:
                hi = min(hi, SAMPLE_BASE - 1)
            ring_issue_upto(min(hi, NLOADS_TOTAL - 1))
            return rv, rb

        sts = []
        for s in range(2):
            for part in range(SEQ // TT):
                sts.append(dict(seq=s, tok0=s * SEQ + part * TT, T=TT, tp=128, first=(part == 0),
                                last=(part == SEQ // TT - 1), prompt=True))
        sts.append(dict(seq=2, tok0=2 * SEQ, T=DEC, tp=DEC, first=True, last=True, prompt=False))
        NLOADS_TOTAL = NSLOT_LOADS * len(sts)

        psum_rr = [0]
        bank_pool = [list(range(8))]

        def next_bank():
            pool = bank_pool[0]
            i = pool[psum_rr[0] % len(pool)]
            psum_rr[0] += 1
            return i

        def proj(wv, bw, ci, src, bsrc, Tn, nn, evac):
            for nt in range(Tn // nn):
                bk = next_bank()
                for kc in range(8):
                    T.op("pe", lambda e, bk=bk, kc=kc, nt=nt: e.matmul(
                        PB[bk][:, 0:nn], lhsT=wv[:, ci, kc, :], rhs=src[:, kc, nt * nn:(nt + 1) * nn],
                        start=(kc == 0), stop=(kc == 7)),
                        reads=[bw] + bsrc(kc, nt), writes=[b_PB[bk]], signal=(kc == 7))
                evac(bk, nt)

        def p0_alloc():
            return dict(xin=[A.alloc(D, F32) for _ in range(2)], xn=[A.alloc(D, BF16) for _ in range(2)],
                        junk=A.alloc(D, BF16), stat=[A.alloc(4, F32) for _ in range(2)],
                        b_xin=bufs(2), b_xn=bufs(2), b_stat=bufs(2), b_junk=Buf())

        def p0_tile(st, j, Z, load_only=False, stages=("stats", "xn", "tr", "ev")):
            s, tp, tok0 = st["seq"], st["tp"], st["tok0"]
            p = j % 2
            xin, xn, junk, stat = Z["xin"][p], Z["xn"][p], Z["junk"], Z["stat"][p]
            b_xin, b_xn, b_stat, b_junk = Z["b_xin"][p], Z["b_xn"][p], Z["b_stat"][p], Z["b_junk"]
            if load_only:
                T.dma("sp", xin[0:tp, :], x_d[tok0 + j * tp: tok0 + (j + 1) * tp, :], [], [b_xin], "xin%d" % p)
                return
            if "stats" in stages:
                T.op("act", lambda e: e.activation(out=junk[0:tp, :], in_=xin[0:tp, :], func=AF.Square,
                                                   accum_out=stat[0:tp, 0:1]), reads=[b_xin], writes=[b_junk, b_stat])
                T.op("act", lambda e: e.activation(out=stat[0:tp, 1:2], in_=stat[0:tp, 0:1], func=AF.Ln,
                                                   scale=1.0 / D, bias=EPS), reads=[b_stat], writes=[b_stat])
                T.op("act", lambda e: e.activation(out=stat[0:tp, 2:3], in_=stat[0:tp, 1:2], func=AF.Exp,
                                                   scale=-0.5), reads=[b_stat], writes=[b_stat])
            if "xn" in stages:
                T.op("pool", lambda e: e.tensor_scalar(out=xn[0:tp, :], in0=xin[0:tp, :], scalar1=stat[0:tp, 2:3],
                                                       scalar2=0.0, op0=ALU.mult, op1=ALU.add),
                     reads=[b_xin, b_stat], writes=[b_xn])
            def bank_of(fc):
                if fc < 6:
                    return 2 * p, PB[2 * p][:, 0:384].bitcast(BF16).rearrange("p (f t) -> p f t", f=6)[:, fc, 0:tp]
                return 2 * p + 1, PB[2 * p + 1][:, 0:128].bitcast(BF16).rearrange("p (f t) -> p f t", f=2)[:, fc - 6, 0:tp]
            if "tr" in stages:
                for fc in range(8):
                    bk, pv_ = bank_of(fc)
                    T.op("pe", lambda e, fc=fc, pv_=pv_: e.transpose(
                        out=pv_, in_=xn[0:tp, fc * 128:(fc + 1) * 128], identity=ident_bf[0:tp, 0:tp]),
                        reads=[b_xn, b_const], writes=[b_PB[bk]], signal=(fc in (5, 7)))
            if "ev" in stages:
                for fc in (0, 6, 1, 2, 7, 3, 4, 5):
                    bk, pv_ = bank_of(fc)
                    dst = uT[:, fc, j * tp:(j + 1) * tp]
                    if fc < 6:
                        T.op("dve", lambda e, fc=fc, dst=dst, pv_=pv_: e.tensor_scalar(
                            out=dst, in0=pv_, scalar1=gs[:, fc, s:s + 1], scalar2=modT[:, fc, s:s + 1],
                            op0=ALU.mult, op1=ALU.add), reads=[b_PB[bk], b_mod], writes=[b_uT[fc][j]])
                    else:
                        T.op("act", lambda e, fc=fc, dst=dst, pv_=pv_: e.activation(
                            out=dst, in_=pv_, func=AF.Identity, scale=gs[:, fc, s:s + 1],
                            bias=modT[:, fc, s:s + 1]), reads=[b_PB[bk], b_mod], writes=[b_uT[fc][j]])

        def pd_alloc(nb=2):
            one = lambda mk: [mk() for _ in range(nb)] * (2 // nb)
            return dict(xr=one(lambda: A.alloc(D, F32)), yt=one(lambda: A.alloc(D, F32)),
                        ot=one(lambda: A.alloc(D, F32)), junkd=A.alloc(D, BF16),
                        std=one(lambda: A.alloc(4, F32)),
                        b_xr=one(Buf), b_yt=one(Buf), b_ot=one(Buf), b_std=one(Buf), b_junkd=Buf())

        def pd_tile(st, j, Z, load_only=False, stages=("pe", "mul", "add", "stats", "fin")):
            tp, tok0 = st["tp"], st["tok0"]
            nn = min(512, st["T"])
            p = j % 2
            xr, yt, ot, junkd, std = Z["xr"][p], Z["yt"][p], Z["ot"][p], Z["junkd"], Z["std"][p]
            b_xr, b_yt, b_ot, b_std, b_junkd = Z["b_xr"][p], Z["b_yt"][p], Z["b_ot"][p], Z["b_std"][p], Z["b_junkd"]
            ntj = (j * tp) // nn
            rows = slice(tok0 + j * tp, tok0 + (j + 1) * tp)
            if load_only:
                T.dma("sp", xr[0:tp, :], x_d[rows, :], [], [b_xr], "xr%d" % p)
                return
            for half in range(2):
                bk = 4 + 2 * p + half
                if "pe" in stages:
                    for kc in range(8):
                        T.op("pe", lambda e, bk=bk, kc=kc, half=half: e.matmul(
                            PB[bk][0:tp, :], lhsT=mgT[:, kc, j * tp:(j + 1) * tp], rhs=wo_bf[:, kc, half * 512:(half + 1) * 512],
                            start=(kc == 0), stop=(kc == 7)),
                            reads=[b_mgT[kc][ntj], b_wo], writes=[b_PB[bk]], signal=(kc == 7))
                if "mul" in stages:
                    T.op("dve", lambda e, bk=bk, half=half: e.tensor_tensor(
                        out=yt[0:tp, half * 512:(half + 1) * 512], in0=PB[bk][0:tp, :],
                        in1=xr[0:tp, half * 512:(half + 1) * 512], op=ALU.add),
                        reads=[b_PB[bk], b_xr], writes=[b_yt])
            if "stats" in stages:
                T.op("act", lambda e: e.activation(out=junkd[0:tp, :], in_=yt[0:tp, :], func=AF.Square,
                                                   accum_out=std[0:tp, 0:1]), reads=[b_yt], writes=[b_junkd, b_std])
                T.op("act", lambda e: e.activation(out=std[0:tp, 1:2], in_=std[0:tp, 0:1], func=AF.Ln,
                                                   scale=1.0 / D, bias=EPS), reads=[b_std], writes=[b_std])
                T.op("act", lambda e: e.activation(out=std[0:tp, 2:3], in_=std[0:tp, 1:2], func=AF.Exp,
                                                   scale=-0.5), reads=[b_std], writes=[b_std])
            if "fin" in stages:
                T.op("dve", lambda e: e.scalar_tensor_tensor(out=ot[0:tp, :], in0=yt[0:tp, :], scalar=std[0:tp, 2:3],
                                                             in1=gfin_bc[0:tp, :], op0=ALU.mult, op1=ALU.mult),
                     reads=[b_yt, b_std, b_const], writes=[b_ot])
                out_toks["y%d" % p] = T.dma(STQ, y_d[rows, :], ot[0:tp, :], [b_ot], [], "yo%d" % p)

        def run_st(idx):
            st = sts[idx]
            s = st["seq"]
            if not st["prompt"]:
                A.size = RS_OFF
            Tn = st["T"]
            tp = st["tp"]
            L = tp
            nch = Tn // L
            nn = min(512, Tn)
            nnt = Tn // nn
            tpn = nn // tp
            tok0 = st["tok0"]

            def uT_bufs(kc, nt):
                return [b_uT[kc][nt * tpn + i] for i in range(tpn)]

            if st["first"]:
                if st["prompt"]:
                    for h in range(4):
                        T.op("pool", lambda e, h=h: e.memset(Cst[:, h].rearrange("p c v -> p (c v)"), 0.0),
                             writes=[b_C[h]])
                    T.op("pool", lambda e: e.memset(mstate[0:4, :], 0.0), writes=[b_ms])
                    T.op("pool", lambda e: e.memset(hcar, 0.0), writes=b_hcar)
                    T.op("pool", lambda e: e.memset(convcar.rearrange("p f j -> p (f j)"), 0.0), writes=b_ccar)
                else:
                    T.dma("sp", Cst.rearrange("p h c v -> p (h c v)"), sCT_d, [], b_C, "s_sCT")
                    T.dma("sp", mstate[0:4, :], sm_d, [], [b_ms], "s_sm")
                    T.dma("sp", hcar, sh_d, [], b_hcar, "s_sh")
                    T.dma("sp", convcar.rearrange("p f j -> p (f j)"), sconv_d, [], b_ccar, "s_sconv")
                gd = A.alloc(D, F32).rearrange("p (f j) -> p f j", f=8)
                b_gd = Buf()
                T.op("dve", lambda e: e.tensor_tensor(
                    out=gd, in0=modT[:, 16:24, s:s + 1].to_broadcast([128, 8, 128]),
                    in1=ident_f.unsqueeze(1).to_broadcast([128, 8, 128]), op=ALU.mult),
                    reads=[b_mod, b_const], writes=[b_gd])
                for half in range(2):
                    T.op("pe", lambda e, half=half: e.matmul(
                        PB[half][:, :], lhsT=ones_f, rhs=gd.rearrange("p f j -> p (f j)")[:, half * 512:(half + 1) * 512],
                        start=True, stop=True), reads=[b_gd, b_const], writes=[b_PB[half]])
                    T.op("act", lambda e, half=half: e.activation(
                        out=gate_bc[:, half * 512:(half + 1) * 512], in_=PB[half][:, :], func=AF.Copy),
                        reads=[b_PB[half]], writes=[b_gbc])
                T.barrier()
                A.off = PERSIST_END
                if s > 0:
                    T.dma("pool", wo_bf.rearrange("p k n -> p (k n)"), wo_d, [], [b_wo], "s_wo")
                fold_wo[0] = True

            if idx == 0:
                gate()
                Z0 = p0_alloc()
                p0_tile(st, 0, Z0, load_only=True)
                for j in range(Tn // tp):
                    if j + 1 < Tn // tp:
                        p0_tile(st, j + 1, Z0, load_only=True)
                    p0_tile(st, j, Z0)
                T.barrier()
                A.off = PERSIST_END

            gate()
            igb = A.alloc(Tn, F32)[0:4, :]
            cs = A.alloc(Tn, F32)[0:4, :]
            gg = A.alloc(Tn, F32)[0:4, :]
            esc = A.alloc(Tn, F32)[0:4, :]
            thr = A.alloc(Tn, F32)[0:4, :]
            sm_ = A.alloc(64, F32)[0:4, :]
            gmax, Ac, Bc, mall, Mc, mprev, dec = (sm_[:, i * 8:i * 8 + nch] for i in range(7))
            decd = A.alloc(32, F32)[0:4, 0:4 * nch].rearrange("p (a c) -> p a c", a=4)
            escT = A.alloc(64, F32)[:, 0:nch * 8].rearrange("p (c j) -> p c j", j=8)
            decbc = A.alloc(32, F32)[:, 0:4 * nch].rearrange("p (a c) -> p a c", a=4)
            b_g = Buf()
            b_escT = Buf()
            b_decbc = Buf()
            T.deferring = True
            for nt in range(nnt):
                sl = slice(nt * nn, (nt + 1) * nn)
                bi, bf_ = 6, 7
                for (bk, c0) in ((bi, 0), (bf_, 4)):
                    for kc in range(8):
                        T.op("pe", lambda e, bk=bk, c0=c0, kc=kc, sl=sl: e.matmul(
                            PB[bk][0:4, 0:nn], lhsT=wif_bf[:, kc, c0:c0 + 4], rhs=uT[:, kc, sl],
                            start=(kc == 0), stop=(kc == 7)),
                            reads=[b_wc] + uT_bufs(kc, nt), writes=[b_PB[bk]], signal=(kc == 7))
                T.op("dve", lambda e, bi=bi, sl=sl: e.tensor_scalar(
                    out=igb[:, sl], in0=PB[bi][0:4, 0:nn], scalar1=bif[0:4, 0:1], scalar2=None, op0=ALU.add),
                    reads=[b_PB[bi], b_const], writes=[b_g])
                T.op("act", lambda e, bf_=bf_, sl=sl: e.activation(
                    out=gg[:, sl], in_=PB[bf_][0:4, 0:nn], func=AF.Exp, scale=-1.0, bias=nbf[0:4, :]),
                    reads=[b_PB[bf_], b_const], writes=[b_g])
                T.op("act", lambda e, sl=sl: e.activation(out=gg[:, sl], in_=gg[:, sl], func=AF.Ln, bias=1.0),
                     reads=[b_g], writes=[b_g])
            T.op("dve", lambda e: e.tensor_tensor_scan(out=cs, data0=rmask[0:4, 0:Tn], data1=gg, initial=0.0,
                                                       op0=ALU.mult, op1=ALU.add), reads=[b_g, b_const], writes=[b_g])
            T.op("dve", lambda e: e.tensor_tensor(out=gg, in0=igb, in1=cs, op=ALU.add), reads=[b_g], writes=[b_g])
            T.op("dve", lambda e: e.tensor_reduce(out=gmax, in_=gg.rearrange("p (c l) -> p c l", l=L), axis=AX.X,
                                                  op=ALU.max), reads=[b_g], writes=[b_g])
            T.op("dve", lambda e: e.tensor_scalar(out=Ac, in0=cs.rearrange("p (c l) -> p c l", l=L)[:, :, L - 1],
                                                  scalar1=-1.0, scalar2=None, op0=ALU.mult), reads=[b_g], writes=[b_g])
            T.op("dve", lambda e: e.tensor_tensor(out=Bc, in0=gmax, in1=Ac, op=ALU.add), reads=[b_g], writes=[b_g])
            T.op("dve", lambda e: e.tensor_tensor_scan(out=mall, data0=Ac, data1=Bc, initial=mstate[0:4, :],
                                                       op0=ALU.add, op1=ALU.max), reads=[b_g, b_ms], writes=[b_g])
            T.op("dve", lambda e: e.tensor_tensor(out=Mc, in0=mall, in1=Ac, op=ALU.subtract), reads=[b_g], writes=[b_g])
            T.op("dve", lambda e: e.tensor_copy(out=mprev[:, 0:1], in_=mstate[0:4, :]), reads=[b_ms], writes=[b_g])
            if nch > 1:
                T.op("dve", lambda e: e.tensor_copy(out=mprev[:, 1:nch], in_=mall[:, 0:nch - 1]),
                     reads=[b_g], writes=[b_g])
            T.op("dve", lambda e: e.tensor_copy(out=mstate[0:4, :], in_=mall[:, nch - 1:nch]),
                 reads=[b_g], writes=[b_ms])
            T.op("dve", lambda e: e.tensor_tensor(out=dec, in0=mprev, in1=Mc, op=ALU.subtract), reads=[b_g], writes=[b_g])
            T.op("act", lambda e: e.activation(out=dec, in_=dec, func=AF.Exp), reads=[b_g], writes=[b_g])
            Mcb = Mc.unsqueeze(2).to_broadcast([4, nch, L])
            T.op("dve", lambda e: e.tensor_tensor(out=esc.rearrange("p (c l) -> p c l", l=L),
                                                  in0=gg.rearrange("p (c l) -> p c l", l=L), in1=Mcb, op=ALU.subtract),
                 reads=[b_g], writes=[b_g])
            T.op("act", lambda e: e.activation(out=esc, in_=esc, func=AF.Exp), reads=[b_g], writes=[b_g])
            T.op("dve", lambda e: e.tensor_tensor(out=thr.rearrange("p (c l) -> p c l", l=L),
                                                  in0=cs.rearrange("p (c l) -> p c l", l=L), in1=Mcb, op=ALU.subtract),
                 reads=[b_g], writes=[b_g])
            T.op("act", lambda e: e.activation(out=thr, in_=thr, func=AF.Exp), reads=[b_g], writes=[b_g])
            bkE = 6
            psE = PB[bkE][:, 0:nch * 8].rearrange("p (c j) -> p c j", j=8)
            for c in range(nch):
                for (srcv, c0) in ((esc, 0), (thr, 4)):
                    T.op("pe", lambda e, c=c, srcv=srcv, c0=c0: e.transpose(
                        out=psE[0:L, c, c0:c0 + 4], in_=srcv[:, c * L:(c + 1) * L], identity=ident_f[0:4, 0:4]),
                        reads=[b_g, b_const], writes=[b_PB[bkE]], signal=(c == nch - 1 and c0 == 4))
            T.op("dve", lambda e: e.tensor_copy(out=escT[0:L], in_=psE[0:L]), reads=[b_PB[bkE]], writes=[b_escT])
            T.op("dve", lambda e: e.tensor_tensor(
                out=decd, in0=dec.unsqueeze(1).to_broadcast([4, 4, nch]),
                in1=ident_f[0:4, 0:4].unsqueeze(2).to_broadcast([4, 4, nch]), op=ALU.mult),
                reads=[b_g, b_const], writes=[b_g])
            bkD = 7
            T.op("pe", lambda e: e.matmul(PB[bkD][:, 0:4 * nch], lhsT=ones_f[0:4, :],
                                          rhs=decd.rearrange("p a c -> p (a c)"), start=True, stop=True),
                 reads=[b_g, b_const], writes=[b_PB[bkD]])
            T.op("dve", lambda e: e.tensor_copy(out=decbc.rearrange("p a c -> p (a c)"), in_=PB[bkD][:, 0:4 * nch]),
                 reads=[b_PB[bkD]], writes=[b_decbc])
            if fold_wo[0]:
                fold_wo[0] = False
                for kc in range(8):
                    T.op("dve", lambda e, kc=kc: e.tensor_tensor(out=wo_bf[:, kc, :], in0=wo_bf[:, kc, :], in1=gate_bc,
                                                                 op=ALU.mult), reads=[b_wo, b_gbc], writes=[b_wo])
            T.deferring = False
            A1_END = A.off

            for g in range(2):
                gate()
                if g == 0:
                    A.off = A1_END
                    qT = A.alloc(4 * Tn, BF16).rearrange("p (i t) -> p i t", i=4)
                    kT = A.alloc(4 * Tn, BF16).rearrange("p (i t) -> p i t", i=4)
                    vT = A.alloc(4 * Tn, BF16).rearrange("p (i t) -> p i t", i=4)
                    gaT = A.alloc(4 * Tn, BF16).rearrange("p (i t) -> p i t", i=4)
                    szt = [A.alloc(nn, F32) for _ in range(2)]
                    tzt = [A.alloc(nn, F32) for _ in range(2)]
                    b_szt, b_tzt = bufs(2), bufs(2)
                    b_q = [bufs(nnt) for _ in range(4)]
                    b_k = [bufs(nnt) for _ in range(4)]
                    b_v = [bufs(nnt) for _ in range(4)]
                    b_ga = [bufs(nnt) for _ in range(4)]
                ecnt = [0]

                def ev_copy(dstT, bdst, i):
                    def f(bk, nt):
                        ecnt[0] += 1
                        dst = dstT[:, i, nt * nn:(nt + 1) * nn]
                        if ecnt[0] % 2 == 0:
                            T.op("act", lambda e: e.activation(out=dst, in_=PB[bk][:, 0:nn], func=AF.Copy),
                                 reads=[b_PB[bk]], writes=[bdst[i][nt]])
                        else:
                            T.op("dve", lambda e: e.tensor_copy(out=dst, in_=PB[bk][:, 0:nn]),
                                 reads=[b_PB[bk]], writes=[bdst[i][nt]])
                    return f

                def ev_k(i):
                    def f(bk, nt):
                        dst = kT[:, i, nt * nn:(nt + 1) * nn]
                        T.op("dve", lambda e: e.tensor_scalar(out=dst, in0=PB[bk][:, 0:nn], scalar1=DH ** -0.5,
                                                              scalar2=None, op0=ALU.mult),
                             reads=[b_PB[bk]], writes=[b_k[i][nt]])
                    return f

                def ev_o(i):
                    def f(bk, nt):
                        dst = gaT[:, i, nt * nn:(nt + 1) * nn]
                        T.op("act", lambda e: e.activation(out=dst, in_=PB[bk][:, 0:nn], func=AF.Sigmoid),
                             reads=[b_PB[bk]], writes=[b_ga[i][nt]])
                    return f

                def ev_z(i):
                    def f(bk, nt):
                        ecnt[0] += 1
                        p = ecnt[0] % 2
                        dst = gaT[:, i, nt * nn:(nt + 1) * nn]
                        fcg = g * 4 + i
                        T.op("act", lambda e: e.activation(out=szt[p], in_=PB[bk][:, 0:nn], func=AF.Sigmoid),
                             reads=[b_PB[bk]], writes=[b_szt[p]])
                        T.op("dve", lambda e: e.scalar_tensor_tensor(
                            out=tzt[p], in0=PB[bk][:, 0:nn], scalar=pv[:, PV_GH + fcg:PV_GH + fcg + 1], in1=szt[p],
                            op0=ALU.mult, op1=ALU.mult), reads=[b_PB[bk], b_szt[p], b_const], writes=[b_tzt[p]])
                        T.op("dve", lambda e: e.tensor_tensor(out=dst, in0=tzt[p], in1=dst, op=ALU.mult),
                             reads=[b_tzt[p], b_ga[i][nt]], writes=[b_ga[i][nt]])
                    return f

                if g == 0:
                    bank_pool[0] = [0, 1, 2, 3, 4, 5]
                    nfl = -(-len(T.deferred) // 18)
                for (kind, evf) in (("q", lambda i: ev_copy(qT, b_q, i)), ("k", ev_k),
                                    ("v", lambda i: ev_copy(vT, b_v, i)), ("o", ev_o), ("z", ev_z)):
                    wv, bw = ring_next()
                    for i in range(4):
                        proj(wv, bw, i, uT, uT_bufs, Tn, nn, evf(i))
                        if g == 0:
                            T.flush(nfl)
                if g == 0:
                    T.flush()
                    bank_pool[0] = list(range(8))

                gate()
                NP3 = 3
                if g == 0:
                    ktok = [A.alloc(256, BF16) for _ in range(NP3)]
                    vpp = [A.alloc(258, BF16) for _ in range(NP3)]
                    SmT = [A.alloc(128, BF16) for _ in range(NP3)]
                    hn = [A.alloc(512, BF16) for _ in range(2)]
                    st3 = [A.alloc(8, F32) for _ in range(NP3)]
                    junk3 = A.alloc(256, BF16)
                    b_ktok, b_vpp, b_SmT, b_hn, b_st3 = bufs(NP3), bufs(NP3), bufs(NP3), bufs(2), bufs(NP3)
                    b_junk3 = Buf()
                iters = [(c, hh) for c in range(nch) for hh in range(2)]

                def views(it):
                    c, hh = iters[it]
                    p = it % NP3
                    bankT = PB[p][:, 0:256].bitcast(BF16)
                    return dict(c=c, hh=hh, h=2 * g + hh, p=p, tsl=slice(c * L, (c + 1) * L), ntc=(c * L) // nn,
                                i0=hh * 2, pk=bankT[:, 0:256], pvv=bankT[:, 256:512], psS=PB[p][:, 256:384],
                                pdn=PB[p][:, 384:386], psN=PB[3 + p][:, 0:257],
                                psD=PB[6][:, :].rearrange("p (c v) -> p c v", c=2),
                                bT=b_PB[p], bN=b_PB[3 + p], bD=b_PB[6])

                def pe1(it):
                    V = views(it)
                    i0, tsl, ntc, p = V["i0"], V["tsl"], V["ntc"], V["p"]
                    for dc in range(2):
                        T.op("pe", lambda e, dc=dc: e.transpose(out=V["pk"][0:L, dc * 128:(dc + 1) * 128],
                                                                in_=kT[:, i0 + dc, tsl], identity=ident_bf),
                             reads=[b_k[i0 + dc][ntc], b_const], writes=[V["bT"]], signal=False)
                    for dc in range(2):
                        T.op("pe", lambda e, dc=dc: e.transpose(out=V["pvv"][0:L, dc * 128:(dc + 1) * 128],
                                                                in_=vT[:, i0 + dc, tsl], identity=ident_bf),
                             reads=[b_v[i0 + dc][ntc], b_const], writes=[V["bT"]], signal=False)
                    for dc in range(2):
                        T.op("pe", lambda e, dc=dc: e.matmul(V["psS"][0:L, 0:L], lhsT=kT[:, i0 + dc, tsl],
                                                             rhs=qT[:, i0 + dc, tsl], start=(dc == 0), stop=(dc == 1)),
                             reads=[b_k[i0 + dc][ntc], b_q[i0 + dc][ntc]], writes=[V["bT"]], signal=(dc == 1))

                def ev1(it):
                    V = views(it)
                    c, h, p = V["c"], V["h"], V["p"]
                    T.op("dve", lambda e: e.tensor_copy(out=ktok[p][0:L, :], in_=V["pk"][0:L, :]),
                         reads=[V["bT"]], writes=[b_ktok[p]])
                    T.op("dve", lambda e: e.tensor_scalar(out=vpp[p][0:L, 0:256], in0=V["pvv"][0:L, :],
                                                          scalar1=escT[0:L, c, h:h + 1], scalar2=None, op0=ALU.mult),
                         reads=[V["bT"], b_escT], writes=[b_vpp[p]])
                    T.op("dve", lambda e: e.tensor_copy(out=vpp[p][0:L, 256:257], in_=escT[0:L, c, h:h + 1]),
                         reads=[b_escT, b_vpp[p]], writes=[b_vpp[p]])
                    T.op("dve", lambda e: e.tensor_tensor(out=SmT[p][0:L, 0:L], in0=V["psS"][0:L, 0:L],
                                                          in1=maskT[0:L, 0:L], op=ALU.mult),
                         reads=[V["bT"], b_const], writes=[b_SmT[p]])

                def cbf(it):
                    V = views(it)
                    c, h = V["c"], V["h"]
                    T.op("act", lambda e: e.activation(out=Cbf[:, h, :, 0:257], in_=Cst[:, h], func=AF.Identity,
                                                       scale=decbc[:, h, c:c + 1]),
                         reads=[b_C[h], b_decbc], writes=[b_Cbf[h]])

                def pe2(it):
                    V = views(it)
                    c, h, p, i0, tsl, ntc = V["c"], V["h"], V["p"], V["i0"], V["tsl"], V["ntc"]
                    psN, psD, pdn = V["psN"], V["psD"], V["pdn"]
                    for dc in range(2):
                        T.op("pe", lambda e, dc=dc: e.matmul(psN[0:L, :], lhsT=qT[:, i0 + dc, tsl],
                                                             rhs=Cbf[:, h, dc, 0:257], start=(dc == 0), stop=False),
                             reads=[b_q[i0 + dc][ntc], b_Cbf[h]], writes=[V["bN"]], signal=False)
                    T.op("pe", lambda e: e.matmul(psN[0:L, :], lhsT=SmT[p][0:L, 0:L], rhs=vpp[p][0:L, 0:257],
                                                  start=False, stop=True),
                         reads=[b_SmT[p], b_vpp[p]], writes=[V["bN"]])
                    for dc in range(2):
                        T.op("pe", lambda e, dc=dc: e.matmul(psD[:, dc, :], lhsT=ktok[p][0:L, dc * 128:(dc + 1) * 128],
                                                             rhs=vpp[p][0:L, 0:256], start=True, stop=True),
                             reads=[b_ktok[p], b_vpp[p]], writes=[V["bD"]], signal=(dc == 1))
                    for dc in range(2):
                        T.op("pe", lambda e, dc=dc: e.matmul(pdn[:, dc:dc + 1], lhsT=ktok[p][0:L, dc * 128:(dc + 1) * 128],
                                                             rhs=vpp[p][0:L, 256:257], start=True, stop=True),
                             reads=[b_ktok[p], b_vpp[p]], writes=[V["bT"]], signal=(dc == 1))

                def upd(it):
                    V = views(it)
                    c, h = V["c"], V["h"]
                    T.op("dve", lambda e: e.scalar_tensor_tensor(
                        out=Cst[:, h, :, 0:256], in0=Cst[:, h, :, 0:256], scalar=decbc[:, h, c:c + 1], in1=V["psD"],
                        op0=ALU.mult, op1=ALU.add), reads=[b_C[h], b_decbc, V["bD"]], writes=[b_C[h]])
                    T.op("dve", lambda e: e.scalar_tensor_tensor(
                        out=Cst[:, h, :, 256], in0=Cst[:, h, :, 256], scalar=decbc[:, h, c:c + 1], in1=V["pdn"],
                        op0=ALU.mult, op1=ALU.add), reads=[b_C[h], b_decbc, V["bT"]], writes=[b_C[h]])

                def normA(it):
                    V = views(it)
                    p, sv = V["p"], st3[V["p"]]
                    T.op("act", lambda e: e.activation(out=sv[0:L, 5:6], in_=V["psN"][0:L, 256:257], func=AF.Abs),
                         reads=[V["bN"], b_st3[p]], writes=[b_st3[p]])

                def normB(it):
                    V = views(it)
                    c, h, p, sv = V["c"], V["h"], V["p"], st3[V["p"]]
                    T.op("dve", lambda e: e.tensor_tensor(out=sv[0:L, 1:2], in0=sv[0:L, 5:6],
                                                          in1=escT[0:L, c, 4 + h:5 + h], op=ALU.max),
                         reads=[b_escT, b_st3[p]], writes=[b_st3[p]])
                    T.op("dve", lambda e: e.reciprocal(out=sv[0:L, 2:3], in_=sv[0:L, 1:2]),
                         reads=[b_st3[p]], writes=[b_st3[p]])

                def normC(it):
                    V = views(it)
                    p, sv = V["p"], st3[V["p"]]
                    T.op("act", lambda e: e.activation(out=junk3[0:L, :], in_=V["psN"][0:L, 0:256], func=AF.Square,
                                                       scale=sv[0:L, 2:3], accum_out=sv[0:L, 0:1]),
                         reads=[V["bN"], b_st3[p]], writes=[b_junk3, b_st3[p]])
                    T.op("act", lambda e: e.activation(out=sv[0:L, 3:4], in_=sv[0:L, 0:1], func=AF.Ln,
                                                       scale=1.0 / DH, bias=EPS),
                         reads=[b_st3[p]], writes=[b_st3[p]])
                    T.op("act", lambda e: e.activation(out=sv[0:L, 6:7], in_=sv[0:L, 3:4], func=AF.Exp, scale=-0.5),
                         reads=[b_st3[p]], writes=[b_st3[p]])

                def normD(it):
                    V = views(it)
                    p, sv = V["p"], st3[V["p"]]
                    T.op("dve", lambda e: e.tensor_tensor(out=sv[0:L, 4:5], in0=sv[0:L, 6:7], in1=sv[0:L, 2:3],
                                                          op=ALU.mult),
                         reads=[b_st3[p]], writes=[b_st3[p]])

                def normE(it):
                    V = views(it)
                    c, hh, p, sv = V["c"], V["hh"], V["p"], st3[V["p"]]
                    hb = (c % 2)
                    T.op("act", lambda e: e.activation(out=hn[hb][0:L, hh * 256:(hh + 1) * 256], in_=V["psN"][0:L, 0:256],
                                                       func=AF.Identity, scale=sv[0:L, 4:5]),
                         reads=[V["bN"], b_st3[p]], writes=[b_hn[hb]])

                def chunk_end(c):
                    tsl = slice(c * L, (c + 1) * L)
                    ntc = (c * L) // nn
                    hb = (c % 2)
                    psH = PB[7][:, 0:256].bitcast(BF16).rearrange("p (i t) -> p i t", i=4)
                    for i in range(4):
                        T.op("pe", lambda e, i=i: e.transpose(out=psH[:, i, 0:L], in_=hn[hb][0:L, i * 128:(i + 1) * 128],
                                                              identity=ident_bf[0:L, 0:L]),
                             reads=[b_hn[hb], b_const], writes=[b_PB[7]], signal=(i == 3))
                    T.op("dve", lambda e: e.tensor_tensor(out=yaT[:, g * 4:(g + 1) * 4, tsl], in0=psH[:, :, 0:L],
                                                          in1=gaT[:, :, tsl], op=ALU.mult),
                         reads=[b_PB[7]] + [b_ga[i][ntc] for i in range(4)],
                         writes=[b_yaT[g * 4 + i][c] for i in range(4)])

                NI = len(iters)

                def okk(t):
                    return 0 <= t < NI

                pe1(0)
                ev1(0)
                for t in range(NI + 3):
                    if okk(t - 3) and iters[t - 3][1] == 1:
                        chunk_end(iters[t - 3][0])
                    if okk(t):
                        cbf(t)
                    if okk(t + 1):
                        pe1(t + 1)
                    if okk(t - 1):
                        normB(t - 1)
                    if okk(t - 2):
                        normD(t - 2)
                    if okk(t + 1):
                        ev1(t + 1)
                    if okk(t):
                        pe2(t)
                        upd(t)
                    if okk(t - 1):
                        normC(t - 1)
                    if okk(t - 2):
                        normE(t - 2)
                    if okk(t):
                        normA(t)
            T.barrier()
            A.off = PERSIST_END

            gate()
            s1 = [A.alloc(nn, F32) for _ in range(2)]
            s2 = [A.alloc(nn, F32) for _ in range(2)]
            depth = dict(xbp=2, szb=8, xc=5, rr=5, ii=5, mu=3)
            pools = {}
            for kname, dep in depth.items():
                n_el = nn + 4 if kname == "xbp" else nn
                dt_ = BF16 if kname == "xcb" else F32
                pools[kname] = [(A.alloc(n_el, dt_), Buf()) for _ in range(dep)]

            def PBf(kname, u):
                return pools[kname][u % depth[kname]]

            units = [(fc, nt) for fc in range(8) for nt in range(nnt)]
            NU = len(units)
            slotw = {}
            zbank = {}

            def pe_proj(u):
                fc, nt = units[u]
                if fc % 2 == 0 and nt == 0:
                    slotw["cur"] = ring_next(dual=True)
                wv, bw = slotw["cur"]
                bks = []
                for kind in range(2):
                    bk = 2 * kind + (u % 2)
                    bks.append(bk)
                    for kc in range(8):
                        T.op("pe", lambda e, kc=kc: e.matmul(
                            PB[bk][:, 0:nn], lhsT=wv[:, (fc % 2) * 2 + kind, kc, :], rhs=uT[:, kc, nt * nn:(nt + 1) * nn],
                            start=(kc == 0), stop=(kc == 7)),
                            reads=[bw] + uT_bufs(kc, nt), writes=[b_PB[bk]], signal=(kc == 7))
                zbank[u] = bks

            def act_p(u):
                xbp, b_xbp = PBf("xbp", u)
                szb, b_szb = PBf("szb", u)
                bx, bz = zbank[u]
                T.op("act", lambda e: e.activation(out=xbp[:, 3:3 + nn], in_=PB[bx][:, 0:nn], func=AF.Copy),
                     reads=[b_PB[bx]], writes=[b_xbp])
                T.op("act", lambda e: e.activation(out=szb, in_=PB[bz][:, 0:nn], func=AF.Sigmoid),
                     reads=[b_PB[bz]], writes=[b_szb])

            def dve_silu(u):
                szb, b_szb = PBf("szb", u)
                bz = zbank[u][1]
                T.op("dve", lambda e: e.tensor_tensor(out=szb, in0=PB[bz][:, 0:nn], in1=szb, op=ALU.mult),
                     reads=[b_PB[bz], b_szb], writes=[b_szb])

            def dve_conv(u):
                fc, nt = units[u]
                xbp, b_xbp = PBf("xbp", u)
                xc, b_xc = PBf("xc", u)
                cw = lambda j: pv[:, PV_CW + j * 8 + fc: PV_CW + j * 8 + fc + 1]
                ce = "pool"
                T.op(ce, lambda e: e.tensor_copy(out=xbp[:, 0:3], in_=convcar[:, fc, :]),
                     reads=[b_ccar[fc]], writes=[b_xbp])
                T.op(ce, lambda e: e.tensor_copy(out=convcar[:, fc, :], in_=xbp[:, nn:nn + 3]),
                     reads=[b_xbp], writes=[b_ccar[fc]])
                if OPT_B & 2:
                    T.op("act", lambda e: e.activation(out=xc, in_=xbp[:, 0:nn], func=AF.Identity, scale=cw(0),
                                                       bias=pv[:, PV_CB + fc:PV_CB + fc + 1]),
                         reads=[b_xbp, b_const], writes=[b_xc])
                else:
                    T.op("dve", lambda e: e.scalar_tensor_tensor(
                        out=xc, in0=xbp[:, 0:nn], scalar=cw(0), in1=pv[:, PV_CB + fc:PV_CB + fc + 1].to_broadcast([128, nn]),
                        op0=ALU.mult, op1=ALU.add), reads=[b_xbp, b_const], writes=[b_xc])
                for j in range(1, 4):
                    T.op("dve", lambda e, j=j: e.scalar_tensor_tensor(out=xc, in0=xbp[:, j:j + nn], scalar=cw(j),
                                                                      in1=xc, op0=ALU.mult, op1=ALU.add),
                         reads=[b_xbp, b_xc, b_const], writes=[b_xc])

            def pool_cast(u):
                return

            ribank = {}

            def pe_ri(u):
                fc, nt = units[u]
                xcb, b_xcb = PBf("xc", u)
                br_, bi_ = 4, 5
                ribank[u] = (br_, bi_)
                T.op("pe", lambda e: e.matmul(PB[br_][:, 0:nn], lhsT=wr_bf[:, 0, fc, :], rhs=xcb,
                                              start=True, stop=True), reads=[b_xcb, b_wc], writes=[b_PB[br_]])
                T.op("pe", lambda e: e.matmul(PB[bi_][:, 0:nn], lhsT=wr_bf[:, 1, fc, :], rhs=xcb,
                                              start=True, stop=True), reads=[b_xcb, b_wc], writes=[b_PB[bi_]])

            def act_ri(u):
                fc, nt = units[u]
                rr, b_rr = PBf("rr", u)
                ii, b_ii = PBf("ii", u)
                br_, bi_ = ribank[u]
                T.op("act", lambda e: e.activation(out=rr, in_=PB[br_][:, 0:nn], func=AF.Sigmoid,
                                                   bias=pv[:, PV_BRA + fc:PV_BRA + fc + 1]),
                     reads=[b_PB[br_], b_const], writes=[b_rr])
                T.op("act", lambda e: e.activation(out=ii, in_=PB[bi_][:, 0:nn], func=AF.Sigmoid,
                                                   bias=pv[:, PV_BRX + fc:PV_BRX + fc + 1]),
                     reads=[b_PB[bi_], b_const], writes=[b_ii])

            def act_exp(u):
                fc, nt = units[u]
                rr, b_rr = PBf("rr", u)
                mu, b_mu = PBf("mu", u)
                T.op("act", lambda e: e.activation(out=rr, in_=rr, func=AF.Exp, scale=cfeat[:, fc:fc + 1]),
                     reads=[b_rr, b_const], writes=[b_rr])
                if OPT_SQ:
                    T.op("pool", lambda e: e.tensor_tensor(out=mu, in0=rr, in1=rr, op=ALU.mult), reads=[b_rr], writes=[b_mu])
                else:
                    T.op("act", lambda e: e.activation(out=mu, in_=rr, func=AF.Square), reads=[b_rr], writes=[b_mu])
                T.op("act", lambda e: e.activation(out=mu, in_=mu, func=AF.Ln, scale=-1.0, bias=1.0),
                     reads=[b_mu], writes=[b_mu])
                T.op("act", lambda e: e.activation(out=mu, in_=mu, func=AF.Exp, scale=0.5), reads=[b_mu], writes=[b_mu])

            def pool_mul(u):
                ii, b_ii = PBf("ii", u)
                xc, b_xc = PBf("xc", u)
                T.op("pool", lambda e: e.tensor_tensor(out=ii, in0=ii, in1=xc, op=ALU.mult),
                     reads=[b_ii, b_xc], writes=[b_ii])

            def dve_r(u):
                fc, nt = units[u]
                rr, b_rr = PBf("rr", u)
                ii, b_ii = PBf("ii", u)
                mu, b_mu = PBf("mu", u)
                if st["prompt"] and st["first"] and nt == 0:
                    T.op("dve", lambda e: e.memset(mu[:, 0:1], 1.0), reads=[b_mu], writes=[b_mu])
                T.op("dve", lambda e: e.tensor_tensor(out=ii, in0=ii, in1=mu, op=ALU.mult),
                     reads=[b_ii, b_mu], writes=[b_ii])
                T.op("dve", lambda e: e.tensor_tensor_scan(out=mu, data0=rr, data1=ii,
                                                           initial=hcar[:, fc:fc + 1], op0=ALU.mult, op1=ALU.add),
                     reads=[b_rr, b_ii, b_hcar[fc], b_mu], writes=[b_mu])
                T.op("pool", lambda e: e.tensor_copy(out=hcar[:, fc:fc + 1], in_=mu[:, nn - 1:nn]),
                     reads=[b_mu], writes=[b_hcar[fc]])

            def pool_y(u):
                fc, nt = units[u]
                szb, b_szb = PBf("szb", u)
                mu, b_mu = PBf("mu", u)
                T.op("pool", lambda e: e.tensor_tensor(out=ybT[:, fc, nt * nn:(nt + 1) * nn], in0=szb, in1=mu,
                                                       op=ALU.mult),
                     reads=[b_szb, b_mu], writes=[b_ybT[fc][nt]])

            def ok(u):
                return 0 <= u < NU

            b_s1, b_s2 = bufs(2), bufs(2)
            slotc = {}

            def cprime(gi):
                m, nt = divmod(gi, nnt)
                if m % 2 == 0 and nt == 0:
                    slotc["cur"] = ring_next(dual=True)
                wv, bw = slotc["cur"]
                base = (m % 2) * 2
                sl = slice(nt * nn, (nt + 1) * nn)
                p = gi % 2
                for (ci, srcT, rb, bk) in ((base, yaT, lambda kc: [b_yaT[kc][nt * tpn + i] for i in range(tpn)], 6),
                                           (base + 1, uT, lambda kc: uT_bufs(kc, nt), 7)):
                    for kc in range(8):
                        T.op("pe", lambda e, ci=ci, kc=kc, srcT=srcT, bk=bk: e.matmul(
                            PB[bk][:, 0:nn], lhsT=wv[:, ci, kc, :], rhs=srcT[:, kc, sl],
                            start=(kc == 0), stop=(kc == 7)),
                            reads=[bw] + rb(kc), writes=[b_PB[bk]], signal=(kc == 7))
                T.op("act", lambda e: e.activation(out=s1[p], in_=PB[7][:, 0:nn], func=AF.Sigmoid,
                                                   bias=pv[:, PV_BG + m:PV_BG + m + 1]),
                     reads=[b_PB[7], b_const], writes=[b_s1[p]])
                T.op("dve", lambda e: e.tensor_tensor(out=mgT[:, m, sl], in0=PB[6][:, 0:nn], in1=s1[p], op=ALU.mult),
                     reads=[b_PB[6], b_s1[p]], writes=[b_mgT[m][nt]])

            NG = 8 * nnt
            for s_ in range(NU + 8):
                if ok(s_ - 4):
                    act_ri(s_ - 4)
                if ok(s_):
                    pe_proj(s_)
                if ok(s_ - 3):
                    pe_ri(s_ - 3)
                if s_ % 2 == 1:
                    for u in (s_ - 6, s_ - 5):
                        if ok(u):
                            act_exp(u)
                if ok(s_ - 1):
                    act_p(s_ - 1)
                if s_ % 2 == 0:
                    for u in (s_ - 7, s_ - 6):
                        if ok(u):
                            dve_r(u)
                if ok(s_ - 2):
                    dve_conv(s_ - 2)
                if ok(s_ - 1):
                    dve_silu(s_ - 1)
                if ok(s_ - 2):
                    pool_cast(s_ - 2)
                if ok(s_ - 5):
                    pool_mul(s_ - 5)
                if s_ % 2 == 0:
                    for u in (s_ - 7, s_ - 6):
                        if ok(u):
                            pool_y(u)
                if s_ < NG:
                    cprime(s_)

            gate()
            cc = 0
            for m in range(8):
                if m % 2 == 0:
                    wv, bw = ring_next()
                base = (m % 2) * 2
                for nt in range(nnt):
                    sl = slice(nt * nn, (nt + 1) * nn)
                    p = cc % 2
                    cc += 1
                    bkb, bkg = next_bank(), next_bank()
                    for (ci, srcT, rb, bk) in ((base, ybT, lambda kc: [b_ybT[kc][0], b_ybT[kc][1]], bkb),
                                               (base + 1, uT, lambda kc: uT_bufs(kc, nt), bkg)):
                        for kc in range(8):
                            T.op("pe", lambda e, ci=ci, kc=kc, srcT=srcT, bk=bk: e.matmul(
                                PB[bk][:, 0:nn], lhsT=wv[:, ci, kc, :], rhs=srcT[:, kc, sl],
                                start=(kc == 0), stop=(kc == 7)),
                                reads=[bw] + rb(kc), writes=[b_PB[bk]], signal=(kc == 7))
                    T.op("act", lambda e: e.activation(out=s2[p], in_=PB[bkg][:, 0:nn], func=AF.Sigmoid,
                                                       bias=pv[:, PV_BG + 8 + m:PV_BG + 8 + m + 1]),
                         reads=[b_PB[bkg], b_const], writes=[b_s2[p]])
                    T.op("dve", lambda e: e.tensor_tensor(out=s2[p], in0=PB[bkb][:, 0:nn], in1=s2[p], op=ALU.mult),
                         reads=[b_PB[bkb], b_s2[p]], writes=[b_s2[p]])
                    T.op("pool", lambda e: e.tensor_tensor(out=mgT[:, m, sl], in0=mgT[:, m, sl], in1=s2[p], op=ALU.add),
                         reads=[b_s2[p], b_mgT[m][nt]], writes=[b_mgT[m][nt]])
            T.barrier()
            A.off = PERSIST_END
            if DBG_DUMP and st["tok0"] == 0:
                dtmp = A.alloc(8 * TT, F32)
                b_dtmp = Buf()
                for di, srcT in enumerate((yaT, ybT, mgT)):
                    T.op("dve", lambda e, srcT=srcT: e.tensor_copy(out=dtmp, in_=srcT.rearrange("p f t -> p (f t)")),
                         writes=[b_dtmp])
                    T.dma("sp", dbg_d[di], dtmp, [b_dtmp], [], "dbgo")
                T.barrier()
                A.off = PERSIST_END

            gate()
            ZD = pd_alloc(2 if st["prompt"] else 1)
            nxt = sts[idx + 1] if idx + 1 < len(sts) else None
            nD = Tn // tp
            nP = (nxt["T"] // nxt["tp"]) if nxt is not None else 0
            Z0 = p0_alloc() if nxt is not None else None
            if nxt is not None and nxt["prompt"]:
                ring_issue_upto(min(total_loads[0] + NS - 1, NLOADS_TOTAL - 1))
            pd_tile(st, 0, ZD, load_only=True)
            if nP > 0:
                p0_tile(nxt, 0, Z0, load_only=True)
            for j in range(max(nD, nP)):
                if j + 1 < nD:
                    pd_tile(st, j + 1, ZD, load_only=True)
                if j + 1 < nP:
                    p0_tile(nxt, j + 1, Z0, load_only=True)
                hd, hp = j < nD, j < nP
                if hd:
                    pd_tile(st, j, ZD, stages=("pe",))
                if hp:
                    p0_tile(nxt, j, Z0, stages=("stats", "xn", "tr"))
                if hd:
                    pd_tile(st, j, ZD, stages=("mul", "add"))
                if hp:
                    p0_tile(nxt, j, Z0, stages=("ev",))
                if hd:
                    pd_tile(st, j, ZD, stages=("stats", "fin"))
            if st["last"]:
                out_toks["oCT%d" % s] = T.dma("sp", oCT_d[s], Cst.rearrange("p h c v -> p (h c v)"), b_C, [], "o_CT")
                out_toks["om%d" % s] = T.dma("sp", om_d[s], mstate[0:4, :], [b_ms], [], "o_m")
                out_toks["oh%d" % s] = T.dma("sp", oh_d[s], hcar, b_hcar, [], "o_h")
                out_toks["oc%d" % s] = T.dma("sp", oconv_d[s], convcar.rearrange("p f j -> p (f j)"), b_ccar, [], "o_c")
            T.barrier()
            A.off = PERSIST_END

        try:
            for i_ in range(len(sts)):
                run_st(i_)
        except StopBuild:
            T.pending = []

        T.barrier()
        T.barrier(engines=["sp"])
    return nc


_CACHE = {}


def _f32(a):
    return np.ascontiguousarray(np.asarray(a, dtype=np.float32))


def kernel(x_prompt, x_sample, c_prompt, c_sample, state_C, state_n, state_m, state_h, state_conv,
           w_mod, b_mod, g_norm, w_in, b_if, g_head, conv_w, conv_b, w_ra, b_ra, w_rx, b_rx, lam,
           w_gate, b_gate, w_out_a, w_out_b, w_o, g_final):
    n = 8
    x_prompt, x_sample = _f32(x_prompt), _f32(x_sample)
    c_prompt, c_sample = _f32(c_prompt), _f32(c_sample)
    w_in0 = _f32(w_in)[0]
    def fm(v, nchunk):
        return np.ascontiguousarray(_f32(v).reshape(nchunk, 128).T)
    pv = np.zeros((128, NPV), np.float32)
    pv[:, PV_BMOD:PV_BMOD + 24] = fm(b_mod[0], 24)
    pv[:, PV_GN:PV_GN + 8] = fm(g_norm[0], 8)
    pv[:, PV_GH:PV_GH + 8] = fm(g_head[0], 8)
    for j in range(4):
        pv[:, PV_CW + j * 8:PV_CW + j * 8 + 8] = fm(_f32(conv_w)[0, j], 8)
    pv[:, PV_CB:PV_CB + 8] = fm(conv_b[0], 8)
    pv[:, PV_BRA:PV_BRA + 8] = fm(b_ra[0], 8)
    pv[:, PV_BRX:PV_BRX + 8] = fm(b_rx[0], 8)
    pv[:, PV_LAM:PV_LAM + 8] = fm(lam[0], 8)
    pv[:, PV_BG:PV_BG + 16] = fm(b_gate[0], 16)
    bif = np.ascontiguousarray(_f32(b_if)[0].reshape(2, 4).T)
    gfin = _f32(g_final).reshape(1, D)
    wmod = np.ascontiguousarray(_f32(w_mod)[0].reshape(8, 128, 3 * D).transpose(1, 0, 2))

    def chunk(W, col0):
        return W[:, col0:col0 + 128].reshape(8, 128, 128).transpose(1, 0, 2).reshape(128, 1024)
    base = {"q": 0, "k": 1024, "v": 2048, "o": 3072, "z": 4096}
    chunks = []
    for g in range(2):
        for name in ("q", "k", "v", "o", "z"):
            for hh in range(2):
                for dc in range(2):
                    chunks.append(chunk(w_in0, base[name] + (2 * g + hh) * 256 + dc * 128))
    wg0, wa0, wb0 = _f32(w_gate)[0], _f32(w_out_a)[0], _f32(w_out_b)[0]
    for k in range(4):
        for fc in (2 * k, 2 * k + 1):
            chunks.append(chunk(w_in0, 5128 + fc * 128))
            chunks.append(chunk(w_in0, 6152 + fc * 128))
        for m in (2 * k, 2 * k + 1):
            chunks.append(chunk(wa0, m * 128))
            chunks.append(chunk(wg0, m * 128))
    for k in range(4):
        for m in (2 * k, 2 * k + 1):
            chunks.append(chunk(wb0, m * 128))
            chunks.append(chunk(wg0, 1024 + m * 128))
    wst = np.ascontiguousarray(np.stack(chunks, 0))
    assert wst.shape == (NCH_W, 128, 1024)
    wif = np.ascontiguousarray(w_in0[:, 5120:5128].reshape(8, 128, 8).transpose(1, 0, 2).reshape(128, 64))
    wr = np.ascontiguousarray(np.stack([_f32(w_ra)[0].transpose(1, 0, 2), _f32(w_rx)[0].transpose(1, 0, 2)], 1)
                              .reshape(128, 2048))
    wo = np.ascontiguousarray(_f32(w_o)[0].reshape(8, 128, D).transpose(1, 0, 2).reshape(128, 8 * D))
    sC, sn, sm_, sh_, sc_ = _f32(state_C)[0], _f32(state_n)[0], _f32(state_m)[0], _f32(state_h)[0], _f32(state_conv)[0]

    in_maps = []
    for c in range(n):
        xs = np.concatenate([x_prompt[2 * c].reshape(SEQ, D), x_prompt[2 * c + 1].reshape(SEQ, D),
                             x_sample[c].reshape(DEC, D)], 0)
        cs_ = np.stack([c_prompt[2 * c], c_prompt[2 * c + 1], c_sample[c]], 0)
        cT = np.ascontiguousarray(cs_.reshape(3, 8, 128).transpose(2, 1, 0))
        CT = sC[c].reshape(4, 256, 2, 128).transpose(3, 0, 2, 1)
        nT = sn[c].reshape(4, 2, 128).transpose(2, 0, 1)[..., None]
        sCT = np.ascontiguousarray(np.concatenate([CT, nT], -1).reshape(128, 4 * 2 * 257))
        in_maps.append({
            "x": np.ascontiguousarray(xs), "cT": cT, "wmod": wmod, "pv": pv, "bif": bif, "gfin": gfin,
            "wst": wst, "wif": wif, "wr": wr, "wo": wo, "sCT": sCT,
            "sm": np.ascontiguousarray(sm_[c].reshape(4, 1)),
            "sh": np.ascontiguousarray(sh_[c].reshape(8, 128).T),
            "sconv": np.ascontiguousarray(sc_[c].reshape(3, 8, 128).transpose(2, 1, 0).reshape(128, 24)),
        })
    if "nc" not in _CACHE:
        _CACHE["nc"] = build_program()
    res = run_bass_kernel_spmd(_CACHE["nc"], in_maps, core_ids=list(range(n)))
    R = res.results
    y_p = np.zeros((16, SEQ, D), np.float32)
    y_s = np.zeros((8, DEC, D), np.float32)
    Cs = np.zeros((24, 4, 256, 256), np.float32)
    ns = np.zeros((24, 4, 256), np.float32)
    ms = np.zeros((24, 4), np.float32)
    hs = np.zeros((24, D), np.float32)
    cvs = np.zeros((24, 3, D), np.float32)
    for c in range(n):
        r = R[c]
        y_p[2 * c] = r["y"][0:SEQ]
        y_p[2 * c + 1] = r["y"][SEQ:2 * SEQ]
        y_s[c] = r["y"][2 * SEQ:]
        for s, gi in enumerate((2 * c, 2 * c + 1, 16 + c)):
            o = r["oCT"][s].reshape(128, 4, 2, 257)
            Cs[gi] = o[..., 0:256].transpose(1, 3, 2, 0).reshape(4, 256, 256)
            ns[gi] = o[..., 256].transpose(1, 2, 0).reshape(4, 256)
            ms[gi] = r["om"][s].reshape(4)
            hs[gi] = r["oh"][s].T.reshape(D)
            cvs[gi] = r["oconv"][s].reshape(128, 8, 3).transpose(2, 1, 0).reshape(3, D)
    return (y_p, y_s, Cs[None, :16], ns[None, :16], ms[None, :16], hs[None, :16], cvs[None, :16],
            Cs[None, 16:], ns[None, 16:], ms[None, 16:], hs[None, 16:], cvs[None, 16:])
```

```python
import numpy as np
from contextlib import ExitStack
import concourse.bass as bass
import concourse.mybir as mybir
from concourse.bass_utils import run_bass_kernel_spmd

F32 = mybir.dt.float32
BF16 = mybir.dt.bfloat16
U8 = mybir.dt.uint8
AF = mybir.ActivationFunctionType
ALU = mybir.AluOpType
AX = mybir.AxisListType

D = 1024
H = 4
DH = 256
SEQ = 2048
DEC = 16
NTOK = 2 * SEQ + DEC
TT = 1024
EPS = 1e-6
NCH_W = 88
NSLOT_LOADS = NCH_W // 4
NS = 3
PV_BMOD, PV_GN, PV_GH, PV_CW, PV_CB, PV_BRA, PV_BRX, PV_LAM, PV_BG = 0, 24, 32, 40, 72, 80, 88, 96, 104
NPV = 120
DBG_SUB = 0
OPT_B = 0
OPT_SQ = 0
STQ = 'sp'


class Buf:
    __slots__ = ("w", "r", "excl")

    def __init__(self, excl=False):
        self.w = None
        self.r = {}
        self.excl = excl


def bufs(n, excl=False):
    return [Buf(excl) for _ in range(n)]


class TR:
    def __init__(self, nc, es):
        self.nc = nc
        self.es = es
        self.eng = {"pe": nc.tensor, "act": nc.scalar, "dve": nc.vector, "pool": nc.gpsimd, "sp": nc.sync}
        self.csem = {k: es.enter_context(nc.semaphore("c_" + k)) for k in ("pe", "act", "dve", "pool")}
        self.cnt = {k: 0 for k in self.csem}
        self.known = {k: {} for k in self.eng}
        self.dsem = {}
        self.pending = []
        self.deferring = False
        self.deferred = []
        self.cur_bundle = []

    def _wait(self, e, tok):
        if tok is None:
            return
        key, sem, val = tok
        if e == "pe" and key == "pe":
            return
        if self.known[e].get(key, 0) >= val:
            return
        self.eng[e].wait_ge(sem, val)
        self.known[e][key] = val

    def _deps(self, e, reads, writes):
        for b in reads:
            self._wait(e, b.w)
            if b.excl:
                for key, (sem, val) in list(b.r.items()):
                    self._wait(e, (key, sem, val))
        for b in writes:
            self._wait(e, b.w)
            for key, (sem, val) in list(b.r.items()):
                self._wait(e, (key, sem, val))

    @staticmethod
    def _mark(tok, reads, writes):
        key, sem, val = tok
        for b in reads:
            if b.excl:
                b.w = tok
                b.r = {}
            else:
                b.r[key] = (sem, val)
        for b in writes:
            b.w = tok
            b.r = {}

    def op(self, e, fn, reads=(), writes=(), signal=True):
        if self.deferring:
            self.cur_bundle.append((e, fn, list(reads), list(writes), signal))
            if signal:
                self.deferred.append(self.cur_bundle)
                self.cur_bundle = []
            return None
        return self._emit(e, fn, reads, writes, signal)

    def flush(self, n=None):
        was = self.deferring
        self.deferring = False
        k = 0
        while self.deferred and (n is None or k < n):
            for args in self.deferred.pop(0):
                self._emit(*args)
            k += 1
        self.deferring = was

    def _emit(self, e, fn, reads=(), writes=(), signal=True):
        self._deps(e, reads, writes)
        inst = fn(self.eng[e])
        if not signal:
            assert e == "pe"
            self.pending.append((reads, writes))
            return None
        self.cnt[e] += 1
        inst.then_inc(self.csem[e], 1)
        tok = (e, self.csem[e], self.cnt[e])
        if e == "pe" and self.pending:
            for r, w in self.pending:
                self._mark(tok, r, w)
            self.pending = []
        self._mark(tok, reads, writes)
        return tok

    def dma(self, q, out, in_, reads, writes, slot, **kw):
        self._deps(q, reads, writes)
        if slot not in self.dsem:
            self.dsem[slot] = [self.es.enter_context(self.nc.semaphore("d_" + slot)), 0]
        ent = self.dsem[slot]
        ent[1] += 16
        self.eng[q].dma_start(out=out, in_=in_, **kw).then_inc(ent[0], 16)
        tok = ("d_" + slot, ent[0], ent[1])
        self._mark(tok, reads, writes)
        return tok

    def barrier(self, engines=None):
        assert not self.pending
        for e in (engines or ("act", "dve", "pool", "sp")):
            for k in self.csem:
                if self.cnt[k] > 0:
                    self._wait(e, (k, self.csem[k], self.cnt[k]))
            for slot, (sem, c) in self.dsem.items():
                if c > 0:
                    self._wait(e, ("d_" + slot, sem, c))


class Arena:
    def __init__(self, ap, size):
        self.ap = ap
        self.size = size
        self.off = 0

    def alloc(self, nelem, dt):
        esz = 4 if dt == F32 else 2
        off = (self.off + 63) // 64 * 64
        self.off = off + nelem * esz
        assert self.off <= self.size, ("arena overflow", self.off, self.size)
        return self.ap[:, off:off + nelem * esz].bitcast(dt)


class StopBuild(Exception):
    pass


def build_program(maxphase=10 ** 9):
    nc = bass.Bass("TRN2", target_bir_lowering=False)
    phase_ctr = [0]

    def gate():
        phase_ctr[0] += 1
        if phase_ctr[0] > maxphase:
            raise StopBuild()

    def din(name, shape):
        return nc.dram_tensor(name, shape, F32, kind="ExternalInput").ap()

    def dout(name, shape):
        return nc.dram_tensor(name, shape, F32, kind="ExternalOutput").ap()

    x_d = din("x", [NTOK, D])
    cT_d = din("cT", [128, 8, 3])
    wmod_d = din("wmod", [128, 8, 3 * D])
    pv_d = din("pv", [128, NPV])
    bif_d = din("bif", [4, 2])
    gfin_d = din("gfin", [1, D])
    wst_d = din("wst", [NCH_W, 128, D])
    wif_d = din("wif", [128, 64])
    wr_d = din("wr", [128, 2048])
    wo_d = din("wo", [128, 8 * D])
    sCT_d = din("sCT", [128, 4 * 2 * 257])
    sm_d = din("sm", [4, 1])
    sh_d = din("sh", [128, 8])
    sconv_d = din("sconv", [128, 24])
    wscr_d = nc.dram_tensor("wscr", [NSLOT_LOADS, 128, 4 * D], BF16, kind="Internal").ap()
    y_d = dout("y", [NTOK, D])
    oCT_d = dout("oCT", [3, 128, 4 * 2 * 257])
    om_d = dout("om", [3, 4, 1])
    oh_d = dout("oh", [3, 128, 8])
    oconv_d = dout("oconv", [3, 128, 24])
    DBG_DUMP = 0
    if DBG_DUMP:
        dbg_d = dout("dbg", [3, 128, 8 * TT])

    with ExitStack() as es:
        ARENA = 207 * 1024
        arena_t = es.enter_context(nc.sbuf_tensor("arena", [128, ARENA], U8))
        PB = [es.enter_context(nc.psum_tensor("pb%d" % i, [128, 512], F32)) for i in range(8)]
        T = TR(nc, es)
        A = Arena(arena_t, ARENA)
        out_toks = {}
        fold_wo = [False]

        wo_bf = A.alloc(8 * D, BF16).rearrange("p (k n) -> p k n", k=8)
        wr_bf = A.alloc(2048, F32).rearrange("p (g n j) -> p g n j", g=2, n=8)
        wif_bf = A.alloc(64, BF16).rearrange("p (k m) -> p k m", k=8)
        ident_bf = A.alloc(128, BF16)
        ident_f = A.alloc(128, F32)
        ones_f = A.alloc(128, F32)
        maskT = A.alloc(128, BF16)
        pv = A.alloc(NPV, F32)
        bif = A.alloc(2, F32)
        nbf = A.alloc(1, F32)
        cT = A.alloc(24, F32).rearrange("p (k s) -> p k s", k=8)
        modT = A.alloc(72, F32).rearrange("p (m s) -> p m s", m=24)
        gs = A.alloc(24, F32).rearrange("p (m s) -> p m s", m=8)
        cfeat = A.alloc(8, F32)
        lamt = A.alloc(8, F32)
        gfin_bc = A.alloc(D, F32)
        gate_bc = A.alloc(D, F32)
        gdiag = None
        Cst = A.alloc(4 * 2 * 257, F32).rearrange("p (h c v) -> p h c v", h=4, c=2)
        Cbf = A.alloc(4 * 2 * 258, BF16).rearrange("p (h c v) -> p h c v", h=4, c=2)
        mstate = A.alloc(1, F32)
        hcar = A.alloc(8, F32)
        convcar = A.alloc(24, F32).rearrange("p (f j) -> p f j", f=8)
        rmask = A.alloc(TT, F32)
        uT = A.alloc(8 * TT, BF16).rearrange("p (f t) -> p f t", f=8)
        yaT = A.alloc(8 * TT, BF16).rearrange("p (f t) -> p f t", f=8)
        ybT = A.alloc(8 * TT, BF16).rearrange("p (f t) -> p f t", f=8)
        mgT = A.alloc(8 * TT, BF16).rearrange("p (f t) -> p f t", f=8)
        ring = [A.alloc(4 * D, BF16).rearrange("p (c k m) -> p c k m", c=4, k=8) for _ in range(NS)]
        PERSIST_END = A.off

        b_const = Buf()
        b_wc = Buf()
        b_wo = Buf()
        b_mod = Buf()
        b_gbc = Buf()
        b_ring = bufs(NS)
        b_uT = [bufs(8) for _ in range(8)]
        b_yaT = [bufs(8) for _ in range(8)]
        b_ybT = [bufs(2) for _ in range(8)]
        b_mgT = [bufs(2) for _ in range(8)]
        b_C = bufs(4)
        b_Cbf = bufs(4)
        b_ms = Buf()
        b_hcar = bufs(8)
        b_ccar = bufs(8)
        b_PB = bufs(8, excl=True)

        T.op("pool", lambda e: e.memset(ident_bf, 0.0), writes=[b_const])
        T.op("pool", lambda e: e.affine_select(out=ident_bf, in_=ident_bf, pattern=[[1, 128]],
                                                compare_op=ALU.not_equal, fill=1.0, base=0,
                                                channel_multiplier=-1), reads=[b_const], writes=[b_const])
        T.op("pool", lambda e: e.memset(ident_f, 0.0), writes=[b_const])
        T.op("pool", lambda e: e.affine_select(out=ident_f, in_=ident_f, pattern=[[1, 128]],
                                                compare_op=ALU.not_equal, fill=1.0, base=0,
                                                channel_multiplier=-1), reads=[b_const], writes=[b_const])
        T.op("pool", lambda e: e.memset(ones_f, 1.0), writes=[b_const])
        T.op("pool", lambda e: e.memset(maskT, 1.0), writes=[b_const])
        T.op("pool", lambda e: e.affine_select(out=maskT, in_=maskT, pattern=[[1, 128]],
                                                compare_op=ALU.is_ge, fill=0.0, base=0,
                                                channel_multiplier=-1), reads=[b_const], writes=[b_const])
        T.op("pool", lambda e: e.memset(rmask[0:4, :], 1.0), writes=[b_const])
        T.op("pool", lambda e: e.memset(rmask[0:4, :].rearrange("p (c l) -> p c l", l=128)[:, :, 0:1], 0.0),
             reads=[b_const], writes=[b_const])
        T.dma("sp", pv, pv_d, [], [b_const], "s_pv")
        T.dma("sp", bif[0:4, :], bif_d, [], [b_const], "s_bif")
        T.dma("sp", cT.rearrange("p k s -> p (k s)"), cT_d.rearrange("p k s -> p (k s)"), [], [b_const], "s_cT")
        T.dma("sp", gfin_bc, gfin_d.to_broadcast([128, D]), [], [b_const], "s_gfin")
        T.op("act", lambda e: e.activation(out=lamt, in_=pv[:, PV_LAM:PV_LAM + 8], func=AF.Exp, scale=-1.0),
             reads=[b_const], writes=[b_const])
        T.op("act", lambda e: e.activation(out=lamt, in_=lamt, func=AF.Ln, bias=1.0),
             reads=[b_const], writes=[b_const])
        T.op("dve", lambda e: e.tensor_scalar(out=cfeat, in0=lamt, scalar1=-8.0, scalar2=None, op0=ALU.mult),
             reads=[b_const], writes=[b_const])
        T.op("dve", lambda e: e.tensor_scalar(out=nbf[0:4, :], in0=bif[0:4, 1:2], scalar1=-1.0, scalar2=None,
                                              op0=ALU.mult), reads=[b_const], writes=[b_const])

        wm = A.alloc(8 * 3072, BF16).rearrange("p (k n) -> p k n", k=8)
        cTb = A.alloc(24, BF16).rearrange("p (k s) -> p k s", k=8)
        b_wm = bufs(4)
        for q4 in range(4):
            T.dma("pool", wm[:, :, q4 * 768:(q4 + 1) * 768], wmod_d[:, :, q4 * 768:(q4 + 1) * 768], [], [b_wm[q4]],
                  "s_wm%d" % q4)
        T.dma("pool", wif_bf.rearrange("p k m -> p (k m)"), wif_d, [], [b_wc], "s_wif")
        T.dma("sp", wr_bf.rearrange("p g n j -> p (g n j)"), wr_d, [], [b_wc], "s_wr")
        T.dma("pool", wo_bf.rearrange("p k n -> p (k n)"), wo_d, [], [b_wo], "s_wo")
        T.op("dve", lambda e: e.tensor_copy(out=cTb, in_=cT), reads=[b_const], writes=[b_const])
        for m in range(24):
            for kc in range(8):
                T.op("pe", lambda e, m=m, kc=kc: e.matmul(
                    PB[0][:, 4 * m:4 * m + 3], lhsT=wm[:, kc, m * 128:(m + 1) * 128], rhs=cTb[:, kc, :],
                    start=(kc == 0), stop=(kc == 7)),
                    reads=[b_wm[m // 6], b_const], writes=[b_PB[0]], signal=(kc == 7))
        T.op("dve", lambda e: e.tensor_tensor(
            out=modT, in0=PB[0][:, 0:96].rearrange("p (m s) -> p m s", s=4)[:, :, 0:3],
            in1=pv[:, PV_BMOD:PV_BMOD + 24].unsqueeze(2).to_broadcast([128, 24, 3]), op=ALU.add),
            reads=[b_PB[0], b_const], writes=[b_mod])
        T.op("dve", lambda e: e.tensor_scalar(out=gs, in0=modT[:, 8:16, :], scalar1=1.0, scalar2=None, op0=ALU.add),
             reads=[b_mod], writes=[b_mod])
        T.op("dve", lambda e: e.tensor_tensor(
            out=gs, in0=gs, in1=pv[:, PV_GN:PV_GN + 8].unsqueeze(2).to_broadcast([128, 8, 3]), op=ALU.mult),
            reads=[b_mod, b_const], writes=[b_mod])
        T.barrier()
        A.off = PERSIST_END

        ring_state = {"issued": 0}
        b_scr = bufs(NSLOT_LOADS)
        NS_S = 6
        SAMPLE_BASE = NSLOT_LOADS * 4
        RS_OFF = ARENA - NS_S * 8192
        ring_s = [arena_t[:, RS_OFF + i * 8192: RS_OFF + (i + 1) * 8192].bitcast(BF16)
                  .rearrange("p (c k m) -> p c k m", c=4, k=8) for i in range(NS_S)]
        b_ring_s = bufs(NS_S)

        def ring_slot(j):
            if j >= SAMPLE_BASE:
                i = (j - SAMPLE_BASE) % NS_S
                return ring_s[i], b_ring_s[i], "ringX%d" % i, NS_S
            return ring[j % NS], b_ring[j % NS], "ring%d" % (j % NS), NS

        def ring_issue_upto(k):
            while ring_state["issued"] <= k:
                j = ring_state["issued"]
                ci = (j % NSLOT_LOADS) * 4
                jj = j % NSLOT_LOADS
                rv, rb, rname, _ = ring_slot(j)
                if j < NSLOT_LOADS:
                    sl = j % NS
                    T.dma("pool", rv.rearrange("p c k m -> p c (k m)"),
                          wst_d[ci:ci + 4].rearrange("c p n -> p c n"), [], [rb], "ringS%d" % sl)
                    T.dma("sp", wscr_d[jj], rv.rearrange("p c k m -> p (c k m)"), [rb], [b_scr[jj]], "scr%d" % sl)
                else:
                    T.dma("sp", rv.rearrange("p c k m -> p (c k m)"), wscr_d[jj], [b_scr[jj]], [rb], rname)
                ring_state["issued"] += 1

        total_loads = [0]

        def ring_next(dual=False):
            k = total_loads[0]
            total_loads[0] += 1
            rv, rb, _, ns = ring_slot(k)
            look = ns - 2 if dual else ns - 1
            hi = k + look
            if k < SAMPLE_BASE:
                hi = min(hi, SAMPLE_BASE - 1)
            ring_issue_upto(min(hi, NLOADS_TOTAL - 1))
            return rv, rb

        sts = []
        for s in range(2):
            for part in range(SEQ // TT):
                sts.append(dict(seq=s, tok0=s * SEQ + part * TT, T=TT, tp=128, first=(part == 0),
                                last=(part == SEQ // TT - 1), prompt=True))
        sts.append(dict(seq=2, tok0=2 * SEQ, T=DEC, tp=DEC, first=True, last=True, prompt=False))
        NLOADS_TOTAL = NSLOT_LOADS * len(sts)

        psum_rr = [0]
        bank_pool = [list(range(8))]

        def next_bank():
            pool = bank_pool[0]
            i = pool[psum_rr[0] % len(pool)]
            psum_rr[0] += 1
            return i

        def proj(wv, bw, ci, src, bsrc, Tn, nn, evac):
            for nt in range(Tn // nn):
                bk = next_bank()
                for kc in range(8):
                    T.op("pe", lambda e, bk=bk, kc=kc, nt=nt: e.matmul(
                        PB[bk][:, 0:nn], lhsT=wv[:, ci, kc, :], rhs=src[:, kc, nt * nn:(nt + 1) * nn],
                        start=(kc == 0), stop=(kc == 7)),
                        reads=[bw] + bsrc(kc, nt), writes=[b_PB[bk]], signal=(kc == 7))
                evac(bk, nt)

        def p0_alloc():
            return dict(xin=[A.alloc(D, F32) for _ in range(2)], xn=[A.alloc(D, BF16) for _ in range(2)],
                        junk=A.alloc(D, BF16), stat=[A.alloc(4, F32) for _ in range(2)],
                        b_xin=bufs(2), b_xn=bufs(2), b_stat=bufs(2), b_junk=Buf())

        def p0_tile(st, j, Z, load_only=False, stages=("stats", "xn", "tr", "ev")):
            s, tp, tok0 = st["seq"], st["tp"], st["tok0"]
            p = j % 2
            xin, xn, junk, stat = Z["xin"][p], Z["xn"][p], Z["junk"], Z["stat"][p]
            b_xin, b_xn, b_stat, b_junk = Z["b_xin"][p], Z["b_xn"][p], Z["b_stat"][p], Z["b_junk"]
            if load_only:
                T.dma("sp", xin[0:tp, :], x_d[tok0 + j * tp: tok0 + (j + 1) * tp, :], [], [b_xin], "xin%d" % p)
                return
            if "stats" in stages:
                T.op("act", lambda e: e.activation(out=junk[0:tp, :], in_=xin[0:tp, :], func=AF.Square,
                                                   accum_out=stat[0:tp, 0:1]), reads=[b_xin], writes=[b_junk, b_stat])
                T.op("act", lambda e: e.activation(out=stat[0:tp, 1:2], in_=stat[0:tp, 0:1], func=AF.Ln,
                                                   scale=1.0 / D, bias=EPS), reads=[b_stat], writes=[b_stat])
                T.op("act", lambda e: e.activation(out=stat[0:tp, 2:3], in_=stat[0:tp, 1:2], func=AF.Exp,
                                                   scale=-0.5), reads=[b_stat], writes=[b_stat])
            if "xn" in stages:
                T.op("pool", lambda e: e.tensor_scalar(out=xn[0:tp, :], in0=xin[0:tp, :], scalar1=stat[0:tp, 2:3],
                                                       scalar2=0.0, op0=ALU.mult, op1=ALU.add),
                     reads=[b_xin, b_stat], writes=[b_xn])
            def bank_of(fc):
                if fc < 6:
                    return 2 * p, PB[2 * p][:, 0:384].bitcast(BF16).rearrange("p (f t) -> p f t", f=6)[:, fc, 0:tp]
                return 2 * p + 1, PB[2 * p + 1][:, 0:128].bitcast(BF16).rearrange("p (f t) -> p f t", f=2)[:, fc - 6, 0:tp]
            if "tr" in stages:
                for fc in range(8):
                    bk, pv_ = bank_of(fc)
                    T.op("pe", lambda e, fc=fc, pv_=pv_: e.transpose(
                        out=pv_, in_=xn[0:tp, fc * 128:(fc + 1) * 128], identity=ident_bf[0:tp, 0:tp]),
                        reads=[b_xn, b_const], writes=[b_PB[bk]], signal=(fc in (5, 7)))
            if "ev" in stages:
                for fc in (0, 6, 1, 2, 7, 3, 4, 5):
                    bk, pv_ = bank_of(fc)
                    dst = uT[:, fc, j * tp:(j + 1) * tp]
                    if fc < 6:
                        T.op("dve", lambda e, fc=fc, dst=dst, pv_=pv_: e.tensor_scalar(
                            out=dst, in0=pv_, scalar1=gs[:, fc, s:s + 1], scalar2=modT[:, fc, s:s + 1],
                            op0=ALU.mult, op1=ALU.add), reads=[b_PB[bk], b_mod], writes=[b_uT[fc][j]])
                    else:
                        T.op("act", lambda e, fc=fc, dst=dst, pv_=pv_: e.activation(
                            out=dst, in_=pv_, func=AF.Identity, scale=gs[:, fc, s:s + 1],
                            bias=modT[:, fc, s:s + 1]), reads=[b_PB[bk], b_mod], writes=[b_uT[fc][j]])

        def pd_alloc(nb=2):
            one = lambda mk: [mk() for _ in range(nb)] * (2 // nb)
            return dict(xr=one(lambda: A.alloc(D, F32)), yt=one(lambda: A.alloc(D, F32)),
                        ot=one(lambda: A.alloc(D, F32)), junkd=A.alloc(D, BF16),
                        std=one(lambda: A.alloc(4, F32)),
                        b_xr=one(Buf), b_yt=one(Buf), b_ot=one(Buf), b_std=one(Buf), b_junkd=Buf())

        def pd_tile(st, j, Z, load_only=False, stages=("pe", "mul", "add", "stats", "fin")):
            tp, tok0 = st["tp"], st["tok0"]
            nn = min(512, st["T"])
            p = j % 2
            xr, yt, ot, junkd, std = Z["xr"][p], Z["yt"][p], Z["ot"][p], Z["junkd"], Z["std"][p]
            b_xr, b_yt, b_ot, b_std, b_junkd = Z["b_xr"][p], Z["b_yt"][p], Z["b_ot"][p], Z["b_std"][p], Z["b_junkd"]
            ntj = (j * tp) // nn
            rows = slice(tok0 + j * tp, tok0 + (j + 1) * tp)
            if load_only:
                T.dma("sp", xr[0:tp, :], x_d[rows, :], [], [b_xr], "xr%d" % p)
                return
            for half in range(2):
                bk = 4 + 2 * p + half
                if "pe" in stages:
                    for kc in range(8):
                        T.op("pe", lambda e, bk=bk, kc=kc, half=half: e.matmul(
                            PB[bk][0:tp, :], lhsT=mgT[:, kc, j * tp:(j + 1) * tp], rhs=wo_bf[:, kc, half * 512:(half + 1) * 512],
                            start=(kc == 0), stop=(kc == 7)),
                            reads=[b_mgT[kc][ntj], b_wo], writes=[b_PB[bk]], signal=(kc == 7))
                if "mul" in stages:
                    T.op("dve", lambda e, bk=bk, half=half: e.tensor_tensor(
                        out=yt[0:tp, half * 512:(half + 1) * 512], in0=PB[bk][0:tp, :],
                        in1=xr[0:tp, half * 512:(half + 1) * 512], op=ALU.add),
                        reads=[b_PB[bk], b_xr], writes=[b_yt])
            if "stats" in stages:
                T.op("act", lambda e: e.activation(out=junkd[0:tp, :], in_=yt[0:tp, :], func=AF.Square,
                                                   accum_out=std[0:tp, 0:1]), reads=[b_yt], writes=[b_junkd, b_std])
                T.op("act", lambda e: e.activation(out=std[0:tp, 1:2], in_=std[0:tp, 0:1], func=AF.Ln,
                                                   scale=1.0 / D, bias=EPS), reads=[b_std], writes=[b_std])
                T.op("act", lambda e: e.activation(out=std[0:tp, 2:3], in_=std[0:tp, 1:2], func=AF.Exp,
                                                   scale=-0.5), reads=[b_std], writes=[b_std])
            if "fin" in stages:
                T.op("dve", lambda e: e.scalar_tensor_tensor(out=ot[0:tp, :], in0=yt[0:tp, :], scalar=std[0:tp, 2:3],
                                                             in1=gfin_bc[0:tp, :], op0=ALU.mult, op1=ALU.mult),
                     reads=[b_yt, b_std, b_const], writes=[b_ot])
                out_toks["y%d" % p] = T.dma(STQ, y_d[rows, :], ot[0:tp, :], [b_ot], [], "yo%d" % p)

        def run_st(idx):
            st = sts[idx]
            s = st["seq"]
            if not st["prompt"]:
                A.size = RS_OFF
            Tn = st["T"]
            tp = st["tp"]
            L = tp
            nch = Tn // L
            nn = min(512, Tn)
            nnt = Tn // nn
            tpn = nn // tp
            tok0 = st["tok0"]

            def uT_bufs(kc, nt):
                return [b_uT[kc][nt * tpn + i] for i in range(tpn)]

            if st["first"]:
                if st["prompt"]:
                    for h in range(4):
                        T.op("pool", lambda e, h=h: e.memset(Cst[:, h].rearrange("p c v -> p (c v)"), 0.0),
                             writes=[b_C[h]])
                    T.op("pool", lambda e: e.memset(mstate[0:4, :], 0.0), writes=[b_ms])
                    T.op("pool", lambda e: e.memset(hcar, 0.0), writes=b_hcar)
                    T.op("pool", lambda e: e.memset(convcar.rearrange("p f j -> p (f j)"), 0.0), writes=b_ccar)
                else:
                    T.dma("sp", Cst.rearrange("p h c v -> p (h c v)"), sCT_d, [], b_C, "s_sCT")
                    T.dma("sp", mstate[0:4, :], sm_d, [], [b_ms], "s_sm")
                    T.dma("sp", hcar, sh_d, [], b_hcar, "s_sh")
                    T.dma("sp", convcar.rearrange("p f j -> p (f j)"), sconv_d, [], b_ccar, "s_sconv")
                gd = A.alloc(D, F32).rearrange("p (f j) -> p f j", f=8)
                b_gd = Buf()
                T.op("dve", lambda e: e.tensor_tensor(
                    out=gd, in0=modT[:, 16:24, s:s + 1].to_broadcast([128, 8, 128]),
                    in1=ident_f.unsqueeze(1).to_broadcast([128, 8, 128]), op=ALU.mult),
                    reads=[b_mod, b_const], writes=[b_gd])
                for half in range(2):
                    T.op("pe", lambda e, half=half: e.matmul(
                        PB[half][:, :], lhsT=ones_f, rhs=gd.rearrange("p f j -> p (f j)")[:, half * 512:(half + 1) * 512],
                        start=True, stop=True), reads=[b_gd, b_const], writes=[b_PB[half]])
                    T.op("act", lambda e, half=half: e.activation(
                        out=gate_bc[:, half * 512:(half + 1) * 512], in_=PB[half][:, :], func=AF.Copy),
                        reads=[b_PB[half]], writes=[b_gbc])
                T.barrier()
                A.off = PERSIST_END
                if s > 0:
                    T.dma("pool", wo_bf.rearrange("p k n -> p (k n)"), wo_d, [], [b_wo], "s_wo")
                fold_wo[0] = True

            if idx == 0:
                gate()
                Z0 = p0_alloc()
                p0_tile(st, 0, Z0, load_only=True)
                for j in range(Tn // tp):
                    if j + 1 < Tn // tp:
                        p0_tile(st, j + 1, Z0, load_only=True)
                    p0_tile(st, j, Z0)
                T.barrier()
                A.off = PERSIST_END

            gate()
            igb = A.alloc(Tn, F32)[0:4, :]
            cs = A.alloc(Tn, F32)[0:4, :]
            gg = A.alloc(Tn, F32)[0:4, :]
            esc = A.alloc(Tn, F32)[0:4, :]
            thr = A.alloc(Tn, F32)[0:4, :]
            sm_ = A.alloc(64, F32)[0:4, :]
            gmax, Ac, Bc, mall, Mc, mprev, dec = (sm_[:, i * 8:i * 8 + nch] for i in range(7))
            decd = A.alloc(32, F32)[0:4, 0:4 * nch].rearrange("p (a c) -> p a c", a=4)
            escT = A.alloc(64, F32)[:, 0:nch * 8].rearrange("p (c j) -> p c j", j=8)
            decbc = A.alloc(32, F32)[:, 0:4 * nch].rearrange("p (a c) -> p a c", a=4)
            b_g = Buf()
            b_escT = Buf()
            b_decbc = Buf()
            T.deferring = True
            for nt in range(nnt):
                sl = slice(nt * nn, (nt + 1) * nn)
                bi, bf_ = 6, 7
                for (bk, c0) in ((bi, 0), (bf_, 4)):
                    for kc in range(8):
                        T.op("pe", lambda e, bk=bk, c0=c0, kc=kc, sl=sl: e.matmul(
                            PB[bk][0:4, 0:nn], lhsT=wif_bf[:, kc, c0:c0 + 4], rhs=uT[:, kc, sl],
                            start=(kc == 0), stop=(kc == 7)),
                            reads=[b_wc] + uT_bufs(kc, nt), writes=[b_PB[bk]], signal=(kc == 7))
                T.op("dve", lambda e, bi=bi, sl=sl: e.tensor_scalar(
                    out=igb[:, sl], in0=PB[bi][0:4, 0:nn], scalar1=bif[0:4, 0:1], scalar2=None, op0=ALU.add),
                    reads=[b_PB[bi], b_const], writes=[b_g])
                T.op("act", lambda e, bf_=bf_, sl=sl: e.activation(
                    out=gg[:, sl], in_=PB[bf_][0:4, 0:nn], func=AF.Exp, scale=-1.0, bias=nbf[0:4, :]),
                    reads=[b_PB[bf_], b_const], writes=[b_g])
                T.op("act", lambda e, sl=sl: e.activation(out=gg[:, sl], in_=gg[:, sl], func=AF.Ln, bias=1.0),
                     reads=[b_g], writes=[b_g])
            T.op("dve", lambda e: e.tensor_tensor_scan(out=cs, data0=rmask[0:4, 0:Tn], data1=gg, initial=0.0,
                                                       op0=ALU.mult, op1=ALU.add), reads=[b_g, b_const], writes=[b_g])
            T.op("dve", lambda e: e.tensor_tensor(out=gg, in0=igb, in1=cs, op=ALU.add), reads=[b_g], writes=[b_g])
            T.op("dve", lambda e: e.tensor_reduce(out=gmax, in_=gg.rearrange("p (c l) -> p c l", l=L), axis=AX.X,
                                                  op=ALU.max), reads=[b_g], writes=[b_g])
            T.op("dve", lambda e: e.tensor_scalar(out=Ac, in0=cs.rearrange("p (c l) -> p c l", l=L)[:, :, L - 1],
                                                  scalar1=-1.0, scalar2=None, op0=ALU.mult), reads=[b_g], writes=[b_g])
            T.op("dve", lambda e: e.tensor_tensor(out=Bc, in0=gmax, in1=Ac, op=ALU.add), reads=[b_g], writes=[b_g])
            T.op("dve", lambda e: e.tensor_tensor_scan(out=mall, data0=Ac, data1=Bc, initial=mstate[0:4, :],
                                                       op0=ALU.add, op1=ALU.max), reads=[b_g, b_ms], writes=[b_g])
            T.op("dve", lambda e: e.tensor_tensor(out=Mc, in0=mall, in1=Ac, op=ALU.subtract), reads=[b_g], writes=[b_g])
            T.op("dve", lambda e: e.tensor_copy(out=mprev[:, 0:1], in_=mstate[0:4, :]), reads=[b_ms], writes=[b_g])
            if nch > 1:
                T.op("dve", lambda e: e.tensor_copy(out=mprev[:, 1:nch], in_=mall[:, 0:nch - 1]),
                     reads=[b_g], writes=[b_g])
            T.op("dve", lambda e: e.tensor_copy(out=mstate[0:4, :], in_=mall[:, nch - 1:nch]),
                 reads=[b_g], writes=[b_ms])
            T.op("dve", lambda e: e.tensor_tensor(out=dec, in0=mprev, in1=Mc, op=ALU.subtract), reads=[b_g], writes=[b_g])
            T.op("act", lambda e: e.activation(out=dec, in_=dec, func=AF.Exp), reads=[b_g], writes=[b_g])
            Mcb = Mc.unsqueeze(2).to_broadcast([4, nch, L])
            T.op("dve", lambda e: e.tensor_tensor(out=esc.rearrange("p (c l) -> p c l", l=L),
                                                  in0=gg.rearrange("p (c l) -> p c l", l=L), in1=Mcb, op=ALU.subtract),
                 reads=[b_g], writes=[b_g])
            T.op("act", lambda e: e.activation(out=esc, in_=esc, func=AF.Exp), reads=[b_g], writes=[b_g])
            T.op("dve", lambda e: e.tensor_tensor(out=thr.rearrange("p (c l) -> p c l", l=L),
                                                  in0=cs.rearrange("p (c l) -> p c l", l=L), in1=Mcb, op=ALU.subtract),
                 reads=[b_g], writes=[b_g])
            T.op("act", lambda e: e.activation(out=thr, in_=thr, func=AF.Exp), reads=[b_g], writes=[b_g])
            bkE = 6
            psE = PB[bkE][:, 0:nch * 8].rearrange("p (c j) -> p c j", j=8)
            for c in range(nch):
                for (srcv, c0) in ((esc, 0), (thr, 4)):
                    T.op("pe", lambda e, c=c, srcv=srcv, c0=c0: e.transpose(
                        out=psE[0:L, c, c0:c0 + 4], in_=srcv[:, c * L:(c + 1) * L], identity=ident_f[0:4, 0:4]),
                        reads=[b_g, b_const], writes=[b_PB[bkE]], signal=(c == nch - 1 and c0 == 4))
            T.op("dve", lambda e: e.tensor_copy(out=escT[0:L], in_=psE[0:L]), reads=[b_PB[bkE]], writes=[b_escT])
            T.op("dve", lambda e: e.tensor_tensor(
                out=decd, in0=dec.unsqueeze(1).to_broadcast([4, 4, nch]),
                in1=ident_f[0:4, 0:4].unsqueeze(2).to_broadcast([4, 4, nch]), op=ALU.mult),
                reads=[b_g, b_const], writes=[b_g])
            bkD = 7
            T.op("pe", lambda e: e.matmul(PB[bkD][:, 0:4 * nch], lhsT=ones_f[0:4, :],
                                          rhs=decd.rearrange("p a c -> p (a c)"), start=True, stop=True),
                 reads=[b_g, b_const], writes=[b_PB[bkD]])
            T.op("dve", lambda e: e.tensor_copy(out=decbc.rearrange("p a c -> p (a c)"), in_=PB[bkD][:, 0:4 * nch]),
                 reads=[b_PB[bkD]], writes=[b_decbc])
            if fold_wo[0]:
                fold_wo[0] = False
                for kc in range(8):
                    T.op("dve", lambda e, kc=kc: e.tensor_tensor(out=wo_bf[:, kc, :], in0=wo_bf[:, kc, :], in1=gate_bc,
                                                                 op=ALU.mult), reads=[b_wo, b_gbc], writes=[b_wo])
            T.deferring = False
            A1_END = A.off

            for g in range(2):
                gate()
                if g == 0:
                    A.off = A1_END
                    qT = A.alloc(4 * Tn, BF16).rearrange("p (i t) -> p i t", i=4)
                    kT = A.alloc(4 * Tn, BF16).rearrange("p (i t) -> p i t", i=4)
                    vT = A.alloc(4 * Tn, BF16).rearrange("p (i t) -> p i t", i=4)
                    gaT = A.alloc(4 * Tn, BF16).rearrange("p (i t) -> p i t", i=4)
                    szt = [A.alloc(nn, F32) for _ in range(2)]
                    tzt = [A.alloc(nn, F32) for _ in range(2)]
                    b_szt, b_tzt = bufs(2), bufs(2)
                    b_q = [bufs(nnt) for _ in range(4)]
                    b_k = [bufs(nnt) for _ in range(4)]
                    b_v = [bufs(nnt) for _ in range(4)]
                    b_ga = [bufs(nnt) for _ in range(4)]
                ecnt = [0]

                def ev_copy(dstT, bdst, i):
                    def f(bk, nt):
                        ecnt[0] += 1
                        dst = dstT[:, i, nt * nn:(nt + 1) * nn]
                        if ecnt[0] % 2 == 0:
                            T.op("act", lambda e: e.activation(out=dst, in_=PB[bk][:, 0:nn], func=AF.Copy),
                                 reads=[b_PB[bk]], writes=[bdst[i][nt]])
                        else:
                            T.op("dve", lambda e: e.tensor_copy(out=dst, in_=PB[bk][:, 0:nn]),
                                 reads=[b_PB[bk]], writes=[bdst[i][nt]])
                    return f

                def ev_k(i):
                    def f(bk, nt):
                        dst = kT[:, i, nt * nn:(nt + 1) * nn]
                        T.op("dve", lambda e: e.tensor_scalar(out=dst, in0=PB[bk][:, 0:nn], scalar1=DH ** -0.5,
                                                              scalar2=None, op0=ALU.mult),
                             reads=[b_PB[bk]], writes=[b_k[i][nt]])
                    return f

                def ev_o(i):
                    def f(bk, nt):
                        dst = gaT[:, i, nt * nn:(nt + 1) * nn]
                        T.op("act", lambda e: e.activation(out=dst, in_=PB[bk][:, 0:nn], func=AF.Sigmoid),
                             reads=[b_PB[bk]], writes=[b_ga[i][nt]])
                    return f

                def ev_z(i):
                    def f(bk, nt):
                        ecnt[0] += 1
                        p = ecnt[0] % 2
                        dst = gaT[:, i, nt * nn:(nt + 1) * nn]
                        fcg = g * 4 + i
                        T.op("act", lambda e: e.activation(out=szt[p], in_=PB[bk][:, 0:nn], func=AF.Sigmoid),
                             reads=[b_PB[bk]], writes=[b_szt[p]])
                        T.op("dve", lambda e: e.scalar_tensor_tensor(
                            out=tzt[p], in0=PB[bk][:, 0:nn], scalar=pv[:, PV_GH + fcg:PV_GH + fcg + 1], in1=szt[p],
                            op0=ALU.mult, op1=ALU.mult), reads=[b_PB[bk], b_szt[p], b_const], writes=[b_tzt[p]])
                        T.op("dve", lambda e: e.tensor_tensor(out=dst, in0=tzt[p], in1=dst, op=ALU.mult),
                             reads=[b_tzt[p], b_ga[i][nt]], writes=[b_ga[i][nt]])
                    return f

                if g == 0:
                    bank_pool[0] = [0, 1, 2, 3, 4, 5]
                    nfl = -(-len(T.deferred) // 18)
                for (kind, evf) in (("q", lambda i: ev_copy(qT, b_q, i)), ("k", ev_k),
                                    ("v", lambda i: ev_copy(vT, b_v, i)), ("o", ev_o), ("z", ev_z)):
                    wv, bw = ring_next()
                    for i in range(4):
                        proj(wv, bw, i, uT, uT_bufs, Tn, nn, evf(i))
                        if g == 0:
                            T.flush(nfl)
                if g == 0:
                    T.flush()
                    bank_pool[0] = list(range(8))

                gate()
                NP3 = 3
                if g == 0:
                    ktok = [A.alloc(256, BF16) for _ in range(NP3)]
                    vpp = [A.alloc(258, BF16) for _ in range(NP3)]
                    SmT = [A.alloc(128, BF16) for _ in range(NP3)]
                    hn = [A.alloc(512, BF16) for _ in range(2)]
                    st3 = [A.alloc(8, F32) for _ in range(NP3)]
                    junk3 = A.alloc(256, BF16)
                    b_ktok, b_vpp, b_SmT, b_hn, b_st3 = bufs(NP3), bufs(NP3), bufs(NP3), bufs(2), bufs(NP3)
                    b_junk3 = Buf()
                iters = [(c, hh) for c in range(nch) for hh in range(2)]

                def views(it):
                    c, hh = iters[it]
                    p = it % NP3
                    bankT = PB[p][:, 0:256].bitcast(BF16)
                    return dict(c=c, hh=hh, h=2 * g + hh, p=p, tsl=slice(c * L, (c + 1) * L), ntc=(c * L) // nn,
                                i0=hh * 2, pk=bankT[:, 0:256], pvv=bankT[:, 256:512], psS=PB[p][:, 256:384],
                                pdn=PB[p][:, 384:386], psN=PB[3 + p][:, 0:257],
                                psD=PB[6][:, :].rearrange("p (c v) -> p c v", c=2),
                                bT=b_PB[p], bN=b_PB[3 + p], bD=b_PB[6])

                def pe1(it):
                    V = views(it)
                    i0, tsl, ntc, p = V["i0"], V["tsl"], V["ntc"], V["p"]
                    for dc in range(2):
                        T.op("pe", lambda e, dc=dc: e.transpose(out=V["pk"][0:L, dc * 128:(dc + 1) * 128],
                                                                in_=kT[:, i0 + dc, tsl], identity=ident_bf),
                             reads=[b_k[i0 + dc][ntc], b_const], writes=[V["bT"]], signal=False)
                    for dc in range(2):
                        T.op("pe", lambda e, dc=dc: e.transpose(out=V["pvv"][0:L, dc * 128:(dc + 1) * 128],
                                                                in_=vT[:, i0 + dc, tsl], identity=ident_bf),
                             reads=[b_v[i0 + dc][ntc], b_const], writes=[V["bT"]], signal=False)
                    for dc in range(2):
                        T.op("pe", lambda e, dc=dc: e.matmul(V["psS"][0:L, 0:L], lhsT=kT[:, i0 + dc, tsl],
                                                             rhs=qT[:, i0 + dc, tsl], start=(dc == 0), stop=(dc == 1)),
                             reads=[b_k[i0 + dc][ntc], b_q[i0 + dc][ntc]], writes=[V["bT"]], signal=(dc == 1))

                def ev1(it):
                    V = views(it)
                    c, h, p = V["c"], V["h"], V["p"]
                    T.op("dve", lambda e: e.tensor_copy(out=ktok[p][0:L, :], in_=V["pk"][0:L, :]),
                         reads=[V["bT"]], writes=[b_ktok[p]])
                    T.op("dve", lambda e: e.tensor_scalar(out=vpp[p][0:L, 0:256], in0=V["pvv"][0:L, :],
                                                          scalar1=escT[0:L, c, h:h + 1], scalar2=None, op0=ALU.mult),
                         reads=[V["bT"], b_escT], writes=[b_vpp[p]])
                    T.op("dve", lambda e: e.tensor_copy(out=vpp[p][0:L, 256:257], in_=escT[0:L, c, h:h + 1]),
                         reads=[b_escT, b_vpp[p]], writes=[b_vpp[p]])
                    T.op("dve", lambda e: e.tensor_tensor(out=SmT[p][0:L, 0:L], in0=V["psS"][0:L, 0:L],
                                                          in1=maskT[0:L, 0:L], op=ALU.mult),
                         reads=[V["bT"], b_const], writes=[b_SmT[p]])

                def cbf(it):
                    V = views(it)
                    c, h = V["c"], V["h"]
                    T.op("act", lambda e: e.activation(out=Cbf[:, h, :, 0:257], in_=Cst[:, h], func=AF.Identity,
                                                       scale=decbc[:, h, c:c + 1]),
                         reads=[b_C[h], b_decbc], writes=[b_Cbf[h]])

                def pe2(it):
                    V = views(it)
                    c, h, p, i0, tsl, ntc = V["c"], V["h"], V["p"], V["i0"], V["tsl"], V["ntc"]
                    psN, psD, pdn = V["psN"], V["psD"], V["pdn"]
                    for dc in range(2):
                        T.op("pe", lambda e, dc=dc: e.matmul(psN[0:L, :], lhsT=qT[:, i0 + dc, tsl],
                                                             rhs=Cbf[:, h, dc, 0:257], start=(dc == 0), stop=False),
                             reads=[b_q[i0 + dc][ntc], b_Cbf[h]], writes=[V["bN"]], signal=False)
                    T.op("pe", lambda e: e.matmul(psN[0:L, :], lhsT=SmT[p][0:L, 0:L], rhs=vpp[p][0:L, 0:257],
                                                  start=False, stop=True),
                         reads=[b_SmT[p], b_vpp[p]], writes=[V["bN"]])
                    for dc in range(2):
                        T.op("pe", lambda e, dc=dc: e.matmul(psD[:, dc, :], lhsT=ktok[p][0:L, dc * 128:(dc + 1) * 128],
                                                             rhs=vpp[p][0:L, 0:256], start=True, stop=True),
                             reads=[b_ktok[p], b_vpp[p]], writes=[V["bD"]], signal=(dc == 1))
                    for dc in range(2):
                        T.op("pe", lambda e, dc=dc: e.matmul(pdn[:, dc:dc + 1], lhsT=ktok[p][0:L, dc * 128:(dc + 1) * 128],
                                                             rhs=vpp[p][0:L, 256:257], start=True, stop=True),
                             reads=[b_ktok[p], b_vpp[p]], writes=[V["bT"]], signal=(dc == 1))

                def upd(it):
                    V = views(it)
                    c, h = V["c"], V["h"]
                    T.op("dve", lambda e: e.scalar_tensor_tensor(
                        out=Cst[:, h, :, 0:256], in0=Cst[:, h, :, 0:256], scalar=decbc[:, h, c:c + 1], in1=V["psD"],
                        op0=ALU.mult, op1=ALU.add), reads=[b_C[h], b_decbc, V["bD"]], writes=[b_C[h]])
                    T.op("dve", lambda e: e.scalar_tensor_tensor(
                        out=Cst[:, h, :, 256], in0=Cst[:, h, :, 256], scalar=decbc[:, h, c:c + 1], in1=V["pdn"],
                        op0=ALU.mult, op1=ALU.add), reads=[b_C[h], b_decbc, V["bT"]], writes=[b_C[h]])

                def normA(it):
                    V = views(it)
                    p, sv = V["p"], st3[V["p"]]
                    T.op("act", lambda e: e.activation(out=sv[0:L, 5:6], in_=V["psN"][0:L, 256:257], func=AF.Abs),
                         reads=[V["bN"], b_st3[p]], writes=[b_st3[p]])

                def normB(it):
                    V = views(it)
                    c, h, p, sv = V["c"], V["h"], V["p"], st3[V["p"]]
                    T.op("dve", lambda e: e.tensor_tensor(out=sv[0:L, 1:2], in0=sv[0:L, 5:6],
                                                          in1=escT[0:L, c, 4 + h:5 + h], op=ALU.max),
                         reads=[b_escT, b_st3[p]], writes=[b_st3[p]])
                    T.op("dve", lambda e: e.reciprocal(out=sv[0:L, 2:3], in_=sv[0:L, 1:2]),
                         reads=[b_st3[p]], writes=[b_st3[p]])

                def normC(it):
                    V = views(it)
                    p, sv = V["p"], st3[V["p"]]
                    T.op("act", lambda e: e.activation(out=junk3[0:L, :], in_=V["psN"][0:L, 0:256], func=AF.Square,
                                                       scale=sv[0:L, 2:3], accum_out=sv[0:L, 0:1]),
                         reads=[V["bN"], b_st3[p]], writes=[b_junk3, b_st3[p]])
                    T.op("act", lambda e: e.activation(out=sv[0:L, 3:4], in_=sv[0:L, 0:1], func=AF.Ln,
                                                       scale=1.0 / DH, bias=EPS),
                         reads=[b_st3[p]], writes=[b_st3[p]])
                    T.op("act", lambda e: e.activation(out=sv[0:L, 6:7], in_=sv[0:L, 3:4], func=AF.Exp, scale=-0.5),
                         reads=[b_st3[p]], writes=[b_st3[p]])

                def normD(it):
                    V = views(it)
                    p, sv = V["p"], st3[V["p"]]
                    T.op("dve", lambda e: e.tensor_tensor(out=sv[0:L, 4:5], in0=sv[0:L, 6:7], in1=sv[0:L, 2:3],
                                                          op=ALU.mult),
                         reads=[b_st3[p]], writes=[b_st3[p]])

                def normE(it):
                    V = views(it)
                    c, hh, p, sv = V["c"], V["hh"], V["p"], st3[V["p"]]
                    hb = (c % 2)
                    T.op("act", lambda e: e.activation(out=hn[hb][0:L, hh * 256:(hh + 1) * 256], in_=V["psN"][0:L, 0:256],
                                                       func=AF.Identity, scale=sv[0:L, 4:5]),
                         reads=[V["bN"], b_st3[p]], writes=[b_hn[hb]])

                def chunk_end(c):
                    tsl = slice(c * L, (c + 1) * L)
                    ntc = (c * L) // nn
                    hb = (c % 2)
                    psH = PB[7][:, 0:256].bitcast(BF16).rearrange("p (i t) -> p i t", i=4)
                    for i in range(4):
                        T.op("pe", lambda e, i=i: e.transpose(out=psH[:, i, 0:L], in_=hn[hb][0:L, i * 128:(i + 1) * 128],
                                                              identity=ident_bf[0:L, 0:L]),
                             reads=[b_hn[hb], b_const], writes=[b_PB[7]], signal=(i == 3))
                    T.op("dve", lambda e: e.tensor_tensor(out=yaT[:, g * 4:(g + 1) * 4, tsl], in0=psH[:, :, 0:L],
                                                          in1=gaT[:, :, tsl], op=ALU.mult),
                         reads=[b_PB[7]] + [b_ga[i][ntc] for i in range(4)],
                         writes=[b_yaT[g * 4 + i][c] for i in range(4)])

                NI = len(iters)

                def okk(t):
                    return 0 <= t < NI

                pe1(0)
                ev1(0)
                for t in range(NI + 3):
                    if okk(t - 3) and iters[t - 3][1] == 1:
                        chunk_end(iters[t - 3][0])
                    if okk(t):
                        cbf(t)
                    if okk(t + 1):
                        pe1(t + 1)
                    if okk(t - 1):
                        normB(t - 1)
                    if okk(t - 2):
                        normD(t - 2)
                    if okk(t + 1):
                        ev1(t + 1)
                    if okk(t):
                        pe2(t)
                        upd(t)
                    if okk(t - 1):
                        normC(t - 1)
                    if okk(t - 2):
                        normE(t - 2)
                    if okk(t):
                        normA(t)
            T.barrier()
            A.off = PERSIST_END

            gate()
            s1 = [A.alloc(nn, F32) for _ in range(2)]
            s2 = [A.alloc(nn, F32) for _ in range(2)]
            depth = dict(xbp=2, szb=8, xc=5, rr=5, ii=5, mu=3)
            pools = {}
            for kname, dep in depth.items():
                n_el = nn + 4 if kname == "xbp" else nn
                dt_ = BF16 if kname == "xcb" else F32
                pools[kname] = [(A.alloc(n_el, dt_), Buf()) for _ in range(dep)]

            def PBf(kname, u):
                return pools[kname][u % depth[kname]]

            units = [(fc, nt) for fc in range(8) for nt in range(nnt)]
            NU = len(units)
            slotw = {}
            zbank = {}

            def pe_proj(u):
                fc, nt = units[u]
                if fc % 2 == 0 and nt == 0:
                    slotw["cur"] = ring_next(dual=True)
                wv, bw = slotw["cur"]
                bks = []
                for kind in range(2):
                    bk = 2 * kind + (u % 2)
                    bks.append(bk)
                    for kc in range(8):
                        T.op("pe", lambda e, kc=kc: e.matmul(
                            PB[bk][:, 0:nn], lhsT=wv[:, (fc % 2) * 2 + kind, kc, :], rhs=uT[:, kc, nt * nn:(nt + 1) * nn],
                            start=(kc == 0), stop=(kc == 7)),
                            reads=[bw] + uT_bufs(kc, nt), writes=[b_PB[bk]], signal=(kc == 7))
                zbank[u] = bks

            def act_p(u):
                xbp, b_xbp = PBf("xbp", u)
                szb, b_szb = PBf("szb", u)
                bx, bz = zbank[u]
                T.op("act", lambda e: e.activation(out=xbp[:, 3:3 + nn], in_=PB[bx][:, 0:nn], func=AF.Copy),
                     reads=[b_PB[bx]], writes=[b_xbp])
                T.op("act", lambda e: e.activation(out=szb, in_=PB[bz][:, 0:nn], func=AF.Sigmoid),
                     reads=[b_PB[bz]], writes=[b_szb])

            def dve_silu(u):
                szb, b_szb = PBf("szb", u)
                bz = zbank[u][1]
                T.op("dve", lambda e: e.tensor_tensor(out=szb, in0=PB[bz][:, 0:nn], in1=szb, op=ALU.mult),
                     reads=[b_PB[bz], b_szb], writes=[b_szb])

            def dve_conv(u):
                fc, nt = units[u]
                xbp, b_xbp = PBf("xbp", u)
                xc, b_xc = PBf("xc", u)
                cw = lambda j: pv[:, PV_CW + j * 8 + fc: PV_CW + j * 8 + fc + 1]
                ce = "pool"
                T.op(ce, lambda e: e.tensor_copy(out=xbp[:, 0:3], in_=convcar[:, fc, :]),
                     reads=[b_ccar[fc]], writes=[b_xbp])
                T.op(ce, lambda e: e.tensor_copy(out=convcar[:, fc, :], in_=xbp[:, nn:nn + 3]),
                     reads=[b_xbp], writes=[b_ccar[fc]])
                if OPT_B & 2:
                    T.op("act", lambda e: e.activation(out=xc, in_=xbp[:, 0:nn], func=AF.Identity, scale=cw(0),
                                                       bias=pv[:, PV_CB + fc:PV_CB + fc + 1]),
                         reads=[b_xbp, b_const], writes=[b_xc])
                else:
                    T.op("dve", lambda e: e.scalar_tensor_tensor(
                        out=xc, in0=xbp[:, 0:nn], scalar=cw(0), in1=pv[:, PV_CB + fc:PV_CB + fc + 1].to_broadcast([128, nn]),
                        op0=ALU.mult, op1=ALU.add), reads=[b_xbp, b_const], writes=[b_xc])
                for j in range(1, 4):
                    T.op("dve", lambda e, j=j: e.scalar_tensor_tensor(out=xc, in0=xbp[:, j:j + nn], scalar=cw(j),
                                                                      in1=xc, op0=ALU.mult, op1=ALU.add),
                         reads=[b_xbp, b_xc, b_const], writes=[b_xc])

            def pool_cast(u):
                return

            ribank = {}

            def pe_ri(u):
                fc, nt = units[u]
                xcb, b_xcb = PBf("xc", u)
                br_, bi_ = 4, 5
                ribank[u] = (br_, bi_)
                T.op("pe", lambda e: e.matmul(PB[br_][:, 0:nn], lhsT=wr_bf[:, 0, fc, :], rhs=xcb,
                                              start=True, stop=True), reads=[b_xcb, b_wc], writes=[b_PB[br_]])
                T.op("pe", lambda e: e.matmul(PB[bi_][:, 0:nn], lhsT=wr_bf[:, 1, fc, :], rhs=xcb,
                                              start=True, stop=True), reads=[b_xcb, b_wc], writes=[b_PB[bi_]])

            def act_ri(u):
                fc, nt = units[u]
                rr, b_rr = PBf("rr", u)
                ii, b_ii = PBf("ii", u)
                br_, bi_ = ribank[u]
                T.op("act", lambda e: e.activation(out=rr, in_=PB[br_][:, 0:nn], func=AF.Sigmoid,
                                                   bias=pv[:, PV_BRA + fc:PV_BRA + fc + 1]),
                     reads=[b_PB[br_], b_const], writes=[b_rr])
                T.op("act", lambda e: e.activation(out=ii, in_=PB[bi_][:, 0:nn], func=AF.Sigmoid,
                                                   bias=pv[:, PV_BRX + fc:PV_BRX + fc + 1]),
                     reads=[b_PB[bi_], b_const], writes=[b_ii])

            def act_exp(u):
                fc, nt = units[u]
                rr, b_rr = PBf("rr", u)
                mu, b_mu = PBf("mu", u)
                T.op("act", lambda e: e.activation(out=rr, in_=rr, func=AF.Exp, scale=cfeat[:, fc:fc + 1]),
                     reads=[b_rr, b_const], writes=[b_rr])
                if OPT_SQ:
                    T.op("pool", lambda e: e.tensor_tensor(out=mu, in0=rr, in1=rr, op=ALU.mult), reads=[b_rr], writes=[b_mu])
                else:
                    T.op("act", lambda e: e.activation(out=mu, in_=rr, func=AF.Square), reads=[b_rr], writes=[b_mu])
                T.op("act", lambda e: e.activation(out=mu, in_=mu, func=AF.Ln, scale=-1.0, bias=1.0),
                     reads=[b_mu], writes=[b_mu])
                T.op("act", lambda e: e.activation(out=mu, in_=mu, func=AF.Exp, scale=0.5), reads=[b_mu], writes=[b_mu])

            def pool_mul(u):
                ii, b_ii = PBf("ii", u)
                xc, b_xc = PBf("xc", u)
                T.op("pool", lambda e: e.tensor_tensor(out=ii, in0=ii, in1=xc, op=ALU.mult),
                     reads=[b_ii, b_xc], writes=[b_ii])

            def dve_r(u):
                fc, nt = units[u]
                rr, b_rr = PBf("rr", u)
                ii, b_ii = PBf("ii", u)
                mu, b_mu = PBf("mu", u)
                if st["prompt"] and st["first"] and nt == 0:
                    T.op("dve", lambda e: e.memset(mu[:, 0:1], 1.0), reads=[b_mu], writes=[b_mu])
                T.op("dve", lambda e: e.tensor_tensor(out=ii, in0=ii, in1=mu, op=ALU.mult),
                     reads=[b_ii, b_mu], writes=[b_ii])
                T.op("dve", lambda e: e.tensor_tensor_scan(out=mu, data0=rr, data1=ii,
                                                           initial=hcar[:, fc:fc + 1], op0=ALU.mult, op1=ALU.add),
                     reads=[b_rr, b_ii, b_hcar[fc], b_mu], writes=[b_mu])
                T.op("pool", lambda e: e.tensor_copy(out=hcar[:, fc:fc + 1], in_=mu[:, nn - 1:nn]),
                     reads=[b_mu], writes=[b_hcar[fc]])

            def pool_y(u):
                fc, nt = units[u]
                szb, b_szb = PBf("szb", u)
                mu, b_mu = PBf("mu", u)
                T.op("pool", lambda e: e.tensor_tensor(out=ybT[:, fc, nt * nn:(nt + 1) * nn], in0=szb, in1=mu,
                                                       op=ALU.mult),
                     reads=[b_szb, b_mu], writes=[b_ybT[fc][nt]])

            def ok(u):
                return 0 <= u < NU

            b_s1, b_s2 = bufs(2), bufs(2)
            slotc = {}

            def cprime(gi):
                m, nt = divmod(gi, nnt)
                if m % 2 == 0 and nt == 0:
                    slotc["cur"] = ring_next(dual=True)
                wv, bw = slotc["cur"]
                base = (m % 2) * 2
                sl = slice(nt * nn, (nt + 1) * nn)
                p = gi % 2
                for (ci, srcT, rb, bk) in ((base, yaT, lambda kc: [b_yaT[kc][nt * tpn + i] for i in range(tpn)], 6),
                                           (base + 1, uT, lambda kc: uT_bufs(kc, nt), 7)):
                    for kc in range(8):
                        T.op("pe", lambda e, ci=ci, kc=kc, srcT=srcT, bk=bk: e.matmul(
                            PB[bk][:, 0:nn], lhsT=wv[:, ci, kc, :], rhs=srcT[:, kc, sl],
                            start=(kc == 0), stop=(kc == 7)),
                            reads=[bw] + rb(kc), writes=[b_PB[bk]], signal=(kc == 7))
                T.op("act", lambda e: e.activation(out=s1[p], in_=PB[7][:, 0:nn], func=AF.Sigmoid,
                                                   bias=pv[:, PV_BG + m:PV_BG + m + 1]),
                     reads=[b_PB[7], b_const], writes=[b_s1[p]])
                T.op("dve", lambda e: e.tensor_tensor(out=mgT[:, m, sl], in0=PB[6][:, 0:nn], in1=s1[p], op=ALU.mult),
                     reads=[b_PB[6], b_s1[p]], writes=[b_mgT[m][nt]])

            NG = 8 * nnt
            for s_ in range(NU + 8):
                if ok(s_ - 4):
                    act_ri(s_ - 4)
                if ok(s_):
                    pe_proj(s_)
                if ok(s_ - 3):
                    pe_ri(s_ - 3)
                if s_ % 2 == 1:
                    for u in (s_ - 6, s_ - 5):
                        if ok(u):
                            act_exp(u)
                if ok(s_ - 1):
                    act_p(s_ - 1)
                if s_ % 2 == 0:
                    for u in (s_ - 7, s_ - 6):
                        if ok(u):
                            dve_r(u)
                if ok(s_ - 2):
                    dve_conv(s_ - 2)
                if ok(s_ - 1):
                    dve_silu(s_ - 1)
                if ok(s_ - 2):
                    pool_cast(s_ - 2)
                if ok(s_ - 5):
                    pool_mul(s_ - 5)
                if s_ % 2 == 0:
                    for u in (s_ - 7, s_ - 6):
                        if ok(u):
                            pool_y(u)
                if s_ < NG:
                    cprime(s_)

            gate()
            cc = 0
            for m in range(8):
                if m % 2 == 0:
                    wv, bw = ring_next()
                base = (m % 2) * 2
                for nt in range(nnt):
                    sl = slice(nt * nn, (nt + 1) * nn)
                    p = cc % 2
                    cc += 1
                    bkb, bkg = next_bank(), next_bank()
                    for (ci, srcT, rb, bk) in ((base, ybT, lambda kc: [b_ybT[kc][0], b_ybT[kc][1]], bkb),
                                               (base + 1, uT, lambda kc: uT_bufs(kc, nt), bkg)):
                        for kc in range(8):
                            T.op("pe", lambda e, ci=ci, kc=kc, srcT=srcT, bk=bk: e.matmul(
                                PB[bk][:, 0:nn], lhsT=wv[:, ci, kc, :], rhs=srcT[:, kc, sl],
                                start=(kc == 0), stop=(kc == 7)),
                                reads=[bw] + rb(kc), writes=[b_PB[bk]], signal=(kc == 7))
                    T.op("act", lambda e: e.activation(out=s2[p], in_=PB[bkg][:, 0:nn], func=AF.Sigmoid,
                                                       bias=pv[:, PV_BG + 8 + m:PV_BG + 8 + m + 1]),
                         reads=[b_PB[bkg], b_const], writes=[b_s2[p]])
                    T.op("dve", lambda e: e.tensor_tensor(out=s2[p], in0=PB[bkb][:, 0:nn], in1=s2[p], op=ALU.mult),
                         reads=[b_PB[bkb], b_s2[p]], writes=[b_s2[p]])
                    T.op("pool", lambda e: e.tensor_tensor(out=mgT[:, m, sl], in0=mgT[:, m, sl], in1=s2[p], op=ALU.add),
                         reads=[b_s2[p], b_mgT[m][nt]], writes=[b_mgT[m][nt]])
            T.barrier()
            A.off = PERSIST_END
            if DBG_DUMP and st["tok0"] == 0:
                dtmp = A.alloc(8 * TT, F32)
                b_dtmp = Buf()
                for di, srcT in enumerate((yaT, ybT, mgT)):
                    T.op("dve", lambda e, srcT=srcT: e.tensor_copy(out=dtmp, in_=srcT.rearrange("p f t -> p (f t)")),
                         writes=[b_dtmp])
                    T.dma("sp", dbg_d[di], dtmp, [b_dtmp], [], "dbgo")
                T.barrier()
                A.off = PERSIST_END

            gate()
            ZD = pd_alloc(2 if st["prompt"] else 1)
            nxt = sts[idx + 1] if idx + 1 < len(sts) else None
            nD = Tn // tp
            nP = (nxt["T"] // nxt["tp"]) if nxt is not None else 0
            Z0 = p0_alloc() if nxt is not None else None
            if nxt is not None and nxt["prompt"]:
                ring_issue_upto(min(total_loads[0] + NS - 1, NLOADS_TOTAL - 1))
            pd_tile(st, 0, ZD, load_only=True)
            if nP > 0:
                p0_tile(nxt, 0, Z0, load_only=True)
            for j in range(max(nD, nP)):
                if j + 1 < nD:
                    pd_tile(st, j + 1, ZD, load_only=True)
                if j + 1 < nP:
                    p0_tile(nxt, j + 1, Z0, load_only=True)
                hd, hp = j < nD, j < nP
                if hd:
                    pd_tile(st, j, ZD, stages=("pe",))
                if hp:
                    p0_tile(nxt, j, Z0, stages=("stats", "xn", "tr"))
                if hd:
                    pd_tile(st, j, ZD, stages=("mul", "add"))
                if hp:
                    p0_tile(nxt, j, Z0, stages=("ev",))
                if hd:
                    pd_tile(st, j, ZD, stages=("stats", "fin"))
            if st["last"]:
                out_toks["oCT%d" % s] = T.dma("sp", oCT_d[s], Cst.rearrange("p h c v -> p (h c v)"), b_C, [], "o_CT")
                out_toks["om%d" % s] = T.dma("sp", om_d[s], mstate[0:4, :], [b_ms], [], "o_m")
                out_toks["oh%d" % s] = T.dma("sp", oh_d[s], hcar, b_hcar, [], "o_h")
                out_toks["oc%d" % s] = T.dma("sp", oconv_d[s], convcar.rearrange("p f j -> p (f j)"), b_ccar, [], "o_c")
            T.barrier()
            A.off = PERSIST_END

        try:
            for i_ in range(len(sts)):
                run_st(i_)
        except StopBuild:
            T.pending = []

        T.barrier()
        T.barrier(engines=["sp"])
    return nc


_CACHE = {}


def _f32(a):
    return np.ascontiguousarray(np.asarray(a, dtype=np.float32))


def kernel(x_prompt, x_sample, c_prompt, c_sample, state_C, state_n, state_m, state_h, state_conv,
           w_mod, b_mod, g_norm, w_in, b_if, g_head, conv_w, conv_b, w_ra, b_ra, w_rx, b_rx, lam,
           w_gate, b_gate, w_out_a, w_out_b, w_o, g_final):
    n = 8
    x_prompt, x_sample = _f32(x_prompt), _f32(x_sample)
    c_prompt, c_sample = _f32(c_prompt), _f32(c_sample)
    w_in0 = _f32(w_in)[0]
    def fm(v, nchunk):
        return np.ascontiguousarray(_f32(v).reshape(nchunk, 128).T)
    pv = np.zeros((128, NPV), np.float32)
    pv[:, PV_BMOD:PV_BMOD + 24] = fm(b_mod[0], 24)
    pv[:, PV_GN:PV_GN + 8] = fm(g_norm[0], 8)
    pv[:, PV_GH:PV_GH + 8] = fm(g_head[0], 8)
    for j in range(4):
        pv[:, PV_CW + j * 8:PV_CW + j * 8 + 8] = fm(_f32(conv_w)[0, j], 8)
    pv[:, PV_CB:PV_CB + 8] = fm(conv_b[0], 8)
    pv[:, PV_BRA:PV_BRA + 8] = fm(b_ra[0], 8)
    pv[:, PV_BRX:PV_BRX + 8] = fm(b_rx[0], 8)
    pv[:, PV_LAM:PV_LAM + 8] = fm(lam[0], 8)
    pv[:, PV_BG:PV_BG + 16] = fm(b_gate[0], 16)
    bif = np.ascontiguousarray(_f32(b_if)[0].reshape(2, 4).T)
    gfin = _f32(g_final).reshape(1, D)
    wmod = np.ascontiguousarray(_f32(w_mod)[0].reshape(8, 128, 3 * D).transpose(1, 0, 2))

    def chunk(W, col0):
        return W[:, col0:col0 + 128].reshape(8, 128, 128).transpose(1, 0, 2).reshape(128, 1024)
    base = {"q": 0, "k": 1024, "v": 2048, "o": 3072, "z": 4096}
    chunks = []
    for g in range(2):
        for name in ("q", "k", "v", "o", "z"):
            for hh in range(2):
                for dc in range(2):
                    chunks.append(chunk(w_in0, base[name] + (2 * g + hh) * 256 + dc * 128))
    wg0, wa0, wb0 = _f32(w_gate)[0], _f32(w_out_a)[0], _f32(w_out_b)[0]
    for k in range(4):
        for fc in (2 * k, 2 * k + 1):
            chunks.append(chunk(w_in0, 5128 + fc * 128))
            chunks.append(chunk(w_in0, 6152 + fc * 128))
        for m in (2 * k, 2 * k + 1):
            chunks.append(chunk(wa0, m * 128))
            chunks.append(chunk(wg0, m * 128))
    for k in range(4):
        for m in (2 * k, 2 * k + 1):
            chunks.append(chunk(wb0, m * 128))
            chunks.append(chunk(wg0, 1024 + m * 128))
    wst = np.ascontiguousarray(np.stack(chunks, 0))
    assert wst.shape == (NCH_W, 128, 1024)
    wif = np.ascontiguousarray(w_in0[:, 5120:5128].reshape(8, 128, 8).transpose(1, 0, 2).reshape(128, 64))
    wr = np.ascontiguousarray(np.stack([_f32(w_ra)[0].transpose(1, 0, 2), _f32(w_rx)[0].transpose(1, 0, 2)], 1)
                              .reshape(128, 2048))
    wo = np.ascontiguousarray(_f32(w_o)[0].reshape(8, 128, D).transpose(1, 0, 2).reshape(128, 8 * D))
    sC, sn, sm_, sh_, sc_ = _f32(state_C)[0], _f32(state_n)[0], _f32(state_m)[0], _f32(state_h)[0], _f32(state_conv)[0]

    in_maps = []
    for c in range(n):
        xs = np.concatenate([x_prompt[2 * c].reshape(SEQ, D), x_prompt[2 * c + 1].reshape(SEQ, D),
                             x_sample[c].reshape(DEC, D)], 0)
        cs_ = np.stack([c_prompt[2 * c], c_prompt[2 * c + 1], c_sample[c]], 0)
        cT = np.ascontiguousarray(cs_.reshape(3, 8, 128).transpose(2, 1, 0))
        CT = sC[c].reshape(4, 256, 2, 128).transpose(3, 0, 2, 1)
        nT = sn[c].reshape(4, 2, 128).transpose(2, 0, 1)[..., None]
        sCT = np.ascontiguousarray(np.concatenate([CT, nT], -1).reshape(128, 4 * 2 * 257))
        in_maps.append({
            "x": np.ascontiguousarray(xs), "cT": cT, "wmod": wmod, "pv": pv, "bif": bif, "gfin": gfin,
            "wst": wst, "wif": wif, "wr": wr, "wo": wo, "sCT": sCT,
            "sm": np.ascontiguousarray(sm_[c].reshape(4, 1)),
            "sh": np.ascontiguousarray(sh_[c].reshape(8, 128).T),
            "sconv": np.ascontiguousarray(sc_[c].reshape(3, 8, 128).transpose(2, 1, 0).reshape(128, 24)),
        })
    if "nc" not in _CACHE:
        _CACHE["nc"] = build_program()
    res = run_bass_kernel_spmd(_CACHE["nc"], in_maps, core_ids=list(range(n)))
    R = res.results
    y_p = np.zeros((16, SEQ, D), np.float32)
    y_s = np.zeros((8, DEC, D), np.float32)
    Cs = np.zeros((24, 4, 256, 256), np.float32)
    ns = np.zeros((24, 4, 256), np.float32)
    ms = np.zeros((24, 4), np.float32)
    hs = np.zeros((24, D), np.float32)
    cvs = np.zeros((24, 3, D), np.float32)
    for c in range(n):
        r = R[c]
        y_p[2 * c] = r["y"][0:SEQ]
        y_p[2 * c + 1] = r["y"][SEQ:2 * SEQ]
        y_s[c] = r["y"][2 * SEQ:]
        for s, gi in enumerate((2 * c, 2 * c + 1, 16 + c)):
            o = r["oCT"][s].reshape(128, 4, 2, 257)
            Cs[gi] = o[..., 0:256].transpose(1, 3, 2, 0).reshape(4, 256, 256)
            ns[gi] = o[..., 256].transpose(1, 2, 0).reshape(4, 256)
            ms[gi] = r["om"][s].reshape(4)
            hs[gi] = r["oh"][s].T.reshape(D)
            cvs[gi] = r["oconv"][s].reshape(128, 8, 3).transpose(2, 1, 0).reshape(3, D)
    return (y_p, y_s, Cs[None, :16], ns[None, :16], ms[None, :16], hs[None, :16], cvs[None, :16],
            Cs[None, 16:], ns[None, 16:], ms[None, 16:], hs[None, 16:], cvs[None, 16:])
```

```python
import numpy as np
from contextlib import ExitStack
import concourse.bass as bass
import concourse.mybir as mybir
from concourse.bass_utils import run_bass_kernel_spmd

F32 = mybir.dt.float32
BF16 = mybir.dt.bfloat16
U8 = mybir.dt.uint8
AF = mybir.ActivationFunctionType
ALU = mybir.AluOpType
AX = mybir.AxisListType

D = 1024
H = 4
DH = 256
SEQ = 2048
DEC = 16
NTOK = 2 * SEQ + DEC
TT = 1024
EPS = 1e-6
NCH_W = 88
NSLOT_LOADS = NCH_W // 4
NS = 3
PV_BMOD, PV_GN, PV_GH, PV_CW, PV_CB, PV_BRA, PV_BRX, PV_LAM, PV_BG = 0, 24, 32, 40, 72, 80, 88, 96, 104
NPV = 120
DBG_SUB = 0
OPT_B = 0
OPT_SQ = 0
STQ = 'sp'


class Buf:
    __slots__ = ("w", "r", "excl")

    def __init__(self, excl=False):
        self.w = None
        self.r = {}
        self.excl = excl


def bufs(n, excl=False):
    return [Buf(excl) for _ in range(n)]


class TR:
    def __init__(self, nc, es):
        self.nc = nc
        self.es = es
        self.eng = {"pe": nc.tensor, "act": nc.scalar, "dve": nc.vector, "pool": nc.gpsimd, "sp": nc.sync}
        self.csem = {k: es.enter_context(nc.semaphore("c_" + k)) for k in ("pe", "act", "dve", "pool")}
        self.cnt = {k: 0 for k in self.csem}
        self.known = {k: {} for k in self.eng}
        self.dsem = {}
        self.pending = []
        self.deferring = False
        self.sp_guard = []
        self.deferred = []
        self.cur_bundle = []

    def _wait(self, e, tok):
        if tok is None:
            return
        key, sem, val = tok
        if e == "pe" and key == "pe":
            return
        if self.known[e].get(key, 0) >= val:
            return
        self.eng[e].wait_ge(sem, val)
        self.known[e][key] = val

    def _deps(self, e, reads, writes):
        for b in reads:
            self._wait(e, b.w)
            if b.excl:
                for key, (sem, val) in list(b.r.items()):
                    self._wait(e, (key, sem, val))
        for b in writes:
            self._wait(e, b.w)
            for key, (sem, val) in list(b.r.items()):
                self._wait(e, (key, sem, val))

    @staticmethod
    def _mark(tok, reads, writes):
        key, sem, val = tok
        for b in reads:
            if b.excl:
                b.w = tok
                b.r = {}
            else:
                b.r[key] = (sem, val)
        for b in writes:
            b.w = tok
            b.r = {}

    def op(self, e, fn, reads=(), writes=(), signal=True):
        if self.deferring:
            self.cur_bundle.append((e, fn, list(reads), list(writes), signal))
            if signal:
                self.deferred.append(self.cur_bundle)
                self.cur_bundle = []
            return None
        return self._emit(e, fn, reads, writes, signal)

    def flush(self, n=None):
        was = self.deferring
        self.deferring = False
        k = 0
        while self.deferred and (n is None or k < n):
            for args in self.deferred.pop(0):
                self._emit(*args)
            k += 1
        self.deferring = was

    def _emit(self, e, fn, reads=(), writes=(), signal=True):
        self._deps(e, reads, writes)
        inst = fn(self.eng[e])
        if not signal:
            assert e == "pe"
            self.pending.append((reads, writes))
            return None
        self.cnt[e] += 1
        inst.then_inc(self.csem[e], 1)
        tok = (e, self.csem[e], self.cnt[e])
        if e == "pe" and self.pending:
            for r, w in self.pending:
                self._mark(tok, r, w)
            self.pending = []
        self._mark(tok, reads, writes)
        return tok

    def dma(self, q, out, in_, reads, writes, slot, guard=False, **kw):
        if guard:
            for t in self.sp_guard:
                self._wait(q, t)
        self._deps(q, reads, writes)
        if slot not in self.dsem:
            self.dsem[slot] = [self.es.enter_context(self.nc.semaphore("d_" + slot)), 0]
        ent = self.dsem[slot]
        ent[1] += 16
        self.eng[q].dma_start(out=out, in_=in_, **kw).then_inc(ent[0], 16)
        tok = ("d_" + slot, ent[0], ent[1])
        self._mark(tok, reads, writes)
        return tok

    PHASE_DMA = ("xin", "xr", "yo", "s_wm", "dbgo")

    def barrier(self, engines=None, final=False):
        assert not self.pending
        toks = [(k, self.csem[k], self.cnt[k]) for k in self.csem if self.cnt[k] > 0]
        for slot, (sem, c) in self.dsem.items():
            if c > 0 and (final or slot.startswith(self.PHASE_DMA)):
                toks.append(("d_" + slot, sem, c))
        if final:
            for e in (engines or self.eng):
                for t in toks:
                    self._wait(e, t)
            return
        for e in (engines or ("act", "dve", "pool")):
            for t in toks:
                self._wait(e, t)
        self.sp_guard = toks


class Arena:
    def __init__(self, ap, size):
        self.ap = ap
        self.size = size
        self.off = 0

    def alloc(self, nelem, dt):
        esz = 4 if dt == F32 else 2
        off = (self.off + 63) // 64 * 64
        self.off = off + nelem * esz
        assert self.off <= self.size, ("arena overflow", self.off, self.size)
        return self.ap[:, off:off + nelem * esz].bitcast(dt)


class StopBuild(Exception):
    pass


def build_program(maxphase=10 ** 9):
    nc = bass.Bass("TRN2", target_bir_lowering=False)
    phase_ctr = [0]

    def gate():
        phase_ctr[0] += 1
        if phase_ctr[0] > maxphase:
            raise StopBuild()

    def din(name, shape):
        return nc.dram_tensor(name, shape, F32, kind="ExternalInput").ap()

    def dout(name, shape):
        return nc.dram_tensor(name, shape, F32, kind="ExternalOutput").ap()

    x_d = din("x", [NTOK, D])
    cT_d = din("cT", [128, 8, 3])
    wmod_d = din("wmod", [128, 8, 3 * D])
    pv_d = din("pv", [128, NPV])
    bif_d = din("bif", [4, 2])
    gfin_d = din("gfin", [1, D])
    wst_d = din("wst", [NCH_W, 128, D])
    wif_d = din("wif", [128, 64])
    wr_d = din("wr", [128, 2048])
    wo_d = din("wo", [128, 8 * D])
    sCT_d = din("sCT", [128, 4 * 2 * 257])
    sm_d = din("sm", [4, 1])
    sh_d = din("sh", [128, 8])
    sconv_d = din("sconv", [128, 24])
    wscr_d = nc.dram_tensor("wscr", [NSLOT_LOADS, 128, 4 * D], BF16, kind="Internal").ap()
    y_d = dout("y", [NTOK, D])
    oCT_d = dout("oCT", [3, 128, 4 * 2 * 257])
    om_d = dout("om", [3, 4, 1])
    oh_d = dout("oh", [3, 128, 8])
    oconv_d = dout("oconv", [3, 128, 24])
    DBG_DUMP = 0
    if DBG_DUMP:
        dbg_d = dout("dbg", [3, 128, 8 * TT])

    with ExitStack() as es:
        ARENA = 207 * 1024
        arena_t = es.enter_context(nc.sbuf_tensor("arena", [128, ARENA], U8))
        PB = [es.enter_context(nc.psum_tensor("pb%d" % i, [128, 512], F32)) for i in range(8)]
        T = TR(nc, es)
        A = Arena(arena_t, ARENA)
        out_toks = {}
        fold_wo = [False]

        wo_bf = A.alloc(8 * D, BF16).rearrange("p (k n) -> p k n", k=8)
        wr_bf = A.alloc(2048, F32).rearrange("p (g n j) -> p g n j", g=2, n=8)
        wif_bf = A.alloc(64, BF16).rearrange("p (k m) -> p k m", k=8)
        ident_bf = A.alloc(128, BF16)
        ident_f = A.alloc(128, F32)
        ones_f = A.alloc(128, F32)
        maskT = A.alloc(128, BF16)
        pv = A.alloc(NPV, F32)
        bif = A.alloc(2, F32)
        nbf = A.alloc(1, F32)
        cT = A.alloc(24, F32).rearrange("p (k s) -> p k s", k=8)
        modT = A.alloc(72, F32).rearrange("p (m s) -> p m s", m=24)
        gs = A.alloc(24, F32).rearrange("p (m s) -> p m s", m=8)
        cfeat = A.alloc(8, F32)
        lamt = A.alloc(8, F32)
        gfin_bc = A.alloc(D, F32)
        gate_bc = A.alloc(D, F32)
        gdiag = None
        Cst = A.alloc(4 * 2 * 257, F32).rearrange("p (h c v) -> p h c v", h=4, c=2)
        Cbf = A.alloc(4 * 2 * 258, BF16).rearrange("p (h c v) -> p h c v", h=4, c=2)
        mstate = A.alloc(1, F32)
        hcar = A.alloc(8, F32)
        convcar = A.alloc(24, F32).rearrange("p (f j) -> p f j", f=8)
        rmask = A.alloc(TT, F32)
        uT = A.alloc(8 * TT, BF16).rearrange("p (f t) -> p f t", f=8)
        yaT = A.alloc(8 * TT, BF16).rearrange("p (f t) -> p f t", f=8)
        ybT = A.alloc(8 * TT, BF16).rearrange("p (f t) -> p f t", f=8)
        mgT = A.alloc(8 * TT, BF16).rearrange("p (f t) -> p f t", f=8)
        ring = [A.alloc(4 * D, BF16).rearrange("p (c k m) -> p c k m", c=4, k=8) for _ in range(NS)]
        PERSIST_END = A.off

        b_const = Buf()
        b_wc = Buf()
        b_wo = Buf()
        b_mod = Buf()
        b_gbc = Buf()
        b_ring = bufs(NS)
        b_uT = [bufs(8) for _ in range(8)]
        b_yaT = [bufs(8) for _ in range(8)]
        b_ybT = [bufs(2) for _ in range(8)]
        b_mgT = [bufs(2) for _ in range(8)]
        b_C = bufs(4)
        b_Cbf = bufs(4)
        b_ms = Buf()
        b_hcar = bufs(8)
        b_ccar = bufs(8)
        b_PB = bufs(8, excl=True)

        T.op("pool", lambda e: e.memset(ident_bf, 0.0), writes=[b_const])
        T.op("pool", lambda e: e.affine_select(out=ident_bf, in_=ident_bf, pattern=[[1, 128]],
                                                compare_op=ALU.not_equal, fill=1.0, base=0,
                                                channel_multiplier=-1), reads=[b_const], writes=[b_const])
        T.op("pool", lambda e: e.memset(ident_f, 0.0), writes=[b_const])
        T.op("pool", lambda e: e.affine_select(out=ident_f, in_=ident_f, pattern=[[1, 128]],
                                                compare_op=ALU.not_equal, fill=1.0, base=0,
                                                channel_multiplier=-1), reads=[b_const], writes=[b_const])
        T.op("pool", lambda e: e.memset(ones_f, 1.0), writes=[b_const])
        T.op("pool", lambda e: e.memset(maskT, 1.0), writes=[b_const])
        T.op("pool", lambda e: e.affine_select(out=maskT, in_=maskT, pattern=[[1, 128]],
                                                compare_op=ALU.is_ge, fill=0.0, base=0,
                                                channel_multiplier=-1), reads=[b_const], writes=[b_const])
        T.op("pool", lambda e: e.memset(rmask[0:4, :], 1.0), writes=[b_const])
        T.op("pool", lambda e: e.memset(rmask[0:4, :].rearrange("p (c l) -> p c l", l=128)[:, :, 0:1], 0.0),
             reads=[b_const], writes=[b_const])
        T.dma("sp", pv, pv_d, [], [b_const], "s_pv")
        T.dma("sp", bif[0:4, :], bif_d, [], [b_const], "s_bif")
        T.dma("sp", cT.rearrange("p k s -> p (k s)"), cT_d.rearrange("p k s -> p (k s)"), [], [b_const], "s_cT")
        T.dma("sp", gfin_bc, gfin_d.to_broadcast([128, D]), [], [b_const], "s_gfin")
        T.op("act", lambda e: e.activation(out=lamt, in_=pv[:, PV_LAM:PV_LAM + 8], func=AF.Exp, scale=-1.0),
             reads=[b_const], writes=[b_const])
        T.op("act", lambda e: e.activation(out=lamt, in_=lamt, func=AF.Ln, bias=1.0),
             reads=[b_const], writes=[b_const])
        T.op("dve", lambda e: e.tensor_scalar(out=cfeat, in0=lamt, scalar1=-8.0, scalar2=None, op0=ALU.mult),
             reads=[b_const], writes=[b_const])
        T.op("dve", lambda e: e.tensor_scalar(out=nbf[0:4, :], in0=bif[0:4, 1:2], scalar1=-1.0, scalar2=None,
                                              op0=ALU.mult), reads=[b_const], writes=[b_const])

        wm = A.alloc(8 * 3072, BF16).rearrange("p (k n) -> p k n", k=8)
        cTb = A.alloc(24, BF16).rearrange("p (k s) -> p k s", k=8)
        b_wm = bufs(4)
        for q4 in range(4):
            T.dma("pool", wm[:, :, q4 * 768:(q4 + 1) * 768], wmod_d[:, :, q4 * 768:(q4 + 1) * 768], [], [b_wm[q4]],
                  "s_wm%d" % q4)
        T.dma("pool", wif_bf.rearrange("p k m -> p (k m)"), wif_d, [], [b_wc], "s_wif")
        T.dma("sp", wr_bf.rearrange("p g n j -> p (g n j)"), wr_d, [], [b_wc], "s_wr")
        T.dma("pool", wo_bf.rearrange("p k n -> p (k n)"), wo_d, [], [b_wo], "s_wo")
        T.op("dve", lambda e: e.tensor_copy(out=cTb, in_=cT), reads=[b_const], writes=[b_const])
        for m in range(24):
            for kc in range(8):
                T.op("pe", lambda e, m=m, kc=kc: e.matmul(
                    PB[0][:, 4 * m:4 * m + 3], lhsT=wm[:, kc, m * 128:(m + 1) * 128], rhs=cTb[:, kc, :],
                    start=(kc == 0), stop=(kc == 7)),
                    reads=[b_wm[m // 6], b_const], writes=[b_PB[0]], signal=(kc == 7))
        T.op("dve", lambda e: e.tensor_tensor(
            out=modT, in0=PB[0][:, 0:96].rearrange("p (m s) -> p m s", s=4)[:, :, 0:3],
            in1=pv[:, PV_BMOD:PV_BMOD + 24].unsqueeze(2).to_broadcast([128, 24, 3]), op=ALU.add),
            reads=[b_PB[0], b_const], writes=[b_mod])
        T.op("dve", lambda e: e.tensor_scalar(out=gs, in0=modT[:, 8:16, :], scalar1=1.0, scalar2=None, op0=ALU.add),
             reads=[b_mod], writes=[b_mod])
        T.op("dve", lambda e: e.tensor_tensor(
            out=gs, in0=gs, in1=pv[:, PV_GN:PV_GN + 8].unsqueeze(2).to_broadcast([128, 8, 3]), op=ALU.mult),
            reads=[b_mod, b_const], writes=[b_mod])
        T.barrier()
        A.off = PERSIST_END

        ring_state = {"issued": 0}
        b_scr = bufs(NSLOT_LOADS)
        NS_S = 6
        SAMPLE_BASE = NSLOT_LOADS * 4
        RS_OFF = ARENA - NS_S * 8192
        ring_s = [arena_t[:, RS_OFF + i * 8192: RS_OFF + (i + 1) * 8192].bitcast(BF16)
                  .rearrange("p (c k m) -> p c k m", c=4, k=8) for i in range(NS_S)]
        b_ring_s = bufs(NS_S)

        def ring_slot(j):
            if j >= SAMPLE_BASE:
                i = (j - SAMPLE_BASE) % NS_S
                return ring_s[i], b_ring_s[i], "ringX%d" % i, NS_S
            return ring[j % NS], b_ring[j % NS], "ring%d" % (j % NS), NS

        def ring_issue_upto(k):
            while ring_state["issued"] <= k:
                j = ring_state["issued"]
                ci = (j % NSLOT_LOADS) * 4
                jj = j % NSLOT_LOADS
                rv, rb, rname, _ = ring_slot(j)
                if j < NSLOT_LOADS:
                    sl = j % NS
                    T.dma("pool", rv.rearrange("p c k m -> p c (k m)"),
                          wst_d[ci:ci + 4].rearrange("c p n -> p c n"), [], [rb], "ringS%d" % sl)
                    T.dma("sp", wscr_d[jj], rv.rearrange("p c k m -> p (c k m)"), [rb], [b_scr[jj]], "scr%d" % sl)
                else:
                    T.dma("sp", rv.rearrange("p c k m -> p (c k m)"), wscr_d[jj], [b_scr[jj]], [rb], rname,
                          guard=(j >= SAMPLE_BASE))
                ring_state["issued"] += 1

        total_loads = [0]

        def ring_next(dual=False):
            k = total_loads[0]
            total_loads[0] += 1
            rv, rb, _, ns = ring_slot(k)
            look = ns - 2 if dual else ns - 1
            hi = k + look
            if k < SAMPLE_BASE:
                hi = min(hi, SAMPLE_BASE - 1)
            ring_issue_upto(min(hi, NLOADS_TOTAL - 1))
            return rv, rb

        sts = []
        for s in range(2):
            for part in range(SEQ // TT):
                sts.append(dict(seq=s, tok0=s * SEQ + part * TT, T=TT, tp=128, first=(part == 0),
                                last=(part == SEQ // TT - 1), prompt=True))
        sts.append(dict(seq=2, tok0=2 * SEQ, T=DEC, tp=DEC, first=True, last=True, prompt=False))
        NLOADS_TOTAL = NSLOT_LOADS * len(sts)

        psum_rr = [0]
        bank_pool = [list(range(8))]

        def next_bank():
            pool = bank_pool[0]
            i = pool[psum_rr[0] % len(pool)]
            psum_rr[0] += 1
            return i

        def proj(wv, bw, ci, src, bsrc, Tn, nn, evac):
            for nt in range(Tn // nn):
                bk = next_bank()
                for kc in range(8):
                    T.op("pe", lambda e, bk=bk, kc=kc, nt=nt: e.matmul(
                        PB[bk][:, 0:nn], lhsT=wv[:, ci, kc, :], rhs=src[:, kc, nt * nn:(nt + 1) * nn],
                        start=(kc == 0), stop=(kc == 7)),
                        reads=[bw] + bsrc(kc, nt), writes=[b_PB[bk]], signal=(kc == 7))
                evac(bk, nt)

        def p0_alloc():
            return dict(xin=[A.alloc(D, F32) for _ in range(2)], xn=[A.alloc(D, BF16) for _ in range(2)],
                        junk=A.alloc(D, BF16), stat=[A.alloc(4, F32) for _ in range(2)],
                        b_xin=bufs(2), b_xn=bufs(2), b_stat=bufs(2), b_junk=Buf())

        def p0_tile(st, j, Z, load_only=False, stages=("stats", "xn", "tr", "ev")):
            s, tp, tok0 = st["seq"], st["tp"], st["tok0"]
            p = j % 2
            xin, xn, junk, stat = Z["xin"][p], Z["xn"][p], Z["junk"], Z["stat"][p]
            b_xin, b_xn, b_stat, b_junk = Z["b_xin"][p], Z["b_xn"][p], Z["b_stat"][p], Z["b_junk"]
            if load_only:
                T.dma("sp", xin[0:tp, :], x_d[tok0 + j * tp: tok0 + (j + 1) * tp, :], [], [b_xin], "xin%d" % p, guard=True)
                return
            if "stats" in stages:
                T.op("act", lambda e: e.activation(out=junk[0:tp, :], in_=xin[0:tp, :], func=AF.Square,
                                                   accum_out=stat[0:tp, 0:1]), reads=[b_xin], writes=[b_junk, b_stat])
                T.op("act", lambda e: e.activation(out=stat[0:tp, 1:2], in_=stat[0:tp, 0:1], func=AF.Ln,
                                                   scale=1.0 / D, bias=EPS), reads=[b_stat], writes=[b_stat])
                T.op("act", lambda e: e.activation(out=stat[0:tp, 2:3], in_=stat[0:tp, 1:2], func=AF.Exp,
                                                   scale=-0.5), reads=[b_stat], writes=[b_stat])
            if "xn" in stages:
                T.op("pool", lambda e: e.tensor_scalar(out=xn[0:tp, :], in0=xin[0:tp, :], scalar1=stat[0:tp, 2:3],
                                                       scalar2=0.0, op0=ALU.mult, op1=ALU.add),
                     reads=[b_xin, b_stat], writes=[b_xn])
            def bank_of(fc):
                if fc < 6:
                    return 2 * p, PB[2 * p][:, 0:384].bitcast(BF16).rearrange("p (f t) -> p f t", f=6)[:, fc, 0:tp]
                return 2 * p + 1, PB[2 * p + 1][:, 0:128].bitcast(BF16).rearrange("p (f t) -> p f t", f=2)[:, fc - 6, 0:tp]
            if "tr" in stages:
                for fc in range(8):
                    bk, pv_ = bank_of(fc)
                    T.op("pe", lambda e, fc=fc, pv_=pv_: e.transpose(
                        out=pv_, in_=xn[0:tp, fc * 128:(fc + 1) * 128], identity=ident_bf[0:tp, 0:tp]),
                        reads=[b_xn, b_const], writes=[b_PB[bk]], signal=(fc in (5, 7)))
            if "ev" in stages:
                for fc in (0, 6, 1, 2, 7, 3, 4, 5):
                    bk, pv_ = bank_of(fc)
                    dst = uT[:, fc, j * tp:(j + 1) * tp]
                    if fc < 6:
                        T.op("dve", lambda e, fc=fc, dst=dst, pv_=pv_: e.tensor_scalar(
                            out=dst, in0=pv_, scalar1=gs[:, fc, s:s + 1], scalar2=modT[:, fc, s:s + 1],
                            op0=ALU.mult, op1=ALU.add), reads=[b_PB[bk], b_mod], writes=[b_uT[fc][j]])
                    else:
                        T.op("act", lambda e, fc=fc, dst=dst, pv_=pv_: e.activation(
                            out=dst, in_=pv_, func=AF.Identity, scale=gs[:, fc, s:s + 1],
                            bias=modT[:, fc, s:s + 1]), reads=[b_PB[bk], b_mod], writes=[b_uT[fc][j]])

        def pd_alloc(nb=2):
            one = lambda mk: [mk() for _ in range(nb)] * (2 // nb)
            return dict(xr=one(lambda: A.alloc(D, F32)), yt=one(lambda: A.alloc(D, F32)),
                        ot=one(lambda: A.alloc(D, F32)), junkd=A.alloc(D, BF16),
                        std=one(lambda: A.alloc(4, F32)),
                        b_xr=one(Buf), b_yt=one(Buf), b_ot=one(Buf), b_std=one(Buf), b_junkd=Buf())

        def pd_tile(st, j, Z, load_only=False, stages=("pe", "mul", "add", "stats", "fin")):
            tp, tok0 = st["tp"], st["tok0"]
            nn = min(512, st["T"])
            p = j % 2
            xr, yt, ot, junkd, std = Z["xr"][p], Z["yt"][p], Z["ot"][p], Z["junkd"], Z["std"][p]
            b_xr, b_yt, b_ot, b_std, b_junkd = Z["b_xr"][p], Z["b_yt"][p], Z["b_ot"][p], Z["b_std"][p], Z["b_junkd"]
            ntj = (j * tp) // nn
            rows = slice(tok0 + j * tp, tok0 + (j + 1) * tp)
            if load_only:
                T.dma("sp", xr[0:tp, :], x_d[rows, :], [], [b_xr], "xr%d" % p, guard=True)
                return
            for half in range(2):
                bk = 4 + 2 * p + half
                if "pe" in stages:
                    for kc in range(8):
                        T.op("pe", lambda e, bk=bk, kc=kc, half=half: e.matmul(
                            PB[bk][0:tp, :], lhsT=mgT[:, kc, j * tp:(j + 1) * tp], rhs=wo_bf[:, kc, half * 512:(half + 1) * 512],
                            start=(kc == 0), stop=(kc == 7)),
                            reads=[b_mgT[kc][ntj], b_wo], writes=[b_PB[bk]], signal=(kc == 7))
                if "mul" in stages:
                    T.op("dve", lambda e, bk=bk, half=half: e.tensor_tensor(
                        out=yt[0:tp, half * 512:(half + 1) * 512], in0=PB[bk][0:tp, :],
                        in1=xr[0:tp, half * 512:(half + 1) * 512], op=ALU.add),
                        reads=[b_PB[bk], b_xr], writes=[b_yt])
            if "stats" in stages:
                T.op("act", lambda e: e.activation(out=junkd[0:tp, :], in_=yt[0:tp, :], func=AF.Square,
                                                   accum_out=std[0:tp, 0:1]), reads=[b_yt], writes=[b_junkd, b_std])
                T.op("act", lambda e: e.activation(out=std[0:tp, 1:2], in_=std[0:tp, 0:1], func=AF.Ln,
                                                   scale=1.0 / D, bias=EPS), reads=[b_std], writes=[b_std])
                T.op("act", lambda e: e.activation(out=std[0:tp, 2:3], in_=std[0:tp, 1:2], func=AF.Exp,
                                                   scale=-0.5), reads=[b_std], writes=[b_std])
            if "fin" in stages:
                T.op("dve", lambda e: e.scalar_tensor_tensor(out=ot[0:tp, :], in0=yt[0:tp, :], scalar=std[0:tp, 2:3],
                                                             in1=gfin_bc[0:tp, :], op0=ALU.mult, op1=ALU.mult),
                     reads=[b_yt, b_std, b_const], writes=[b_ot])
                out_toks["y%d" % p] = T.dma(STQ, y_d[rows, :], ot[0:tp, :], [b_ot], [], "yo%d" % p)

        def run_st(idx):
            st = sts[idx]
            s = st["seq"]
            if not st["prompt"]:
                A.size = RS_OFF
            Tn = st["T"]
            tp = st["tp"]
            L = tp
            nch = Tn // L
            nn = min(512, Tn)
            nnt = Tn // nn
            tpn = nn // tp
            tok0 = st["tok0"]

            def uT_bufs(kc, nt):
                return [b_uT[kc][nt * tpn + i] for i in range(tpn)]

            if st["first"]:
                if st["prompt"]:
                    for h in range(4):
                        T.op("pool", lambda e, h=h: e.memset(Cst[:, h].rearrange("p c v -> p (c v)"), 0.0),
                             writes=[b_C[h]])
                    T.op("pool", lambda e: e.memset(mstate[0:4, :], 0.0), writes=[b_ms])
                    T.op("pool", lambda e: e.memset(hcar, 0.0), writes=b_hcar)
                    T.op("pool", lambda e: e.memset(convcar.rearrange("p f j -> p (f j)"), 0.0), writes=b_ccar)
                else:
                    T.dma("sp", Cst.rearrange("p h c v -> p (h c v)"), sCT_d, [], b_C, "s_sCT")
                    T.dma("sp", mstate[0:4, :], sm_d, [], [b_ms], "s_sm")
                    T.dma("sp", hcar, sh_d, [], b_hcar, "s_sh")
                    T.dma("sp", convcar.rearrange("p f j -> p (f j)"), sconv_d, [], b_ccar, "s_sconv")
                gd = A.alloc(D, F32).rearrange("p (f j) -> p f j", f=8)
                b_gd = Buf()
                T.op("dve", lambda e: e.tensor_tensor(
                    out=gd, in0=modT[:, 16:24, s:s + 1].to_broadcast([128, 8, 128]),
                    in1=ident_f.unsqueeze(1).to_broadcast([128, 8, 128]), op=ALU.mult),
                    reads=[b_mod, b_const], writes=[b_gd])
                for half in range(2):
                    T.op("pe", lambda e, half=half: e.matmul(
                        PB[half][:, :], lhsT=ones_f, rhs=gd.rearrange("p f j -> p (f j)")[:, half * 512:(half + 1) * 512],
                        start=True, stop=True), reads=[b_gd, b_const], writes=[b_PB[half]])
                    T.op("act", lambda e, half=half: e.activation(
                        out=gate_bc[:, half * 512:(half + 1) * 512], in_=PB[half][:, :], func=AF.Copy),
                        reads=[b_PB[half]], writes=[b_gbc])
                T.barrier()
                A.off = PERSIST_END
                if s > 0:
                    T.dma("pool", wo_bf.rearrange("p k n -> p (k n)"), wo_d, [], [b_wo], "s_wo")
                fold_wo[0] = True

            if idx == 0:
                gate()
                Z0 = p0_alloc()
                p0_tile(st, 0, Z0, load_only=True)
                for j in range(Tn // tp):
                    if j + 1 < Tn // tp:
                        p0_tile(st, j + 1, Z0, load_only=True)
                    p0_tile(st, j, Z0)
                T.barrier()
                A.off = PERSIST_END

            gate()
            igb = A.alloc(Tn, F32)[0:4, :]
            cs = A.alloc(Tn, F32)[0:4, :]
            gg = A.alloc(Tn, F32)[0:4, :]
            esc = A.alloc(Tn, F32)[0:4, :]
            thr = A.alloc(Tn, F32)[0:4, :]
            sm_ = A.alloc(64, F32)[0:4, :]
            gmax, Ac, Bc, mall, Mc, mprev, dec = (sm_[:, i * 8:i * 8 + nch] for i in range(7))
            decd = A.alloc(32, F32)[0:4, 0:4 * nch].rearrange("p (a c) -> p a c", a=4)
            escT = A.alloc(64, F32)[:, 0:nch * 8].rearrange("p (c j) -> p c j", j=8)
            decbc = A.alloc(32, F32)[:, 0:4 * nch].rearrange("p (a c) -> p a c", a=4)
            b_g = Buf()
            b_escT = Buf()
            b_decbc = Buf()
            T.deferring = True
            for nt in range(nnt):
                sl = slice(nt * nn, (nt + 1) * nn)
                bi, bf_ = 6, 7
                for (bk, c0) in ((bi, 0), (bf_, 4)):
                    for kc in range(8):
                        T.op("pe", lambda e, bk=bk, c0=c0, kc=kc, sl=sl: e.matmul(
                            PB[bk][0:4, 0:nn], lhsT=wif_bf[:, kc, c0:c0 + 4], rhs=uT[:, kc, sl],
                            start=(kc == 0), stop=(kc == 7)),
                            reads=[b_wc] + uT_bufs(kc, nt), writes=[b_PB[bk]], signal=(kc == 7))
                T.op("dve", lambda e, bi=bi, sl=sl: e.tensor_scalar(
                    out=igb[:, sl], in0=PB[bi][0:4, 0:nn], scalar1=bif[0:4, 0:1], scalar2=None, op0=ALU.add),
                    reads=[b_PB[bi], b_const], writes=[b_g])
                T.op("act", lambda e, bf_=bf_, sl=sl: e.activation(
                    out=gg[:, sl], in_=PB[bf_][0:4, 0:nn], func=AF.Exp, scale=-1.0, bias=nbf[0:4, :]),
                    reads=[b_PB[bf_], b_const], writes=[b_g])
                T.op("act", lambda e, sl=sl: e.activation(out=gg[:, sl], in_=gg[:, sl], func=AF.Ln, bias=1.0),
                     reads=[b_g], writes=[b_g])
            T.op("dve", lambda e: e.tensor_tensor_scan(out=cs, data0=rmask[0:4, 0:Tn], data1=gg, initial=0.0,
                                                       op0=ALU.mult, op1=ALU.add), reads=[b_g, b_const], writes=[b_g])
            T.op("dve", lambda e: e.tensor_tensor(out=gg, in0=igb, in1=cs, op=ALU.add), reads=[b_g], writes=[b_g])
            T.op("dve", lambda e: e.tensor_reduce(out=gmax, in_=gg.rearrange("p (c l) -> p c l", l=L), axis=AX.X,
                                                  op=ALU.max), reads=[b_g], writes=[b_g])
            T.op("dve", lambda e: e.tensor_scalar(out=Ac, in0=cs.rearrange("p (c l) -> p c l", l=L)[:, :, L - 1],
                                                  scalar1=-1.0, scalar2=None, op0=ALU.mult), reads=[b_g], writes=[b_g])
            T.op("dve", lambda e: e.tensor_tensor(out=Bc, in0=gmax, in1=Ac, op=ALU.add), reads=[b_g], writes=[b_g])
            T.op("dve", lambda e: e.tensor_tensor_scan(out=mall, data0=Ac, data1=Bc, initial=mstate[0:4, :],
                                                       op0=ALU.add, op1=ALU.max), reads=[b_g, b_ms], writes=[b_g])
            T.op("dve", lambda e: e.tensor_tensor(out=Mc, in0=mall, in1=Ac, op=ALU.subtract), reads=[b_g], writes=[b_g])
            T.op("dve", lambda e: e.tensor_copy(out=mprev[:, 0:1], in_=mstate[0:4, :]), reads=[b_ms], writes=[b_g])
            if nch > 1:
                T.op("dve", lambda e: e.tensor_copy(out=mprev[:, 1:nch], in_=mall[:, 0:nch - 1]),
                     reads=[b_g], writes=[b_g])
            T.op("dve", lambda e: e.tensor_copy(out=mstate[0:4, :], in_=mall[:, nch - 1:nch]),
                 reads=[b_g], writes=[b_ms])
            T.op("dve", lambda e: e.tensor_tensor(out=dec, in0=mprev, in1=Mc, op=ALU.subtract), reads=[b_g], writes=[b_g])
            T.op("act", lambda e: e.activation(out=dec, in_=dec, func=AF.Exp), reads=[b_g], writes=[b_g])
            Mcb = Mc.unsqueeze(2).to_broadcast([4, nch, L])
            T.op("dve", lambda e: e.tensor_tensor(out=esc.rearrange("p (c l) -> p c l", l=L),
                                                  in0=gg.rearrange("p (c l) -> p c l", l=L), in1=Mcb, op=ALU.subtract),
                 reads=[b_g], writes=[b_g])
            T.op("act", lambda e: e.activation(out=esc, in_=esc, func=AF.Exp), reads=[b_g], writes=[b_g])
            T.op("dve", lambda e: e.tensor_tensor(out=thr.rearrange("p (c l) -> p c l", l=L),
                                                  in0=cs.rearrange("p (c l) -> p c l", l=L), in1=Mcb, op=ALU.subtract),
                 reads=[b_g], writes=[b_g])
            T.op("act", lambda e: e.activation(out=thr, in_=thr, func=AF.Exp), reads=[b_g], writes=[b_g])
            bkE = 6
            psE = PB[bkE][:, 0:nch * 8].rearrange("p (c j) -> p c j", j=8)
            for c in range(nch):
                for (srcv, c0) in ((esc, 0), (thr, 4)):
                    T.op("pe", lambda e, c=c, srcv=srcv, c0=c0: e.transpose(
                        out=psE[0:L, c, c0:c0 + 4], in_=srcv[:, c * L:(c + 1) * L], identity=ident_f[0:4, 0:4]),
                        reads=[b_g, b_const], writes=[b_PB[bkE]], signal=(c == nch - 1 and c0 == 4))
            T.op("dve", lambda e: e.tensor_copy(out=escT[0:L], in_=psE[0:L]), reads=[b_PB[bkE]], writes=[b_escT])
            T.op("dve", lambda e: e.tensor_tensor(
                out=decd, in0=dec.unsqueeze(1).to_broadcast([4, 4, nch]),
                in1=ident_f[0:4, 0:4].unsqueeze(2).to_broadcast([4, 4, nch]), op=ALU.mult),
                reads=[b_g, b_const], writes=[b_g])
            bkD = 7
            T.op("pe", lambda e: e.matmul(PB[bkD][:, 0:4 * nch], lhsT=ones_f[0:4, :],
                                          rhs=decd.rearrange("p a c -> p (a c)"), start=True, stop=True),
                 reads=[b_g, b_const], writes=[b_PB[bkD]])
            T.op("dve", lambda e: e.tensor_copy(out=decbc.rearrange("p a c -> p (a c)"), in_=PB[bkD][:, 0:4 * nch]),
                 reads=[b_PB[bkD]], writes=[b_decbc])
            if fold_wo[0]:
                fold_wo[0] = False
                for kc in range(8):
                    T.op("dve", lambda e, kc=kc: e.tensor_tensor(out=wo_bf[:, kc, :], in0=wo_bf[:, kc, :], in1=gate_bc,
                                                                 op=ALU.mult), reads=[b_wo, b_gbc], writes=[b_wo])
            T.deferring = False
            A1_END = A.off

            for g in range(2):
                gate()
                if g == 0:
                    A.off = A1_END
                    qT = A.alloc(4 * Tn, BF16).rearrange("p (i t) -> p i t", i=4)
                    kT = A.alloc(4 * Tn, BF16).rearrange("p (i t) -> p i t", i=4)
                    vT = A.alloc(4 * Tn, BF16).rearrange("p (i t) -> p i t", i=4)
                    gaT = A.alloc(4 * Tn, BF16).rearrange("p (i t) -> p i t", i=4)
                    szt = [A.alloc(nn, F32) for _ in range(2)]
                    tzt = [A.alloc(nn, F32) for _ in range(2)]
                    b_szt, b_tzt = bufs(2), bufs(2)
                    b_q = [bufs(nnt) for _ in range(4)]
                    b_k = [bufs(nnt) for _ in range(4)]
                    b_v = [bufs(nnt) for _ in range(4)]
                    b_ga = [bufs(nnt) for _ in range(4)]
                ecnt = [0]

                def ev_copy(dstT, bdst, i):
                    def f(bk, nt):
                        ecnt[0] += 1
                        dst = dstT[:, i, nt * nn:(nt + 1) * nn]
                        if ecnt[0] % 2 == 0:
                            T.op("act", lambda e: e.activation(out=dst, in_=PB[bk][:, 0:nn], func=AF.Copy),
                                 reads=[b_PB[bk]], writes=[bdst[i][nt]])
                        else:
                            T.op("dve", lambda e: e.tensor_copy(out=dst, in_=PB[bk][:, 0:nn]),
                                 reads=[b_PB[bk]], writes=[bdst[i][nt]])
                    return f

                def ev_k(i):
                    def f(bk, nt):
                        dst = kT[:, i, nt * nn:(nt + 1) * nn]
                        T.op("dve", lambda e: e.tensor_scalar(out=dst, in0=PB[bk][:, 0:nn], scalar1=DH ** -0.5,
                                                              scalar2=None, op0=ALU.mult),
                             reads=[b_PB[bk]], writes=[b_k[i][nt]])
                    return f

                def ev_o(i):
                    def f(bk, nt):
                        dst = gaT[:, i, nt * nn:(nt + 1) * nn]
                        T.op("act", lambda e: e.activation(out=dst, in_=PB[bk][:, 0:nn], func=AF.Sigmoid),
                             reads=[b_PB[bk]], writes=[b_ga[i][nt]])
                    return f

                def ev_z(i):
                    def f(bk, nt):
                        ecnt[0] += 1
                        p = ecnt[0] % 2
                        dst = gaT[:, i, nt * nn:(nt + 1) * nn]
                        fcg = g * 4 + i
                        T.op("act", lambda e: e.activation(out=szt[p], in_=PB[bk][:, 0:nn], func=AF.Sigmoid),
                             reads=[b_PB[bk]], writes=[b_szt[p]])
                        T.op("dve", lambda e: e.scalar_tensor_tensor(
                            out=tzt[p], in0=PB[bk][:, 0:nn], scalar=pv[:, PV_GH + fcg:PV_GH + fcg + 1], in1=szt[p],
                            op0=ALU.mult, op1=ALU.mult), reads=[b_PB[bk], b_szt[p], b_const], writes=[b_tzt[p]])
                        T.op("dve", lambda e: e.tensor_tensor(out=dst, in0=tzt[p], in1=dst, op=ALU.mult),
                             reads=[b_tzt[p], b_ga[i][nt]], writes=[b_ga[i][nt]])
                    return f

                if g == 0:
                    bank_pool[0] = [0, 1, 2, 3, 4, 5]
                    nfl = -(-len(T.deferred) // 18)
                for (kind, evf) in (("q", lambda i: ev_copy(qT, b_q, i)), ("k", ev_k),
                                    ("v", lambda i: ev_copy(vT, b_v, i)), ("o", ev_o), ("z", ev_z)):
                    wv, bw = ring_next()
                    for i in range(4):
                        proj(wv, bw, i, uT, uT_bufs, Tn, nn, evf(i))
                        if g == 0:
                            T.flush(nfl)
                if g == 0:
                    T.flush()
                    bank_pool[0] = list(range(8))

                gate()
                NP3 = 3
                if g == 0:
                    ktok = [A.alloc(256, BF16) for _ in range(NP3)]
                    vpp = [A.alloc(258, BF16) for _ in range(NP3)]
                    SmT = [A.alloc(128, BF16) for _ in range(NP3)]
                    hn = [A.alloc(512, BF16) for _ in range(2)]
                    st3 = [A.alloc(8, F32) for _ in range(NP3)]
                    junk3 = A.alloc(256, BF16)
                    b_ktok, b_vpp, b_SmT, b_hn, b_st3 = bufs(NP3), bufs(NP3), bufs(NP3), bufs(2), bufs(NP3)
                    b_junk3 = Buf()
                iters = [(c, hh) for c in range(nch) for hh in range(2)]

                def views(it):
                    c, hh = iters[it]
                    p = it % NP3
                    bankT = PB[p][:, 0:256].bitcast(BF16)
                    return dict(c=c, hh=hh, h=2 * g + hh, p=p, tsl=slice(c * L, (c + 1) * L), ntc=(c * L) // nn,
                                i0=hh * 2, pk=bankT[:, 0:256], pvv=bankT[:, 256:512], psS=PB[p][:, 256:384],
                                pdn=PB[p][:, 384:386], psN=PB[3 + p][:, 0:257],
                                psD=PB[6][:, :].rearrange("p (c v) -> p c v", c=2),
                                bT=b_PB[p], bN=b_PB[3 + p], bD=b_PB[6])

                def pe1(it):
                    V = views(it)
                    i0, tsl, ntc, p = V["i0"], V["tsl"], V["ntc"], V["p"]
                    for dc in range(2):
                        T.op("pe", lambda e, dc=dc: e.transpose(out=V["pk"][0:L, dc * 128:(dc + 1) * 128],
                                                                in_=kT[:, i0 + dc, tsl], identity=ident_bf),
                             reads=[b_k[i0 + dc][ntc], b_const], writes=[V["bT"]], signal=False)
                    for dc in range(2):
                        T.op("pe", lambda e, dc=dc: e.transpose(out=V["pvv"][0:L, dc * 128:(dc + 1) * 128],
                                                                in_=vT[:, i0 + dc, tsl], identity=ident_bf),
                             reads=[b_v[i0 + dc][ntc], b_const], writes=[V["bT"]], signal=False)
                    for dc in range(2):
                        T.op("pe", lambda e, dc=dc: e.matmul(V["psS"][0:L, 0:L], lhsT=kT[:, i0 + dc, tsl],
                                                             rhs=qT[:, i0 + dc, tsl], start=(dc == 0), stop=(dc == 1)),
                             reads=[b_k[i0 + dc][ntc], b_q[i0 + dc][ntc]], writes=[V["bT"]], signal=(dc == 1))

                def ev1(it):
                    V = views(it)
                    c, h, p = V["c"], V["h"], V["p"]
                    T.op("dve", lambda e: e.tensor_copy(out=ktok[p][0:L, :], in_=V["pk"][0:L, :]),
                         reads=[V["bT"]], writes=[b_ktok[p]])
                    T.op("dve", lambda e: e.tensor_scalar(out=vpp[p][0:L, 0:256], in0=V["pvv"][0:L, :],
                                                          scalar1=escT[0:L, c, h:h + 1], scalar2=None, op0=ALU.mult),
                         reads=[V["bT"], b_escT], writes=[b_vpp[p]])
                    T.op("dve", lambda e: e.tensor_copy(out=vpp[p][0:L, 256:257], in_=escT[0:L, c, h:h + 1]),
                         reads=[b_escT, b_vpp[p]], writes=[b_vpp[p]])
                    T.op("dve", lambda e: e.tensor_tensor(out=SmT[p][0:L, 0:L], in0=V["psS"][0:L, 0:L],
                                                          in1=maskT[0:L, 0:L], op=ALU.mult),
                         reads=[V["bT"], b_const], writes=[b_SmT[p]])

                def cbf(it):
                    V = views(it)
                    c, h = V["c"], V["h"]
                    T.op("act", lambda e: e.activation(out=Cbf[:, h, :, 0:257], in_=Cst[:, h], func=AF.Identity,
                                                       scale=decbc[:, h, c:c + 1]),
                         reads=[b_C[h], b_decbc], writes=[b_Cbf[h]])

                def pe2(it):
                    V = views(it)
                    c, h, p, i0, tsl, ntc = V["c"], V["h"], V["p"], V["i0"], V["tsl"], V["ntc"]
                    psN, psD, pdn = V["psN"], V["psD"], V["pdn"]
                    for dc in range(2):
                        T.op("pe", lambda e, dc=dc: e.matmul(psN[0:L, :], lhsT=qT[:, i0 + dc, tsl],
                                                             rhs=Cbf[:, h, dc, 0:257], start=(dc == 0), stop=False),
                             reads=[b_q[i0 + dc][ntc], b_Cbf[h]], writes=[V["bN"]], signal=False)
                    T.op("pe", lambda e: e.matmul(psN[0:L, :], lhsT=SmT[p][0:L, 0:L], rhs=vpp[p][0:L, 0:257],
                                                  start=False, stop=True),
                         reads=[b_SmT[p], b_vpp[p]], writes=[V["bN"]])
                    for dc in range(2):
                        T.op("pe", lambda e, dc=dc: e.matmul(psD[:, dc, :], lhsT=ktok[p][0:L, dc * 128:(dc + 1) * 128],
                                                             rhs=vpp[p][0:L, 0:256], start=True, stop=True),
                             reads=[b_ktok[p], b_vpp[p]], writes=[V["bD"]], signal=(dc == 1))
                    for dc in range(2):
                        T.op("pe", lambda e, dc=dc: e.matmul(pdn[:, dc:dc + 1], lhsT=ktok[p][0:L, dc * 128:(dc + 1) * 128],
                                                             rhs=vpp[p][0:L, 256:257], start=True, stop=True),
                             reads=[b_ktok[p], b_vpp[p]], writes=[V["bT"]], signal=(dc == 1))

                def upd(it):
                    V = views(it)
                    c, h = V["c"], V["h"]
                    T.op("dve", lambda e: e.scalar_tensor_tensor(
                        out=Cst[:, h, :, 0:256], in0=Cst[:, h, :, 0:256], scalar=decbc[:, h, c:c + 1], in1=V["psD"],
                        op0=ALU.mult, op1=ALU.add), reads=[b_C[h], b_decbc, V["bD"]], writes=[b_C[h]])
                    T.op("dve", lambda e: e.scalar_tensor_tensor(
                        out=Cst[:, h, :, 256], in0=Cst[:, h, :, 256], scalar=decbc[:, h, c:c + 1], in1=V["pdn"],
                        op0=ALU.mult, op1=ALU.add), reads=[b_C[h], b_decbc, V["bT"]], writes=[b_C[h]])

                def normA(it):
                    V = views(it)
                    p, sv = V["p"], st3[V["p"]]
                    T.op("act", lambda e: e.activation(out=sv[0:L, 5:6], in_=V["psN"][0:L, 256:257], func=AF.Abs),
                         reads=[V["bN"], b_st3[p]], writes=[b_st3[p]])

                def normB(it):
                    V = views(it)
                    c, h, p, sv = V["c"], V["h"], V["p"], st3[V["p"]]
                    T.op("dve", lambda e: e.tensor_tensor(out=sv[0:L, 1:2], in0=sv[0:L, 5:6],
                                                          in1=escT[0:L, c, 4 + h:5 + h], op=ALU.max),
                         reads=[b_escT, b_st3[p]], writes=[b_st3[p]])
                    T.op("dve", lambda e: e.reciprocal(out=sv[0:L, 2:3], in_=sv[0:L, 1:2]),
                         reads=[b_st3[p]], writes=[b_st3[p]])

                def normC(it):
                    V = views(it)
                    p, sv = V["p"], st3[V["p"]]
                    T.op("act", lambda e: e.activation(out=junk3[0:L, :], in_=V["psN"][0:L, 0:256], func=AF.Square,
                                                       scale=sv[0:L, 2:3], accum_out=sv[0:L, 0:1]),
                         reads=[V["bN"], b_st3[p]], writes=[b_junk3, b_st3[p]])
                    T.op("act", lambda e: e.activation(out=sv[0:L, 3:4], in_=sv[0:L, 0:1], func=AF.Ln,
                                                       scale=1.0 / DH, bias=EPS),
                         reads=[b_st3[p]], writes=[b_st3[p]])
                    T.op("act", lambda e: e.activation(out=sv[0:L, 6:7], in_=sv[0:L, 3:4], func=AF.Exp, scale=-0.5),
                         reads=[b_st3[p]], writes=[b_st3[p]])

                def normD(it):
                    V = views(it)
                    p, sv = V["p"], st3[V["p"]]
                    T.op("dve", lambda e: e.tensor_tensor(out=sv[0:L, 4:5], in0=sv[0:L, 6:7], in1=sv[0:L, 2:3],
                                                          op=ALU.mult),
                         reads=[b_st3[p]], writes=[b_st3[p]])

                def normE(it):
                    V = views(it)
                    c, hh, p, sv = V["c"], V["hh"], V["p"], st3[V["p"]]
                    hb = (c % 2)
                    T.op("act", lambda e: e.activation(out=hn[hb][0:L, hh * 256:(hh + 1) * 256], in_=V["psN"][0:L, 0:256],
                                                       func=AF.Identity, scale=sv[0:L, 4:5]),
                         reads=[V["bN"], b_st3[p]], writes=[b_hn[hb]])

                def chunk_end(c):
                    tsl = slice(c * L, (c + 1) * L)
                    ntc = (c * L) // nn
                    hb = (c % 2)
                    psH = PB[7][:, 0:256].bitcast(BF16).rearrange("p (i t) -> p i t", i=4)
                    for i in range(4):
                        T.op("pe", lambda e, i=i: e.transpose(out=psH[:, i, 0:L], in_=hn[hb][0:L, i * 128:(i + 1) * 128],
                                                              identity=ident_bf[0:L, 0:L]),
                             reads=[b_hn[hb], b_const], writes=[b_PB[7]], signal=(i == 3))
                    T.op("dve", lambda e: e.tensor_tensor(out=yaT[:, g * 4:(g + 1) * 4, tsl], in0=psH[:, :, 0:L],
                                                          in1=gaT[:, :, tsl], op=ALU.mult),
                         reads=[b_PB[7]] + [b_ga[i][ntc] for i in range(4)],
                         writes=[b_yaT[g * 4 + i][c] for i in range(4)])

                NI = len(iters)

                def okk(t):
                    return 0 <= t < NI

                pe1(0)
                ev1(0)
                for t in range(NI + 3):
                    if okk(t - 3) and iters[t - 3][1] == 1:
                        chunk_end(iters[t - 3][0])
                    if okk(t):
                        cbf(t)
                    if okk(t + 1):
                        pe1(t + 1)
                    if okk(t - 1):
                        normB(t - 1)
                    if okk(t - 2):
                        normD(t - 2)
                    if okk(t + 1):
                        ev1(t + 1)
                    if okk(t):
                        pe2(t)
                        upd(t)
                    if okk(t - 1):
                        normC(t - 1)
                    if okk(t - 2):
                        normE(t - 2)
                    if okk(t):
                        normA(t)
            T.barrier()
            A.off = PERSIST_END

            gate()
            s1 = [A.alloc(nn, F32) for _ in range(2)]
            s2 = [A.alloc(nn, F32) for _ in range(2)]
            depth = dict(xbp=2, szb=8, xc=5, rr=5, ii=5, mu=3)
            pools = {}
            for kname, dep in depth.items():
                n_el = nn + 4 if kname == "xbp" else nn
                dt_ = BF16 if kname == "xcb" else F32
                pools[kname] = [(A.alloc(n_el, dt_), Buf()) for _ in range(dep)]

            def PBf(kname, u):
                return pools[kname][u % depth[kname]]

            units = [(fc, nt) for fc in range(8) for nt in range(nnt)]
            NU = len(units)
            slotw = {}
            zbank = {}

            def pe_proj(u):
                fc, nt = units[u]
                if fc % 2 == 0 and nt == 0:
                    slotw["cur"] = ring_next(dual=True)
                wv, bw = slotw["cur"]
                bks = []
                for kind in range(2):
                    bk = 2 * kind + (u % 2)
                    bks.append(bk)
                    for kc in range(8):
                        T.op("pe", lambda e, kc=kc: e.matmul(
                            PB[bk][:, 0:nn], lhsT=wv[:, (fc % 2) * 2 + kind, kc, :], rhs=uT[:, kc, nt * nn:(nt + 1) * nn],
                            start=(kc == 0), stop=(kc == 7)),
                            reads=[bw] + uT_bufs(kc, nt), writes=[b_PB[bk]], signal=(kc == 7))
                zbank[u] = bks

            def act_p(u):
                xbp, b_xbp = PBf("xbp", u)
                szb, b_szb = PBf("szb", u)
                bx, bz = zbank[u]
                T.op("act", lambda e: e.activation(out=xbp[:, 3:3 + nn], in_=PB[bx][:, 0:nn], func=AF.Copy),
                     reads=[b_PB[bx]], writes=[b_xbp])
                T.op("act", lambda e: e.activation(out=szb, in_=PB[bz][:, 0:nn], func=AF.Sigmoid),
                     reads=[b_PB[bz]], writes=[b_szb])

            def dve_silu(u):
                szb, b_szb = PBf("szb", u)
                bz = zbank[u][1]
                T.op("dve", lambda e: e.tensor_tensor(out=szb, in0=PB[bz][:, 0:nn], in1=szb, op=ALU.mult),
                     reads=[b_PB[bz], b_szb], writes=[b_szb])

            def dve_conv(u):
                fc, nt = units[u]
                xbp, b_xbp = PBf("xbp", u)
                xc, b_xc = PBf("xc", u)
                cw = lambda j: pv[:, PV_CW + j * 8 + fc: PV_CW + j * 8 + fc + 1]
                ce = "pool"
                T.op(ce, lambda e: e.tensor_copy(out=xbp[:, 0:3], in_=convcar[:, fc, :]),
                     reads=[b_ccar[fc]], writes=[b_xbp])
                T.op(ce, lambda e: e.tensor_copy(out=convcar[:, fc, :], in_=xbp[:, nn:nn + 3]),
                     reads=[b_xbp], writes=[b_ccar[fc]])
                if OPT_B & 2:
                    T.op("act", lambda e: e.activation(out=xc, in_=xbp[:, 0:nn], func=AF.Identity, scale=cw(0),
                                                       bias=pv[:, PV_CB + fc:PV_CB + fc + 1]),
                         reads=[b_xbp, b_const], writes=[b_xc])
                else:
                    T.op("dve", lambda e: e.scalar_tensor_tensor(
                        out=xc, in0=xbp[:, 0:nn], scalar=cw(0), in1=pv[:, PV_CB + fc:PV_CB + fc + 1].to_broadcast([128, nn]),
                        op0=ALU.mult, op1=ALU.add), reads=[b_xbp, b_const], writes=[b_xc])
                for j in range(1, 4):
                    T.op("dve", lambda e, j=j: e.scalar_tensor_tensor(out=xc, in0=xbp[:, j:j + nn], scalar=cw(j),
                                                                      in1=xc, op0=ALU.mult, op1=ALU.add),
                         reads=[b_xbp, b_xc, b_const], writes=[b_xc])

            def pool_cast(u):
                return

            ribank = {}

            def pe_ri(u):
                fc, nt = units[u]
                xcb, b_xcb = PBf("xc", u)
                br_, bi_ = 4, 5
                ribank[u] = (br_, bi_)
                T.op("pe", lambda e: e.matmul(PB[br_][:, 0:nn], lhsT=wr_bf[:, 0, fc, :], rhs=xcb,
                                              start=True, stop=True), reads=[b_xcb, b_wc], writes=[b_PB[br_]])
                T.op("pe", lambda e: e.matmul(PB[bi_][:, 0:nn], lhsT=wr_bf[:, 1, fc, :], rhs=xcb,
                                              start=True, stop=True), reads=[b_xcb, b_wc], writes=[b_PB[bi_]])

            def act_ri(u):
                fc, nt = units[u]
                rr, b_rr = PBf("rr", u)
                ii, b_ii = PBf("ii", u)
                br_, bi_ = ribank[u]
                T.op("act", lambda e: e.activation(out=rr, in_=PB[br_][:, 0:nn], func=AF.Sigmoid,
                                                   bias=pv[:, PV_BRA + fc:PV_BRA + fc + 1]),
                     reads=[b_PB[br_], b_const], writes=[b_rr])
                T.op("act", lambda e: e.activation(out=ii, in_=PB[bi_][:, 0:nn], func=AF.Sigmoid,
                                                   bias=pv[:, PV_BRX + fc:PV_BRX + fc + 1]),
                     reads=[b_PB[bi_], b_const], writes=[b_ii])

            def act_exp(u):
                fc, nt = units[u]
                rr, b_rr = PBf("rr", u)
                mu, b_mu = PBf("mu", u)
                T.op("act", lambda e: e.activation(out=rr, in_=rr, func=AF.Exp, scale=cfeat[:, fc:fc + 1]),
                     reads=[b_rr, b_const], writes=[b_rr])
                if OPT_SQ:
                    T.op("pool", lambda e: e.tensor_tensor(out=mu, in0=rr, in1=rr, op=ALU.mult), reads=[b_rr], writes=[b_mu])
                else:
                    T.op("act", lambda e: e.activation(out=mu, in_=rr, func=AF.Square), reads=[b_rr], writes=[b_mu])
                T.op("act", lambda e: e.activation(out=mu, in_=mu, func=AF.Ln, scale=-1.0, bias=1.0),
                     reads=[b_mu], writes=[b_mu])
                T.op("act", lambda e: e.activation(out=mu, in_=mu, func=AF.Exp, scale=0.5), reads=[b_mu], writes=[b_mu])

            def pool_mul(u):
                ii, b_ii = PBf("ii", u)
                xc, b_xc = PBf("xc", u)
                T.op("pool", lambda e: e.tensor_tensor(out=ii, in0=ii, in1=xc, op=ALU.mult),
                     reads=[b_ii, b_xc], writes=[b_ii])

            def dve_r(u):
                fc, nt = units[u]
                rr, b_rr = PBf("rr", u)
                ii, b_ii = PBf("ii", u)
                mu, b_mu = PBf("mu", u)
                if st["prompt"] and st["first"] and nt == 0:
                    T.op("dve", lambda e: e.memset(mu[:, 0:1], 1.0), reads=[b_mu], writes=[b_mu])
                T.op("dve", lambda e: e.tensor_tensor(out=ii, in0=ii, in1=mu, op=ALU.mult),
                     reads=[b_ii, b_mu], writes=[b_ii])
                T.op("dve", lambda e: e.tensor_tensor_scan(out=mu, data0=rr, data1=ii,
                                                           initial=hcar[:, fc:fc + 1], op0=ALU.mult, op1=ALU.add),
                     reads=[b_rr, b_ii, b_hcar[fc], b_mu], writes=[b_mu])
                T.op("pool", lambda e: e.tensor_copy(out=hcar[:, fc:fc + 1], in_=mu[:, nn - 1:nn]),
                     reads=[b_mu], writes=[b_hcar[fc]])

            def pool_y(u):
                fc, nt = units[u]
                szb, b_szb = PBf("szb", u)
                mu, b_mu = PBf("mu", u)
                T.op("pool", lambda e: e.tensor_tensor(out=ybT[:, fc, nt * nn:(nt + 1) * nn], in0=szb, in1=mu,
                                                       op=ALU.mult),
                     reads=[b_szb, b_mu], writes=[b_ybT[fc][nt]])

            def ok(u):
                return 0 <= u < NU

            b_s1, b_s2 = bufs(2), bufs(2)
            slotc = {}

            def cprime(gi):
                m, nt = divmod(gi, nnt)
                if m % 2 == 0 and nt == 0:
                    slotc["cur"] = ring_next(dual=True)
                wv, bw = slotc["cur"]
                base = (m % 2) * 2
                sl = slice(nt * nn, (nt + 1) * nn)
                p = gi % 2
                for (ci, srcT, rb, bk) in ((base, yaT, lambda kc: [b_yaT[kc][nt * tpn + i] for i in range(tpn)], 6),
                                           (base + 1, uT, lambda kc: uT_bufs(kc, nt), 7)):
                    for kc in range(8):
                        T.op("pe", lambda e, ci=ci, kc=kc, srcT=srcT, bk=bk: e.matmul(
                            PB[bk][:, 0:nn], lhsT=wv[:, ci, kc, :], rhs=srcT[:, kc, sl],
                            start=(kc == 0), stop=(kc == 7)),
                            reads=[bw] + rb(kc), writes=[b_PB[bk]], signal=(kc == 7))
                T.op("act", lambda e: e.activation(out=s1[p], in_=PB[7][:, 0:nn], func=AF.Sigmoid,
                                                   bias=pv[:, PV_BG + m:PV_BG + m + 1]),
                     reads=[b_PB[7], b_const], writes=[b_s1[p]])
                T.op("dve", lambda e: e.tensor_tensor(out=mgT[:, m, sl], in0=PB[6][:, 0:nn], in1=s1[p], op=ALU.mult),
                     reads=[b_PB[6], b_s1[p]], writes=[b_mgT[m][nt]])

            NG = 8 * nnt
            for s_ in range(NU + 8):
                if ok(s_ - 4):
                    act_ri(s_ - 4)
                if ok(s_):
                    pe_proj(s_)
                if ok(s_ - 3):
                    pe_ri(s_ - 3)
                if s_ % 2 == 1:
                    for u in (s_ - 6, s_ - 5):
                        if ok(u):
                            act_exp(u)
                if ok(s_ - 1):
                    act_p(s_ - 1)
                if s_ % 2 == 0:
                    for u in (s_ - 7, s_ - 6):
                        if ok(u):
                            dve_r(u)
                if ok(s_ - 2):
                    dve_conv(s_ - 2)
                if ok(s_ - 1):
                    dve_silu(s_ - 1)
                if ok(s_ - 2):
                    pool_cast(s_ - 2)
                if ok(s_ - 5):
                    pool_mul(s_ - 5)
                if s_ % 2 == 0:
                    for u in (s_ - 7, s_ - 6):
                        if ok(u):
                            pool_y(u)
                if s_ < NG:
                    cprime(s_)

            gate()
            cc = 0
            for m in range(8):
                if m % 2 == 0:
                    wv, bw = ring_next()
                base = (m % 2) * 2
                for nt in range(nnt):
                    sl = slice(nt * nn, (nt + 1) * nn)
                    p = cc % 2
                    cc += 1
                    bkb, bkg = next_bank(), next_bank()
                    for (ci, srcT, rb, bk) in ((base, ybT, lambda kc: [b_ybT[kc][0], b_ybT[kc][1]], bkb),
                                               (base + 1, uT, lambda kc: uT_bufs(kc, nt), bkg)):
                        for kc in range(8):
                            T.op("pe", lambda e, ci=ci, kc=kc, srcT=srcT, bk=bk: e.matmul(
                                PB[bk][:, 0:nn], lhsT=wv[:, ci, kc, :], rhs=srcT[:, kc, sl],
                                start=(kc == 0), stop=(kc == 7)),
                                reads=[bw] + rb(kc), writes=[b_PB[bk]], signal=(kc == 7))
                    T.op("act", lambda e: e.activation(out=s2[p], in_=PB[bkg][:, 0:nn], func=AF.Sigmoid,
                                                       bias=pv[:, PV_BG + 8 + m:PV_BG + 8 + m + 1]),
                         reads=[b_PB[bkg], b_const], writes=[b_s2[p]])
                    T.op("dve", lambda e: e.tensor_tensor(out=s2[p], in0=PB[bkb][:, 0:nn], in1=s2[p], op=ALU.mult),
                         reads=[b_PB[bkb], b_s2[p]], writes=[b_s2[p]])
                    T.op("pool", lambda e: e.tensor_tensor(out=mgT[:, m, sl], in0=mgT[:, m, sl], in1=s2[p], op=ALU.add),
                         reads=[b_s2[p], b_mgT[m][nt]], writes=[b_mgT[m][nt]])
            T.barrier()
            A.off = PERSIST_END
            if DBG_DUMP and st["tok0"] == 0:
                dtmp = A.alloc(8 * TT, F32)
                b_dtmp = Buf()
                for di, srcT in enumerate((yaT, ybT, mgT)):
                    T.op("dve", lambda e, srcT=srcT: e.tensor_copy(out=dtmp, in_=srcT.rearrange("p f t -> p (f t)")),
                         writes=[b_dtmp])
                    T.dma("sp", dbg_d[di], dtmp, [b_dtmp], [], "dbgo")
                T.barrier()
                A.off = PERSIST_END

            gate()
            ZD = pd_alloc(2 if st["prompt"] else 1)
            nxt = sts[idx + 1] if idx + 1 < len(sts) else None
            nD = Tn // tp
            nP = (nxt["T"] // nxt["tp"]) if nxt is not None else 0
            Z0 = p0_alloc() if nxt is not None else None
            if nxt is not None and nxt["prompt"]:
                ring_issue_upto(min(total_loads[0] + NS - 1, NLOADS_TOTAL - 1))
            pd_tile(st, 0, ZD, load_only=True)
            if nP > 0:
                p0_tile(nxt, 0, Z0, load_only=True)
            for j in range(max(nD, nP)):
                if j + 1 < nD:
                    pd_tile(st, j + 1, ZD, load_only=True)
                if j + 1 < nP:
                    p0_tile(nxt, j + 1, Z0, load_only=True)
                hd, hp = j < nD, j < nP
                if hd:
                    pd_tile(st, j, ZD, stages=("pe",))
                if hp:
                    p0_tile(nxt, j, Z0, stages=("stats", "xn", "tr"))
                if hd:
                    pd_tile(st, j, ZD, stages=("mul", "add"))
                if hp:
                    p0_tile(nxt, j, Z0, stages=("ev",))
                if hd:
                    pd_tile(st, j, ZD, stages=("stats", "fin"))
            if st["last"]:
                out_toks["oCT%d" % s] = T.dma("sp", oCT_d[s], Cst.rearrange("p h c v -> p (h c v)"), b_C, [], "o_CT")
                out_toks["om%d" % s] = T.dma("sp", om_d[s], mstate[0:4, :], [b_ms], [], "o_m")
                out_toks["oh%d" % s] = T.dma("sp", oh_d[s], hcar, b_hcar, [], "o_h")
                out_toks["oc%d" % s] = T.dma("sp", oconv_d[s], convcar.rearrange("p f j -> p (f j)"), b_ccar, [], "o_c")
            T.barrier()
            A.off = PERSIST_END

        try:
            for i_ in range(len(sts)):
                run_st(i_)
        except StopBuild:
            T.pending = []

        T.barrier(final=True)
    return nc


_CACHE = {}


def _f32(a):
    return np.ascontiguousarray(np.asarray(a, dtype=np.float32))


def kernel(x_prompt, x_sample, c_prompt, c_sample, state_C, state_n, state_m, state_h, state_conv,
           w_mod, b_mod, g_norm, w_in, b_if, g_head, conv_w, conv_b, w_ra, b_ra, w_rx, b_rx, lam,
           w_gate, b_gate, w_out_a, w_out_b, w_o, g_final):
    n = 8
    x_prompt, x_sample = _f32(x_prompt), _f32(x_sample)
    c_prompt, c_sample = _f32(c_prompt), _f32(c_sample)
    w_in0 = _f32(w_in)[0]
    def fm(v, nchunk):
        return np.ascontiguousarray(_f32(v).reshape(nchunk, 128).T)
    pv = np.zeros((128, NPV), np.float32)
    pv[:, PV_BMOD:PV_BMOD + 24] = fm(b_mod[0], 24)
    pv[:, PV_GN:PV_GN + 8] = fm(g_norm[0], 8)
    pv[:, PV_GH:PV_GH + 8] = fm(g_head[0], 8)
    for j in range(4):
        pv[:, PV_CW + j * 8:PV_CW + j * 8 + 8] = fm(_f32(conv_w)[0, j], 8)
    pv[:, PV_CB:PV_CB + 8] = fm(conv_b[0], 8)
    pv[:, PV_BRA:PV_BRA + 8] = fm(b_ra[0], 8)
    pv[:, PV_BRX:PV_BRX + 8] = fm(b_rx[0], 8)
    pv[:, PV_LAM:PV_LAM + 8] = fm(lam[0], 8)
    pv[:, PV_BG:PV_BG + 16] = fm(b_gate[0], 16)
    bif = np.ascontiguousarray(_f32(b_if)[0].reshape(2, 4).T)
    gfin = _f32(g_final).reshape(1, D)
    wmod = np.ascontiguousarray(_f32(w_mod)[0].reshape(8, 128, 3 * D).transpose(1, 0, 2))

    def chunk(W, col0):
        return W[:, col0:col0 + 128].reshape(8, 128, 128).transpose(1, 0, 2).reshape(128, 1024)
    base = {"q": 0, "k": 1024, "v": 2048, "o": 3072, "z": 4096}
    chunks = []
    for g in range(2):
        for name in ("q", "k", "v", "o", "z"):
            for hh in range(2):
                for dc in range(2):
                    chunks.append(chunk(w_in0, base[name] + (2 * g + hh) * 256 + dc * 128))
    wg0, wa0, wb0 = _f32(w_gate)[0], _f32(w_out_a)[0], _f32(w_out_b)[0]
    for k in range(4):
        for fc in (2 * k, 2 * k + 1):
            chunks.append(chunk(w_in0, 5128 + fc * 128))
            chunks.append(chunk(w_in0, 6152 + fc * 128))
        for m in (2 * k, 2 * k + 1):
            chunks.append(chunk(wa0, m * 128))
            chunks.append(chunk(wg0, m * 128))
    for k in range(4):
        for m in (2 * k, 2 * k + 1):
            chunks.append(chunk(wb0, m * 128))
            chunks.append(chunk(wg0, 1024 + m * 128))
    wst = np.ascontiguousarray(np.stack(chunks, 0))
    assert wst.shape == (NCH_W, 128, 1024)
    wif = np.ascontiguousarray(w_in0[:, 5120:5128].reshape(8, 128, 8).transpose(1, 0, 2).reshape(128, 64))
    wr = np.ascontiguousarray(np.stack([_f32(w_ra)[0].transpose(1, 0, 2), _f32(w_rx)[0].transpose(1, 0, 2)], 1)
                              .reshape(128, 2048))
    wo = np.ascontiguousarray(_f32(w_o)[0].reshape(8, 128, D).transpose(1, 0, 2).reshape(128, 8 * D))
    sC, sn, sm_, sh_, sc_ = _f32(state_C)[0], _f32(state_n)[0], _f32(state_m)[0], _f32(state_h)[0], _f32(state_conv)[0]

    in_maps = []
    for c in range(n):
        xs = np.concatenate([x_prompt[2 * c].reshape(SEQ, D), x_prompt[2 * c + 1].reshape(SEQ, D),
                             x_sample[c].reshape(DEC, D)], 0)
        cs_ = np.stack([c_prompt[2 * c], c_prompt[2 * c + 1], c_sample[c]], 0)
        cT = np.ascontiguousarray(cs_.reshape(3, 8, 128).transpose(2, 1, 0))
        CT = sC[c].reshape(4, 256, 2, 128).transpose(3, 0, 2, 1)
        nT = sn[c].reshape(4, 2, 128).transpose(2, 0, 1)[..., None]
        sCT = np.ascontiguousarray(np.concatenate([CT, nT], -1).reshape(128, 4 * 2 * 257))
        in_maps.append({
            "x": np.ascontiguousarray(xs), "cT": cT, "wmod": wmod, "pv": pv, "bif": bif, "gfin": gfin,
            "wst": wst, "wif": wif, "wr": wr, "wo": wo, "sCT": sCT,
            "sm": np.ascontiguousarray(sm_[c].reshape(4, 1)),
            "sh": np.ascontiguousarray(sh_[c].reshape(8, 128).T),
            "sconv": np.ascontiguousarray(sc_[c].reshape(3, 8, 128).transpose(2, 1, 0).reshape(128, 24)),
        })
    if "nc" not in _CACHE:
        _CACHE["nc"] = build_program()
    res = run_bass_kernel_spmd(_CACHE["nc"], in_maps, core_ids=list(range(n)))
    R = res.results
    y_p = np.zeros((16, SEQ, D), np.float32)
    y_s = np.zeros((8, DEC, D), np.float32)
    Cs = np.zeros((24, 4, 256, 256), np.float32)
    ns = np.zeros((24, 4, 256), np.float32)
    ms = np.zeros((24, 4), np.float32)
    hs = np.zeros((24, D), np.float32)
    cvs = np.zeros((24, 3, D), np.float32)
    for c in range(n):
        r = R[c]
        y_p[2 * c] = r["y"][0:SEQ]
        y_p[2 * c + 1] = r["y"][SEQ:2 * SEQ]
        y_s[c] = r["y"][2 * SEQ:]
        for s, gi in enumerate((2 * c, 2 * c + 1, 16 + c)):
            o = r["oCT"][s].reshape(128, 4, 2, 257)
            Cs[gi] = o[..., 0:256].transpose(1, 3, 2, 0).reshape(4, 256, 256)
            ns[gi] = o[..., 256].transpose(1, 2, 0).reshape(4, 256)
            ms[gi] = r["om"][s].reshape(4)
            hs[gi] = r["oh"][s].T.reshape(D)
            cvs[gi] = r["oconv"][s].reshape(128, 8, 3).transpose(2, 1, 0).reshape(3, D)
    return (y_p, y_s, Cs[None, :16], ns[None, :16], ms[None, :16], hs[None, :16], cvs[None, :16],
            Cs[None, 16:], ns[None, 16:], ms[None, 16:], hs[None, 16:], cvs[None, 16:])
```
